# Optimizing a Trainium2 kernel written in Bass

```python
import jax, jax.numpy as jnp
from jax import lax
import numpy as np

D_MODEL = 1024
BATCH = 8
SEQ = 4096
DEPTH = 1

MEM_LEN = 256
D_MIX = D_MODEL
DSA_HEADS = 8
DSA_HEAD_DIM = 64
IDX_HEADS = 8
IDX_DIM = 32
TOPK_MAX = 256
Q_BLOCK = 128
GLA_HEADS = 4
GLA_DK = 64
GLA_DV = 128
GLA_GATE_RANK = 16
GLA_GATE_TEMP = 16.0
GLA_CHUNK = 64
ROPE_THETA = 500000.0
ROPE_FRACTION = 4
XATTN_HEADS = 4
XATTN_HEAD_DIM = D_MODEL // XATTN_HEADS
PEER_N_KEYS = 128
PEER_N_EXPERTS = PEER_N_KEYS * PEER_N_KEYS
PEER_HEADS = 8
PEER_D_KEY = 256
PEER_TOPK = 16
PEER_BLOCK = 128
LN_EPS = 1e-5
RMS_EPS = 1e-6
DEEPNORM_ALPHA = (2.0 * DEPTH) ** 0.25
DEEPNORM_BETA = (8.0 * DEPTH) ** -0.25
IN_SPLITS = (
    DSA_HEADS * DSA_HEAD_DIM,
    DSA_HEADS * DSA_HEAD_DIM,
    DSA_HEADS * DSA_HEAD_DIM,
    IDX_HEADS * IDX_DIM,
    IDX_DIM,
    IDX_HEADS,
    GLA_HEADS * GLA_DK,
    GLA_HEADS * GLA_DK,
    GLA_HEADS * GLA_DV,
    GLA_GATE_RANK,
    GLA_HEADS * GLA_DV,
)
IN_IS_VALUE = (False, False, True, False, False, False, False, False, True, False, False)
IN_WIDTH = sum(IN_SPLITS)

kernel_name = "hybrid_dsa_gla_peer_deepnorm"


def layer_norm(x, g, b):
    xf = x.astype(jnp.float32)
    mu = jnp.mean(xf, axis=-1, keepdims=True)
    var = jnp.mean(jnp.square(xf - mu), axis=-1, keepdims=True)
    return ((xf - mu) * lax.rsqrt(var + LN_EPS) * g.astype(jnp.float32) + b.astype(jnp.float32)).astype(x.dtype)


def rms_norm(x, g):
    xf = x.astype(jnp.float32)
    return xf * lax.rsqrt(jnp.mean(jnp.square(xf), axis=-1, keepdims=True) + RMS_EPS) * g.astype(jnp.float32)


def rotary_partial(x, positions):
    d = x.shape[-1]
    r = d // ROPE_FRACTION
    half = r // 2
    inv_freq = ROPE_THETA ** (-jnp.arange(half, dtype=jnp.float32) / half)
    ang = positions.astype(jnp.float32)[..., None] * inv_freq
    cos = jnp.cos(ang)[:, :, None, :]
    sin = jnp.sin(ang)[:, :, None, :]
    xf = x.astype(jnp.float32)
    x1, x2, x_pass = xf[..., :half], xf[..., half:r], xf[..., r:]
    out = jnp.concatenate([x1 * cos - x2 * sin, x2 * cos + x1 * sin, x_pass], axis=-1)
    return out.astype(x.dtype)


def dsa_attention(q, k, v, q_idx, k_idx, w_idx):
    B, S = q.shape[0], q.shape[1]
    n_sel = min(TOPK_MAX, S // 4)
    nb = S // Q_BLOCK

    def to_blocks(a):
        return jnp.moveaxis(a.reshape((B, nb, Q_BLOCK) + a.shape[2:]), 1, 0)

    k_idx_f = k_idx.astype(jnp.float32)
    key_pos = jnp.arange(S)
    b_ix = jnp.arange(B)[:, None, None]
    idx_scale = IDX_DIM ** -0.5
    w_scale = IDX_HEADS ** -0.5
    attn_scale = DSA_HEAD_DIM ** -0.5

    def block(args):
        blk, qb, qib, wb = args
        q_pos = blk * Q_BLOCK + jnp.arange(Q_BLOCK)
        causal = key_pos[None, :] <= q_pos[:, None]
        dots = jnp.einsum('bqhd,bsd->bqhs', qib.astype(jnp.float32), k_idx_f) * idx_scale
        score = jnp.einsum('bqh,bqhs->bqs', wb.astype(jnp.float32) * w_scale, jax.nn.relu(dots))
        score = jnp.where(causal[None], score, -jnp.inf)
        _, sel = lax.top_k(score, n_sel)
        valid = sel <= q_pos[None, :, None]
        k_sel = k[b_ix, sel]
        v_sel = v[b_ix, sel]
        logits = jnp.einsum('bqhd,bqkhd->bqhk', qb.astype(jnp.float32), k_sel.astype(jnp.float32)) * attn_scale
        logits = jnp.where(valid[:, :, None, :], logits, -jnp.inf)
        p = jax.nn.softmax(logits, axis=-1)
        return jnp.einsum('bqhk,bqkhd->bqhd', p.astype(v.dtype), v_sel)

    outs = lax.map(block, (jnp.arange(nb), to_blocks(q), to_blocks(q_idx), to_blocks(w_idx)))
    return jnp.moveaxis(outs, 0, 1).reshape(B, S, DSA_HEADS * DSA_HEAD_DIM)


def gla_chunked(q, k, v, log_a):
    B, S, H, dk = q.shape
    dv = v.shape[-1]
    C = GLA_CHUNK
    N = S // C

    def chunks(a):
        return a.astype(jnp.float32).reshape(B, N, C, H, a.shape[-1]).transpose(0, 3, 1, 2, 4)

    qc = chunks(q) * dk ** -0.5
    kc, vc, ac = chunks(k), chunks(v), chunks(log_a)
    bcum = jnp.cumsum(ac, axis=3)
    b_last = bcum[:, :, :, -1:, :]
    q_dec = qc * jnp.exp(bcum)
    k_inv = kc * jnp.exp(-bcum)
    k_to_end = kc * jnp.exp(b_last - bcum)
    causal = jnp.tril(jnp.ones((C, C), dtype=bool))
    attn = jnp.where(causal, jnp.einsum('bhncd,bhnsd->bhncs', q_dec, k_inv), 0.0)
    o_intra = jnp.einsum('bhncs,bhnse->bhnce', attn, vc)
    chunk_kv = jnp.einsum('bhnsd,bhnse->bhnde', k_to_end, vc)
    chunk_decay = jnp.exp(b_last[:, :, :, 0, :])

    def step(state, inp):
        decay, kv = inp
        return decay[..., None] * state + kv, state

    init = jnp.zeros((B, H, dk, dv), jnp.float32)
    _, prev = lax.scan(step, init, (jnp.moveaxis(chunk_decay, 2, 0), jnp.moveaxis(chunk_kv, 2, 0)))
    prev = jnp.moveaxis(prev, 0, 2)
    o_inter = jnp.einsum('bhncd,bhnde->bhnce', q_dec, prev)
    return (o_intra + o_inter).transpose(0, 2, 3, 1, 4).reshape(B, S, H, dv)


def hybrid_mixer(x, positions, w_in, gate_up, gate_bias, norm_g, w_out):
    B, S, _ = x.shape
    proj = jnp.einsum('bsd,de->bse', x, w_in)
    splits = np.cumsum(IN_SPLITS)[:-1].tolist()
    (q, k, v, q_idx, k_idx, w_idx, g_q, g_k, g_v, g_lr, g_r) = jnp.split(proj, splits, axis=-1)

    def heads(a, n):
        return a.reshape(B, S, n, a.shape[-1] // n)

    q = rotary_partial(heads(q, DSA_HEADS), positions)
    k = rotary_partial(heads(k, DSA_HEADS), positions)
    q_idx = rotary_partial(heads(q_idx, IDX_HEADS), positions)
    k_idx = rotary_partial(k_idx[:, :, None, :], positions)[:, :, 0, :]
    y_dsa = dsa_attention(q, k, heads(v, DSA_HEADS), q_idx, k_idx, w_idx)

    log_a = jax.nn.log_sigmoid((g_lr @ gate_up + gate_bias).astype(jnp.float32)) / GLA_GATE_TEMP
    o = gla_chunked(heads(g_q, GLA_HEADS), heads(g_k, GLA_HEADS), heads(g_v, GLA_HEADS),
                    log_a.reshape(B, S, GLA_HEADS, GLA_DK))
    o = rms_norm(o, norm_g).reshape(B, S, GLA_HEADS * GLA_DV).astype(x.dtype)
    y_gla = o * jax.nn.silu(g_r)

    y = jnp.concatenate([y_dsa, y_gla], axis=-1)
    return y @ w_out


def memory_cross_attention(x, mem, w_q, w_k, w_v, w_o):
    B, S, _ = x.shape
    M = mem.shape[1]
    q = (x @ w_q).reshape(B, S, XATTN_HEADS, XATTN_HEAD_DIM)
    k = (mem @ w_k).reshape(B, M, XATTN_HEADS, XATTN_HEAD_DIM)
    v = (mem @ w_v).reshape(B, M, XATTN_HEADS, XATTN_HEAD_DIM)
    logits = jnp.einsum('bshd,bmhd->bhsm', q.astype(jnp.float32), k.astype(jnp.float32)) * XATTN_HEAD_DIM ** -0.5
    p = jax.nn.softmax(logits, axis=-1).astype(v.dtype)
    o = jnp.einsum('bhsm,bmhd->bshd', p, v).reshape(B, S, D_MODEL)
    return o @ w_o


def peer(x, w_query, sub_keys_1, sub_keys_2, expert_down, expert_up):
    B, S, D = x.shape
    half = PEER_D_KEY // 2
    q = (x @ w_query).reshape(B, S, PEER_HEADS, PEER_D_KEY).astype(jnp.float32)
    s1 = jnp.einsum('bshd,nd->bshn', q[..., :half], sub_keys_1.astype(jnp.float32))
    s2 = jnp.einsum('bshd,nd->bshn', q[..., half:], sub_keys_2.astype(jnp.float32))
    v1, i1 = lax.top_k(s1, PEER_TOPK)
    v2, i2 = lax.top_k(s2, PEER_TOPK)
    n_cand = PEER_TOPK * PEER_TOPK
    cand = (v1[..., :, None] + v2[..., None, :]).reshape(B, S, PEER_HEADS, n_cand)
    cand_idx = (i1[..., :, None] * PEER_N_KEYS + i2[..., None, :]).reshape(B, S, PEER_HEADS, n_cand)
    top_s, pos = lax.top_k(cand, PEER_TOPK)
    experts = jnp.take_along_axis(cand_idx, pos, axis=-1)
    gates = jax.nn.softmax(top_s, axis=-1)

    T = B * S
    nb = T // PEER_BLOCK
    hk = PEER_HEADS * PEER_TOPK
    xb = x.reshape(nb, PEER_BLOCK, D)
    eb = experts.reshape(nb, PEER_BLOCK, hk)
    gb = gates.reshape(nb, PEER_BLOCK, hk)

    def block(args):
        xt, et, gt = args
        u = expert_down[et]
        act = jax.nn.gelu(jnp.einsum('td,tkd->tk', xt, u).astype(jnp.float32), approximate=False)
        vv = expert_up[et]
        return jnp.einsum('tk,tkd->td', (gt * act).astype(xt.dtype), vv)

    y = lax.map(block, (xb, eb, gb))
    return y.reshape(B, S, D)


def setup_inputs(seed: int = 0) -> dict:
    key = jax.random.key(seed)
    ks = jax.random.split(key, 24)
    f32 = jnp.float32
    nrm = lambda k, shape, scale: jax.random.normal(k, shape, f32) * scale
    col_scale = jnp.concatenate([jnp.full((n,), DEEPNORM_BETA if is_v else 1.0, f32)
                                 for n, is_v in zip(IN_SPLITS, IN_IS_VALUE)])
    return {
        "x": nrm(ks[0], (BATCH, SEQ, D_MODEL), 1.0),
        "positions": jnp.broadcast_to(jnp.arange(SEQ, dtype=jnp.int32)[None, :], (BATCH, SEQ)),
        "mem": nrm(ks[1], (BATCH, MEM_LEN, D_MODEL), 1.0),
        "w_in": nrm(ks[2], (DEPTH, D_MODEL, IN_WIDTH), D_MODEL ** -0.5) * col_scale,
        "gla_gate_up": nrm(ks[3], (DEPTH, GLA_GATE_RANK, GLA_HEADS * GLA_DK), GLA_GATE_RANK ** -0.5),
        "gla_gate_bias": nrm(ks[4], (DEPTH, GLA_HEADS * GLA_DK), 0.1),
        "gla_norm_g": 1.0 + nrm(ks[5], (DEPTH, GLA_DV), 0.01),
        "w_out": nrm(ks[6], (DEPTH, D_MIX, D_MODEL), D_MIX ** -0.5) * DEEPNORM_BETA,
        "ln_mix_g": 1.0 + nrm(ks[7], (DEPTH, D_MODEL), 0.01),
        "ln_mix_b": nrm(ks[8], (DEPTH, D_MODEL), 0.01),
        "xattn_w_q": nrm(ks[9], (DEPTH, D_MODEL, D_MODEL), D_MODEL ** -0.5),
        "xattn_w_k": nrm(ks[10], (DEPTH, D_MODEL, D_MODEL), D_MODEL ** -0.5),
        "xattn_w_v": nrm(ks[11], (DEPTH, D_MODEL, D_MODEL), D_MODEL ** -0.5) * DEEPNORM_BETA,
        "xattn_w_o": nrm(ks[12], (DEPTH, D_MODEL, D_MODEL), D_MODEL ** -0.5) * DEEPNORM_BETA,
        "ln_mem_g": 1.0 + nrm(ks[13], (DEPTH, D_MODEL), 0.01),
        "ln_mem_b": nrm(ks[14], (DEPTH, D_MODEL), 0.01),
        "peer_w_query": nrm(ks[15], (DEPTH, D_MODEL, PEER_HEADS * PEER_D_KEY), D_MODEL ** -0.5),
        "peer_sub_keys_1": nrm(ks[16], (DEPTH, PEER_N_KEYS, PEER_D_KEY // 2), (PEER_D_KEY // 2) ** -0.5),
        "peer_sub_keys_2": nrm(ks[17], (DEPTH, PEER_N_KEYS, PEER_D_KEY // 2), (PEER_D_KEY // 2) ** -0.5),
        "peer_expert_down": nrm(ks[18], (DEPTH, PEER_N_EXPERTS, D_MODEL), D_MODEL ** -0.5),
        "peer_expert_up": nrm(ks[19], (DEPTH, PEER_N_EXPERTS, D_MODEL), PEER_HEADS ** -0.5) * DEEPNORM_BETA,
        "ln_ffn_g": 1.0 + nrm(ks[20], (DEPTH, D_MODEL), 0.01),
        "ln_ffn_b": nrm(ks[21], (DEPTH, D_MODEL), 0.01),
    }


def reference(x, positions, mem, w_in, gla_gate_up, gla_gate_bias, gla_norm_g, w_out,
              ln_mix_g, ln_mix_b, xattn_w_q, xattn_w_k, xattn_w_v, xattn_w_o, ln_mem_g, ln_mem_b,
              peer_w_query, peer_sub_keys_1, peer_sub_keys_2, peer_expert_down, peer_expert_up,
              ln_ffn_g, ln_ffn_b):
    h = x
    for l in range(DEPTH):
        mix = hybrid_mixer(h, positions, w_in[l], gla_gate_up[l], gla_gate_bias[l], gla_norm_g[l], w_out[l])
        h = layer_norm(DEEPNORM_ALPHA * h + mix, ln_mix_g[l], ln_mix_b[l])
        ca = memory_cross_attention(h, mem, xattn_w_q[l], xattn_w_k[l], xattn_w_v[l], xattn_w_o[l])
        h = layer_norm(DEEPNORM_ALPHA * h + ca, ln_mem_g[l], ln_mem_b[l])
        ff = peer(h, peer_w_query[l], peer_sub_keys_1[l], peer_sub_keys_2[l],
                  peer_expert_down[l], peer_expert_up[l])
        h = layer_norm(DEEPNORM_ALPHA * h + ff, ln_ffn_g[l], ln_ffn_b[l])
    return h
```

```python
import contextlib
import numpy as np
import concourse.bass as bass
import concourse.mybir as mybir

F32 = mybir.dt.float32
BF16 = mybir.dt.bfloat16
I32 = mybir.dt.int32
U32 = mybir.dt.uint32
U16 = mybir.dt.uint16
ALU = mybir.AluOpType
AF = mybir.ActivationFunctionType
AX = mybir.AxisListType

from concourse.bass_utils import run_bass_kernel_spmd

ENGS = ("pe", "act", "dve", "pool", "sp")


class T:
    __slots__ = ("h", "name", "w", "r", "dl", "dlr", "sem", "dcnt")

    def __init__(self, h, name):
        self.h = h
        self.name = name
        self.w = None
        self.r = {}
        self.dl = {}
        self.dlr = {}
        self.sem = None
        self.dcnt = 0

    def __getitem__(self, k):
        return self.h[k]


class TV:
    def __init__(self, t, c0, c1):
        object.__setattr__(self, "_t", t); object.__setattr__(self, "_c0", c0); object.__setattr__(self, "_c1", c1)

    def __getattr__(self, k):
        return getattr(object.__getattribute__(self, "_t"), k)

    def __setattr__(self, k, v):
        setattr(object.__getattribute__(self, "_t"), k, v)

    def __getitem__(self, key):
        p, c = key
        c0 = object.__getattribute__(self, "_c0"); c1 = object.__getattribute__(self, "_c1")
        a = c0 + (c.start or 0); b = c0 + (c.stop if c.stop is not None else (c1 - c0))
        return object.__getattribute__(self, "_t").h[p, a:b]


class AV(TV):
    def __init__(self, t, fn):
        object.__setattr__(self, "_t", t); object.__setattr__(self, "_fn", fn)

    def __getitem__(self, key):
        return object.__getattribute__(self, "_fn")()[key]


class Op:
    __slots__ = ("eng", "fn", "waits", "dwaits", "idx", "need_inc", "dma_tile", "dma_val")

    def __init__(self, eng, fn):
        self.eng = eng
        self.fn = fn
        self.waits = []
        self.dwaits = []
        self.idx = None
        self.need_inc = False
        self.dma_tile = None
        self.dma_val = 0


class MK:
    def __init__(self, nc, st):
        self.nc = nc
        self.st = st
        self.ops = []
        self.cnt = {e: 0 for e in ENGS}
        self.eng_ops = {e: [] for e in ENGS}
        self.seen = {e: {o: -1 for o in ENGS} for e in ENGS}
        self.dseen = {e: {} for e in ENGS}
        self.tiles = []
        self.pend = {e: None for e in ENGS}
        self.defer = None

    def sb(self, name, shape, dt, st=None):
        h = (st or self.st).enter_context(self.nc.sbuf_tensor(name, list(shape), dt))
        t = T(h, name)
        self.tiles.append(t)
        return t

    def ps(self, name, shape, dt, st=None):
        h = (st or self.st).enter_context(self.nc.psum_tensor(name, list(shape), dt))
        t = T(h, name)
        self.tiles.append(t)
        return t

    def alias(self, name, h):
        t = T(h, name)
        self.tiles.append(t)
        return t

    def _dep(self, op, eng, tgt):
        if tgt is None:
            return
        te, ti = tgt
        if te == eng:
            return
        if self.seen[eng][te] >= ti:
            return
        op.waits.append((te, ti))
        self.seen[eng][te] = ti

    def barrier(self):
        for e in ENGS:
            w = [(o, self.cnt[o] - 1) for o in ENGS if o != e and self.cnt[o] > 0]
            d = [(t, t.dcnt) for t in self.tiles if t.sem is not None and t.dcnt > 0]
            self.pend[e] = (w, d)

    def _apply_pend(self, o, eng):
        p = self.pend[eng]
        if p is None:
            return
        self.pend[eng] = None
        for tgt in p[0]:
            self._dep_any(o, eng, tgt)
        for (t, v) in p[1]:
            if self.dseen[eng].get(id(t), 0) < v:
                o.dwaits.append((t, v))
                self.dseen[eng][id(t)] = v

    def replay(self, lst, n):
        assert self.defer is None
        for _ in range(min(n, len(lst))):
            it = lst.pop(0)
            if it[0] == "op":
                self.op(*it[1:])
            else:
                self.dma(*it[1:])

    def op(self, eng, fn, reads=(), writes=()):
        import os
        if self.defer is not None:
            self.defer.append(("op", eng, fn, list(reads), list(writes)))
            return None
        reads = [object.__getattribute__(t, "_t") if isinstance(t, TV) else t for t in reads]
        writes = [object.__getattribute__(t, "_t") if isinstance(t, TV) else t for t in writes]
        if len(self.ops) >= int(os.environ.get("MK_MAXOPS", "100000000")):
            return None
        o = Op(eng, fn)
        self._apply_pend(o, eng)
        for t in reads:
            if t.w is not None:
                if t.w[0] == eng:
                    if eng != "pe" and self.seen[eng][eng] < t.w[1]:
                        o.waits.append(t.w)
                        self.seen[eng][eng] = t.w[1]
                else:
                    self._dep(o, eng, t.w)
            self._ddep(o, eng, t, False)
        for t in writes:
            if t.w is not None:
                if t.w[0] != eng:
                    self._dep(o, eng, t.w)
                elif eng != "pe" and self.seen[eng][eng] < t.w[1]:
                    o.waits.append(t.w)
                    self.seen[eng][eng] = t.w[1]
            for re_, ri in t.r.items():
                if re_ != eng:
                    self._dep(o, eng, (re_, ri))
                elif eng != "pe" and self.seen[eng][eng] < ri:
                    o.waits.append((re_, ri))
                    self.seen[eng][eng] = ri
            self._ddep(o, eng, t)
        o.idx = self.cnt[eng]
        self.cnt[eng] += 1
        self.ops.append(o)
        self.eng_ops[eng].append(o)
        for t in reads:
            t.r[eng] = o.idx
        for t in writes:
            t.w = (eng, o.idx)
            t.r = {}
        return o

    def _ddep(self, o, eng, t, write=True):
        for d in ((t.dl, t.dlr) if write else (t.dl,)):
            for k, (stile, v) in d.items():
                if self.dseen[eng].get(k, 0) < v:
                    o.dwaits.append((stile, v))
                    self.dseen[eng][k] = v

    def _dep_any(self, o, eng, tgt):
        te, ti = tgt
        if self.seen[eng][te] >= ti:
            return
        o.waits.append((te, ti))
        self.seen[eng][te] = ti

    def dma(self, eng, fn, semtile, reads=(), writes=()):
        import os
        if self.defer is not None:
            self.defer.append(("dma", eng, fn, semtile, list(reads), list(writes)))
            return None
        if len(self.ops) >= int(os.environ.get("MK_MAXOPS", "100000000")):
            return None
        reads = [object.__getattribute__(t, "_t") if isinstance(t, TV) else t for t in reads]
        writes = [object.__getattribute__(t, "_t") if isinstance(t, TV) else t for t in writes]
        if isinstance(semtile, TV):
            semtile = object.__getattribute__(semtile, "_t")
        o = Op(eng, fn)
        self._apply_pend(o, eng)
        for t in reads:
            if t.w is not None:
                self._dep_any(o, eng, t.w)
            self._ddep(o, eng, t, False)
        for t in writes:
            if t.w is not None:
                self._dep_any(o, eng, t.w)
            for re_, ri in t.r.items():
                self._dep_any(o, eng, (re_, ri))
            self._ddep(o, eng, t)
        if semtile.sem is None:
            semtile.sem = self.st.enter_context(self.nc.semaphore("d_" + semtile.name))
        semtile.dcnt += 16
        o.dma_tile = semtile
        o.dma_val = semtile.dcnt
        self.ops.append(o)
        for t in reads:
            t.dlr[id(semtile)] = (semtile, semtile.dcnt)
        for t in writes:
            t.dl[id(semtile)] = (semtile, semtile.dcnt)
        for t in writes:
            t.w = None
            t.r = {}
        return o

    def flush(self, final=False):
        nc = self.nc
        if not hasattr(self, "esem"):
            self.esem = {e: self.st.enter_context(nc.semaphore("e_" + e)) for e in ENGS}
            self.ordinal = {e: [] for e in ENGS}
            self.eflushed = {e: 0 for e in ENGS}
            self.flushed = 0
            self.n_wait = 0
        E = {"pe": nc.tensor, "act": nc.scalar, "dve": nc.vector, "pool": nc.gpsimd, "sp": nc.sync}
        chunk = self.ops[self.flushed:]
        for e in ENGS:
            if len(self.eng_ops[e]) > self.eflushed[e]:
                self.eng_ops[e][-1].need_inc = True
        for o in chunk:
            for (te, ti) in o.waits:
                if ti >= self.eflushed[te]:
                    self.eng_ops[te][ti].need_inc = True
        for e in ENGS:
            c = self.ordinal[e][-1] if self.ordinal[e] else 0
            for o in self.eng_ops[e][self.eflushed[e]:]:
                if o.need_inc:
                    c += 1
                self.ordinal[e].append(c)
        for o in chunk:
            eng = E[o.eng]
            for (te, ti) in o.waits:
                v = self.ordinal[te][ti] if self.eng_ops[te][ti].need_inc else self.ordinal[te][ti] + 1
                eng.wait_ge(self.esem[te], v)
                self.n_wait += 1
            for (t, v) in o.dwaits:
                eng.wait_ge(t.sem, v)
                self.n_wait += 1
            inst = o.fn()
            if o.dma_tile is not None:
                inst.then_inc(o.dma_tile.sem, 16)
            elif o.need_inc:
                inst.then_inc(self.esem[o.eng], 1)
        self.flushed = len(self.ops)
        for e in ENGS:
            self.eflushed[e] = len(self.eng_ops[e])
        if final:
            for t in self.tiles:
                if t.sem is not None and t.dcnt > 0:
                    nc.sync.wait_ge(t.sem, t.dcnt)
        return self.n_wait

D = 1024
INW = 3384
ALPHA = 2.0 ** 0.25
THETA = 500000.0
NEG = -1.0e30


def build(S, n_sel, phases="AB", dbg=False):
    import os
    STOP = int(os.environ.get("MK_STOP", "99"))
    NT = S // 128
    nc = bass.Bass("TRN2", target_bir_lowering=False)
    V, A, G, PE, SP = nc.vector, nc.scalar, nc.gpsimd, nc.tensor, nc.sync

    def din(name, shape, dt=F32):
        return nc.dram_tensor(name, shape, dt, kind="ExternalInput").ap()

    x = din("x", [S, D]); pos = din("pos", [128, NT], I32); mem = din("mem", [256, D])
    w_in = din("w_in", [D, INW]); gate_up = din("gate_up", [16, 256]); gate_bias = din("gate_bias", [1, 256])
    norm_g = din("norm_g", [1, 128]); w_out = din("w_out", [D, D])
    ln1g = din("ln1g", [1, D]); ln1b = din("ln1b", [1, D])
    wq = din("wq", [D, D]); wk = din("wk", [D, D]); wv = din("wv", [D, D]); wo = din("wo", [D, D])
    ln2g = din("ln2g", [1, D]); ln2b = din("ln2b", [1, D])
    wpq = din("wpq", [D, 2048]); sk1 = din("sk1", [128, 128]); sk2 = din("sk2", [128, 128])
    edown = din("edown", [16384, D]); eup = din("eup", [16384, D])
    ln3g = din("ln3g", [1, D]); ln3b = din("ln3b", [1, D])
    y = nc.dram_tensor("y", [S, D], F32, kind="ExternalOutput").ap()
    h1d = nc.dram_tensor("h1d", [S, D], F32, kind="Internal").ap()
    woutd = nc.dram_tensor("woutd", [D, D], BF16, kind="Internal").ap()
    edu = nc.dram_tensor("edu", [16384, 2 * D], BF16, kind="Internal").ap()

    with contextlib.ExitStack() as st:
        mk = MK(nc, st)
        H1D = mk.alias("h1dT", None)
        EDU = mk.alias("eduT", None)
        WOD = mk.alias("woutdT", None)
        conv_jobs = [(c, src, off) for c in range(16) for (src, off) in ((edown, 0), (eup, D))]

        def conv_issue(n):
            for _ in range(n):
                if conv_jobs:
                    c, src, off = conv_jobs.pop(0)
                    mk.dma("pool", lambda c=c, src=src, off=off: G.dma_start(out=edu[c * 1024:(c + 1) * 1024, off:off + D].rearrange("(p a) d -> p a d", p=128), in_=src[c * 1024:(c + 1) * 1024, :].rearrange("(p a) d -> p a d", p=128), max_dma_last_dim=4096), EDU, writes=[EDU])
        banks = [mk.ps("bank%d" % i, [128, 512], F32) for i in range(8)]

        def bf(i):
            return banks[i][:, :]

        def bb(i):
            return banks[i][:, :].bitcast(BF16)

        identf = mk.sb("identf", [128, 128], F32)
        ident = mk.sb("ident", [128, 128], BF16)
        mk.op("pool", lambda: G.memset(identf[:, :], 1.0), writes=[identf])
        mk.op("pool", lambda: G.affine_select(out=identf[:, :], in_=identf[:, :], pattern=[[-1, 128]], compare_op=ALU.is_equal, fill=0.0, base=0, channel_multiplier=1), reads=[identf], writes=[identf])
        mk.op("dve", lambda: V.tensor_copy(out=ident[:, :], in_=identf[:, :]), reads=[identf], writes=[ident])
        negI = mk.sb("negI", [128, 4, 128], BF16)
        mk.op("dve", lambda: V.tensor_scalar(out=negI[:, :, :], in0=identf[:, :].unsqueeze(1).to_broadcast([128, 4, 128]), scalar1=-30000.0, scalar2=None, op0=ALU.mult), reads=[identf], writes=[negI])

        def ln_consts(gd, bd, stk, nm, dt=F32):
            g_t = mk.sb(nm + "gs", [128, D], dt, st=stk)
            b_t = mk.sb(nm + "bs", [128, D], dt, st=stk)
            if dt == F32:
                mk.dma("sp", lambda: SP.dma_start(out=g_t[:, :], in_=gd.broadcast_to([128, D])), g_t, writes=[g_t])
                mk.dma("sp", lambda: SP.dma_start(out=b_t[:, :], in_=bd.broadcast_to([128, D])), b_t, writes=[b_t])
            else:
                mk.dma("pool", lambda: G.dma_start(out=g_t[:, :], in_=gd.broadcast_to([128, D])), g_t, writes=[g_t])
                mk.dma("pool", lambda: G.dma_start(out=b_t[:, :], in_=bd.broadcast_to([128, D])), b_t, writes=[b_t])
            return g_t, b_t

        def layer_norm(r_t, g_t, b_t, out_t, wk_t):
            mk.op("dve", lambda: V.bn_stats(out=wk_t[:, 0:6], in_=r_t[:, 0:512]), reads=[r_t], writes=[wk_t])
            mk.op("dve", lambda: V.bn_stats(out=wk_t[:, 6:12], in_=r_t[:, 512:1024]), reads=[r_t], writes=[wk_t])
            mk.op("dve", lambda: V.bn_aggr(out=wk_t[:, 12:14], in_=wk_t[:, 0:12]), reads=[wk_t], writes=[wk_t])
            mk.op("dve", lambda: V.tensor_scalar_add(out=wk_t[:, 14:15], in0=wk_t[:, 13:14], scalar1=1e-5), reads=[wk_t], writes=[wk_t])
            mk.op("act", lambda: A.sqrt(out=wk_t[:, 15:16], in_=wk_t[:, 14:15]), reads=[wk_t], writes=[wk_t])
            mk.op("dve", lambda: V.reciprocal(out=wk_t[:, 16:17], in_=wk_t[:, 15:16]), reads=[wk_t], writes=[wk_t])
            mk.op("dve", lambda: V.tensor_scalar(out=r_t[:, :], in0=r_t[:, :], scalar1=wk_t[:, 12:13], scalar2=wk_t[:, 16:17], op0=ALU.subtract, op1=ALU.mult), reads=[r_t, wk_t], writes=[r_t])
            mk.op("pool", lambda: G.tensor_tensor(out=r_t[:, :], in0=r_t[:, :], in1=g_t[:, :], op=ALU.mult), reads=[r_t, g_t], writes=[r_t])
            mk.op("pool", lambda: G.tensor_tensor(out=out_t[:, :], in0=r_t[:, :], in1=b_t[:, :], op=ALU.add), reads=[r_t, b_t], writes=[out_t])

        def transpose8(src_bf, dst_fn, bank_i, nblk=8):
            bk = banks[bank_i]
            for k in range(nblk):
                mk.op("pe", lambda k=k: PE.transpose(out=bb(bank_i)[:, k * 128:(k + 1) * 128], in_=src_bf[:, k * 128:(k + 1) * 128], identity=ident[:, :]), reads=[src_bf, ident], writes=[bk])
            dst_fn(bk)

        if "A" in phases:
          with contextlib.ExitStack() as sa:
            def sb(name, shape, dt):
                return mk.sb(name, shape, dt, st=sa)
            winb = sb("winb", [128, 8, INW], BF16)
            WQ = [sb("WQ%d" % q, [128, 8, 256], BF16) for q in range(2)]
            for k in range(8):
                mk.dma("pool", lambda k=k: G.dma_start(out=woutd[k * 128:(k + 1) * 128, :], in_=w_out[k * 128:(k + 1) * 128, :], max_dma_last_dim=4096), WOD, writes=[WOD])

            def wq_load(q):
                wt = WQ[q % 2]
                mk.dma("sp", lambda q=q, wt=wt: SP.dma_start(out=wt[:, :, :], in_=woutd[:, q * 256:(q + 1) * 256].rearrange("(k p) n -> p k n", p=128)), wt, reads=[WOD], writes=[wt])
            for k in range(8):
                mk.dma("pool", lambda k=k: G.dma_start(out=winb[:, k, :], in_=w_in[k * 128:(k + 1) * 128, :], max_dma_last_dim=4096), winb, writes=[winb])
            g1, b1 = ln_consts(ln1g, ln1b, sa, "ln1", BF16)
            kTc = sb("kTc", [128, 4, S], BF16)
            Vc = sb("Vc", [128, NT, 8, 66], BF16)
            kidxT = sb("kidxT", [128, S], BF16)
            mk.op("pool", lambda: G.memset(Vc[:, :, :, :], 1.0), writes=[Vc])
            CB = sb("CB", [128, 128], F32)
            mk.op("pool", lambda: G.memset(CB[:, :], 0.0), writes=[CB])
            mk.op("pool", lambda: G.affine_select(out=CB[:, :], in_=CB[:, :], pattern=[[-1, 128]], compare_op=ALU.is_ge, fill=NEG, base=0, channel_multiplier=1), reads=[CB], writes=[CB])
            MT = sb("MT", [128, 128], F32)
            mk.op("pool", lambda: G.memset(MT[:, :], 1.0), writes=[MT])
            mk.op("pool", lambda: G.affine_select(out=MT[:, :], in_=MT[:, :], pattern=[[1, 128]], compare_op=ALU.is_ge, fill=0.0, base=0, channel_multiplier=-1), reads=[MT], writes=[MT])
            TRI = sb("TRI", [128, 128], F32)
            mk.op("pool", lambda: G.memset(TRI[:, :], -1.0 / 16), writes=[TRI])
            mk.op("pool", lambda: G.affine_select(out=TRI[:, :], in_=TRI[:, :], pattern=[[1, 128]], compare_op=ALU.is_ge, fill=0.0, base=0, channel_multiplier=-1), reads=[TRI], writes=[TRI])
            TRU = sb("TRU", [128, 128], F32)
            mk.op("pool", lambda: G.memset(TRU[:, :], -1.0 / 16), writes=[TRU])
            mk.op("pool", lambda: G.affine_select(out=TRU[:, :], in_=TRU[:, :], pattern=[[-1, 128]], compare_op=ALU.is_gt, fill=0.0, base=0, channel_multiplier=1), reads=[TRU], writes=[TRU])
            POW2 = sb("POW2", [128, 16], F32)
            for j in range(16):
                mk.op("pool", lambda j=j: G.memset(POW2[:, j:j + 1], 2.0 ** -(j + 1)), writes=[POW2])
            F12 = sb("F12", [128, 12], F32)
            for j in range(12):
                f = THETA ** (-(j / 8.0)) if j < 8 else THETA ** (-((j - 8) / 4.0))
                mk.op("pool", lambda j=j, f=f: G.memset(F12[:, j:j + 1], f / (2 * np.pi)), writes=[F12])
            posi = sb("posi", [128, NT], I32)
            posf = sb("posf", [128, NT], F32)
            PROJ = sb("PROJ", [128, INW], F32)
            class _V3:
                def __init__(self, c0, dt=None):
                    self.c0 = c0; self.dt = dt
                def __getitem__(self, key):
                    a = PROJ[:, self.c0:self.c0 + NT * 24]
                    if self.dt is not None:
                        a = a.bitcast(self.dt)
                    return a.rearrange("p (t f) -> p t f", t=NT)[key]
            SCt = _V3(0); SCn = _V3(NT * 24, I32); SCf = _V3(2 * NT * 24); SCm = _V3(3 * NT * 24)
            SC = sb("SC", [128, NT, 24], F32)
            mk.dma("sp", lambda: SP.dma_start(out=posi[:, :], in_=pos[:, :]), posi, writes=[posi])
            mk.op("dve", lambda: V.tensor_copy(out=posf[:, :], in_=posi[:, :]), reads=[posi], writes=[posf])
            mk.op("dve", lambda: V.tensor_tensor(out=SCt[:, :, 0:12], in0=posf[:, :].unsqueeze(2).to_broadcast([128, NT, 12]), in1=F12[:, :].unsqueeze(1).to_broadcast([128, NT, 12]), op=ALU.mult), reads=[posf, F12], writes=[PROJ])
            mk.op("dve", lambda: V.tensor_scalar_add(out=SCt[:, :, 12:24], in0=SCt[:, :, 0:12], scalar1=0.25), reads=[PROJ], writes=[PROJ])
            mk.op("dve", lambda: V.tensor_copy(out=SCn[:, :, :], in_=SCt[:, :, :]), reads=[PROJ], writes=[PROJ])
            mk.op("dve", lambda: V.tensor_copy(out=SCf[:, :, :], in_=SCn[:, :, :]), reads=[PROJ], writes=[PROJ])
            mk.op("dve", lambda: V.tensor_tensor(out=SCt[:, :, :], in0=SCt[:, :, :], in1=SCf[:, :, :], op=ALU.subtract), reads=[PROJ], writes=[PROJ])
            mk.op("dve", lambda: V.tensor_single_scalar(out=SCm[:, :, :], in_=SCt[:, :, :], scalar=0.5, op=ALU.is_gt), reads=[PROJ], writes=[PROJ])
            mk.op("dve", lambda: V.tensor_tensor(out=SCt[:, :, :], in0=SCt[:, :, :], in1=SCm[:, :, :], op=ALU.subtract), reads=[PROJ], writes=[PROJ])
            mk.op("dve", lambda: V.tensor_single_scalar(out=SCm[:, :, :], in_=SCt[:, :, :], scalar=-0.5, op=ALU.is_lt), reads=[PROJ], writes=[PROJ])
            mk.op("dve", lambda: V.tensor_tensor(out=SCt[:, :, :], in0=SCt[:, :, :], in1=SCm[:, :, :], op=ALU.add), reads=[PROJ], writes=[PROJ])
            mk.op("act", lambda: A.activation(out=SC[:, :, :], in_=SCt[:, :, :], func=AF.Sin, scale=2 * np.pi * (1 - 2e-6)), reads=[PROJ], writes=[SC])
            GU = sb("GU", [17, 256], F32)
            mk.dma("sp", lambda: SP.dma_start(out=GU[0:16, :], in_=gate_up[:, :]), GU, writes=[GU])
            mk.dma("sp", lambda: SP.dma_start(out=GU[16:17, :], in_=gate_bias[:, :]), GU, writes=[GU])
            NG = sb("NG", [128, 128], F32)
            mk.dma("sp", lambda: SP.dma_start(out=NG[:, :], in_=norm_g.broadcast_to([128, 128])), NG, writes=[NG])
            glrT = sb("glrT", [17, 128], F32)
            mk.op("pool", lambda: G.memset(glrT[:, :], 1.0), writes=[glrT])
            Sst = sb("Sst", [64, 4, 128], F32)
            Sb = sb("Sb", [64, 4, 128], BF16)
            mk.op("pool", lambda: G.memset(Sst[:, :, :], 0.0), writes=[Sst])
            mk.op("pool", lambda: G.memset(Sb[:, :, :], 0.0), writes=[Sb])

            xs = sb("xs", [128, D], F32)
            xb = sb("xb", [128, D], BF16)
            xT = sb("xT", [128, 8, 128], BF16)
            GQK = sb("GQK", [64, 8, 128], BF16)
            QK16 = xb
            IDX16 = sb("IDX16", [128, 384], BF16)
            qT = sb("qT", [128, 4, 128], BF16)
            qidxT = sb("qidxT", [128, 2, 128], BF16)
            wst = sb("wst", [128, 16], F32)
            Ssc = sb("Ssc", [128, max(S, 3200)], F32)
            MB = sb("MB", [128, S], BF16)
            rtmp = [sb("rtmp%d" % i, [128, 512], F32) for i in range(2)]
            t1, t2, t3, t4 = [AV(rtmp[1], (lambda q=q: rtmp[1][:, q * 128:(q + 1) * 128].rearrange("p (a b) -> p a b", a=16))) for q in range(4)]
            sq = rtmp[0]; sil = rtmp[1]
            Lg = AV(Ssc, lambda: Ssc[:, 0:256])
            E1 = AV(Ssc, lambda: Ssc[0:64, 256:768].rearrange("p (a b) -> p a b", a=4))
            E2 = AV(Ssc, lambda: Ssc[0:64, 768:1280].rearrange("p (a b) -> p a b", a=4))
            Ee = AV(Ssc, lambda: Ssc[:, 1280:1536])
            og = AV(Ssc, lambda: Ssc[:, 1536:2048].rearrange("p (a b) -> p a b", a=4))
            qdT = AV(Ssc, lambda: Ssc[0:64, 2048:2304].bitcast(BF16).rearrange("p (a b) -> p a b", a=4))
            kiT = AV(Ssc, lambda: Ssc[0:64, 2304:2560].bitcast(BF16).rearrange("p (a b) -> p a b", a=4))
            kte = AV(Ssc, lambda: Ssc[:, 2560:2688].bitcast(BF16))
            gvb = AV(Ssc, lambda: Ssc[:, 2688:2944].bitcast(BF16))
            ATb = AV(Ssc, lambda: Ssc[:, 2944:3200].bitcast(BF16).rearrange("p (a b) -> p a b", a=4))
            tk = sb("tk", [128, 64], F32)
            dmy = sb("dmy", [128, 8], F32)
            PT0 = sb("PT0", [128, 8, 128], BF16)
            PT = [PT0, PT0]
            Y16 = xb
            yT = xT
            rec8 = sb("rec8", [128, 8], F32)
            gst = sb("gst", [128, 16], F32)
            rr = TV(PROJ, 0, D)
            h1 = rr
            lnw = sb("lnw", [128, 32], F32)

            for i in range(NT):
                nk = (i + 1) * 128
                mk.dma("sp", lambda i=i: SP.dma_start(out=xs[:, :], in_=x[i * 128:(i + 1) * 128, :]), xs, writes=[xs])
                wq_load(0); wq_load(1)
                mk.op("act", lambda: A.copy(out=xb[:, :], in_=xs[:, :]), reads=[xs], writes=[xb])
                transpose8(xb, lambda bk: mk.op("dve", lambda: V.tensor_copy(out=xT[:, :, :], in_=bb(0).rearrange("p (a b) -> p a b", a=8)), reads=[bk], writes=[xT]), 0)
                if STOP < 1:
                    continue
                c0 = 0
                ci = 0
                while c0 < INW:
                    cw = min(512, INW - c0)
                    bi = 1 + (ci % 2)
                    for k in range(8):
                        mk.op("pe", lambda k=k, c0=c0, cw=cw, bi=bi: PE.matmul(bf(bi)[:, 0:cw], lhsT=xT[:, k, :], rhs=winb[:, k, c0:c0 + cw], start=(k == 0), stop=(k == 7)), reads=[xT, winb], writes=[banks[bi]])
                    if ci % 2 == 0:
                        mk.op("act", lambda c0=c0, cw=cw, bi=bi: A.copy(out=PROJ[:, c0:c0 + cw], in_=bf(bi)[:, 0:cw]), reads=[banks[bi]], writes=[PROJ])
                    else:
                        mk.op("dve", lambda c0=c0, cw=cw, bi=bi: V.tensor_copy(out=PROJ[:, c0:c0 + cw], in_=bf(bi)[:, 0:cw]), reads=[banks[bi]], writes=[PROJ])
                    c0 += cw
                    ci += 1
                for j in range(8):
                    col = (1832 if j < 4 else 2088) + (j % 4) * 64
                    bi = 3 + (j // 4)
                    for k in range(8):
                        mk.op("pe", lambda k=k, col=col, bi=bi, j=j: PE.matmul(bf(bi)[0:64, (j % 4) * 128:(j % 4 + 1) * 128], lhsT=winb[:, k, col:col + 64], rhs=xT[:, k, :], start=(k == 0), stop=(k == 7)), reads=[xT, winb], writes=[banks[bi]])
                mk.op("act", lambda: A.copy(out=GQK[:, 0:4, :], in_=bf(3)[0:64, :].rearrange("p (a b) -> p a b", a=4)), reads=[banks[3]], writes=[GQK])
                mk.op("act", lambda: A.copy(out=GQK[:, 4:8, :], in_=bf(4)[0:64, :].rearrange("p (a b) -> p a b", a=4)), reads=[banks[4]], writes=[GQK])

                if STOP < 2:
                    continue
                def rot(base, nh, hd, half, f0, i=i):
                    v3 = PROJ[:, base:base + nh * hd].rearrange("p (h d) -> p h d", h=nh)
                    x1 = v3[:, :, 0:half]; x2 = v3[:, :, half:2 * half]
                    sn = SC[:, i, f0:f0 + half].unsqueeze(1).to_broadcast([128, nh, half])
                    cs = SC[:, i, 12 + f0:12 + f0 + half].unsqueeze(1).to_broadcast([128, nh, half])
                    a1 = t1[:, 0:nh, 0:half]; a2 = t2[:, 0:nh, 0:half]; a3 = t3[:, 0:nh, 0:half]; a4 = t4[:, 0:nh, 0:half]
                    mk.op("dve", lambda: V.tensor_tensor(out=a1, in0=x1, in1=cs, op=ALU.mult), reads=[PROJ, SC], writes=[t1])
                    mk.op("dve", lambda: V.tensor_tensor(out=a2, in0=x2, in1=sn, op=ALU.mult), reads=[PROJ, SC], writes=[t2])
                    mk.op("dve", lambda: V.tensor_tensor(out=a3, in0=x2, in1=cs, op=ALU.mult), reads=[PROJ, SC], writes=[t3])
                    mk.op("dve", lambda: V.tensor_tensor(out=a4, in0=x1, in1=sn, op=ALU.mult), reads=[PROJ, SC], writes=[t4])
                    mk.op("dve", lambda: V.tensor_tensor(out=x1, in0=a1, in1=a2, op=ALU.subtract), reads=[t1, t2], writes=[PROJ])
                    mk.op("dve", lambda: V.tensor_tensor(out=x2, in0=a3, in1=a4, op=ALU.add), reads=[t3, t4], writes=[PROJ])
                rot(0, 16, 64, 8, 0)
                rot(1536, 9, 32, 4, 8)

                if i == 0:
                    print("OPS before stage3", len(mk.ops))
                if STOP < 3:
                    continue
                mk.op("act", lambda: A.copy(out=QK16[:, :], in_=PROJ[:, 0:1024]), reads=[PROJ], writes=[QK16])
                def ev_qk(bk, i=i):
                    mk.op("dve", lambda: V.tensor_copy(out=qT[:, :, :], in_=bb(0)[:, 0:512].rearrange("p (a b) -> p a b", a=4)), reads=[bk], writes=[qT])
                    mk.op("dve", lambda: V.tensor_copy(out=kTc[:, :, i * 128:(i + 1) * 128], in_=bb(0)[:, 512:1024].rearrange("p (a b) -> p a b", a=4)), reads=[bk], writes=[kTc])
                transpose8(QK16, ev_qk, 0)
                mk.op("act", lambda i=i: A.copy(out=Vc[:, i, :, 0:64], in_=PROJ[:, 1024:1536].rearrange("p (h d) -> p h d", h=8)), reads=[PROJ], writes=[Vc])
                mk.op("dve", lambda: V.tensor_copy(out=IDX16[:, 0:256], in_=PROJ[:, 1536:1792]), reads=[PROJ], writes=[IDX16])
                mk.op("dve", lambda: V.tensor_copy(out=IDX16[:, 256:384].rearrange("p (r d) -> p r d", r=4), in_=PROJ[:, 1792:1824].unsqueeze(1).to_broadcast([128, 4, 32])), reads=[PROJ], writes=[IDX16])
                def ev_idx(bk, i=i):
                    mk.op("dve", lambda: V.tensor_copy(out=qidxT[:, :, :], in_=bb(0)[:, 0:256].rearrange("p (a b) -> p a b", a=2)), reads=[bk], writes=[qidxT])
                    mk.op("dve", lambda: V.tensor_copy(out=kidxT[:, i * 128:(i + 1) * 128], in_=bb(0)[:, 256:384]), reads=[bk], writes=[kidxT])
                transpose8(IDX16, ev_idx, 0, nblk=3)
                mk.op("dve", lambda: V.tensor_scalar(out=wst[:, 8:16], in0=PROJ[:, 1824:1832], scalar1=0.0, scalar2=2.0, op0=ALU.is_gt, op1=ALU.mult), reads=[PROJ], writes=[wst])
                mk.op("dve", lambda: V.tensor_scalar_add(out=wst[:, 8:16], in0=wst[:, 8:16], scalar1=-1.0), reads=[wst], writes=[wst])
                mk.op("dve", lambda: V.tensor_tensor(out=wst[:, 0:8], in0=PROJ[:, 1824:1832], in1=wst[:, 8:16], op=ALU.mult), reads=[PROJ, wst], writes=[wst])

                if i == 0:
                    print("OPS before stage4", len(mk.ops))
                if STOP < 4:
                    continue
                nblk = (nk + 511) // 512
                for b in range(nblk):
                    k0 = b * 512
                    kw = min(512, nk - k0)
                    for h in range(8):
                        bi = 1 + (h % 2)
                        rt = rtmp[h % 2]
                        pb = 32 * (h % 4)
                        mk.op("pe", lambda h=h, bi=bi, pb=pb, k0=k0, kw=kw: PE.matmul(bf(bi)[:, 0:kw], lhsT=qidxT[pb:pb + 32, h // 4, :], rhs=kidxT[pb:pb + 32, k0:k0 + kw], start=True, stop=True, tile_position=(pb, 0)), reads=[qidxT, kidxT], writes=[banks[bi]])
                        mk.op("act", lambda h=h, bi=bi, kw=kw, rt=rt: A.activation(out=rt[:, 0:kw], in_=bf(bi)[:, 0:kw], func=AF.Relu, scale=wst[:, h:h + 1]), reads=[banks[bi], wst], writes=[rt])
                        if h == 0:
                            mk.op("dve", lambda h=h, k0=k0, kw=kw, rt=rt: V.tensor_scalar(out=Ssc[:, k0:k0 + kw], in0=rt[:, 0:kw], scalar1=wst[:, 8 + h:9 + h], scalar2=None, op0=ALU.mult), reads=[rt, wst], writes=[Ssc])
                        else:
                            mk.op("dve", lambda h=h, k0=k0, kw=kw, rt=rt: V.scalar_tensor_tensor(out=Ssc[:, k0:k0 + kw], in0=rt[:, 0:kw], scalar=wst[:, 8 + h:9 + h], in1=Ssc[:, k0:k0 + kw], op0=ALU.mult, op1=ALU.add), reads=[rt, wst, Ssc], writes=[Ssc])
                if STOP < 5:
                    continue
                use_topk = nk > n_sel
                if use_topk:
                    mk.op("dve", lambda nk=nk: V.tensor_reduce(out=tk[:, 0:1], in_=Ssc[:, 0:nk], axis=AX.X, op=ALU.max), reads=[Ssc], writes=[tk])
                    mk.op("dve", lambda nk=nk: V.tensor_reduce(out=tk[:, 1:2], in_=Ssc[:, 0:nk], axis=AX.X, op=ALU.min), reads=[Ssc], writes=[tk])
                mk.op("dve", lambda i=i: V.tensor_tensor(out=Ssc[:, i * 128:(i + 1) * 128], in0=Ssc[:, i * 128:(i + 1) * 128], in1=CB[:, :], op=ALU.add), reads=[Ssc, CB], writes=[Ssc])
                if use_topk:
                    mk.op("dve", lambda: V.tensor_tensor(out=tk[:, 2:3], in0=tk[:, 0:1], in1=tk[:, 1:2], op=ALU.subtract), reads=[tk], writes=[tk])
                    mk.op("dve", lambda: V.tensor_scalar(out=tk[:, 8:24], in0=POW2[:, :], scalar1=tk[:, 2:3], scalar2=None, op0=ALU.mult), reads=[tk, POW2], writes=[tk])
                    mk.op("dve", lambda: V.tensor_tensor(out=tk[:, 3:4], in0=tk[:, 1:2], in1=tk[:, 8:9], op=ALU.add), reads=[tk], writes=[tk])
                    for it in range(16):
                        mk.op("dve", lambda nk=nk: V.tensor_scalar(out=MB[:, 0:nk], in0=Ssc[:, 0:nk], scalar1=tk[:, 3:4], scalar2=None, op0=ALU.is_ge, op1=ALU.add, accum_out=tk[:, 4:5]), reads=[Ssc, tk], writes=[MB, tk])
                        mk.op("dve", lambda: V.memset(dmy[:, 0:1], 0.0), writes=[dmy])
                        mk.op("dve", lambda it=it: V.scalar_tensor_tensor(out=tk[:, 5:6], in0=tk[:, 4:5], scalar=float(n_sel), in1=tk[:, 8 + it:9 + it], op0=ALU.is_ge, op1=ALU.mult), reads=[tk], writes=[tk])
                        if it < 15:
                            mk.op("dve", lambda it=it: V.scalar_tensor_tensor(out=tk[:, 3:4], in0=tk[:, 5:6], scalar=tk[:, 9 + it:10 + it], in1=tk[:, 3:4], op0=ALU.subtract, op1=ALU.add), reads=[tk], writes=[tk])
                        else:
                            mk.op("dve", lambda: V.scalar_tensor_tensor(out=tk[:, 1:2], in0=tk[:, 5:6], scalar=tk[:, 23:24], in1=tk[:, 3:4], op0=ALU.subtract, op1=ALU.add), reads=[tk], writes=[tk])
                    mk.op("dve", lambda nk=nk: V.tensor_scalar(out=MB[:, 0:nk], in0=Ssc[:, 0:nk], scalar1=tk[:, 1:2], scalar2=None, op0=ALU.is_lt), reads=[Ssc, tk], writes=[MB])
                else:
                    mk.op("dve", lambda nk=nk: V.tensor_scalar(out=MB[:, 0:nk], in0=Ssc[:, 0:nk], scalar1=-1.0e29, scalar2=None, op0=ALU.is_lt), reads=[Ssc], writes=[MB])

                if STOP < 6:
                    continue
                for j in range(i + 1):
                    lb = 2 + 2 * (j % 2)
                    pt = PT[j % 2]
                    for h in range(8) if os.environ.get("MK_MERGE", "1") == "0" else []:
                        bi = lb + h // 4
                        pb = 64 * (h % 2)
                        o_ap = (lambda bi=bi, h=h: bf(bi)[:, (h % 4) * 128:(h % 4 + 1) * 128])
                        mk.op("pe", lambda h=h, pb=pb, j=j, o_ap=o_ap: PE.matmul(o_ap(), lhsT=kTc[pb:pb + 64, h // 2, j * 128:(j + 1) * 128], rhs=qT[pb:pb + 64, h // 2, :], start=True, stop=False, tile_position=(pb, 0)), reads=[kTc, qT], writes=[banks[bi]])
                        mk.op("pe", lambda j=j, o_ap=o_ap: PE.matmul(o_ap(), lhsT=MB[:, j * 128:(j + 1) * 128], rhs=negI[:, 0, :], start=False, stop=True), reads=[MB, negI], writes=[banks[bi]])
                    if os.environ.get("MK_MERGE", "1") == "1":
                        for r_ in range(4):
                            for par in range(2):
                                h = 2 * r_ + par
                                pb = 64 * par
                                bi = lb + par
                                mk.op("pe", lambda h=h, pb=pb, j=j, bi=bi, r_=r_: PE.matmul(bf(bi)[:, r_ * 128:(r_ + 1) * 128], lhsT=kTc[pb:pb + 64, h // 2, j * 128:(j + 1) * 128], rhs=qT[pb:pb + 64, h // 2, :], start=(r_ == 0), stop=False, tile_position=(pb, 0), skip_group_check=True), reads=[kTc, qT], writes=[banks[bi]])
                        for par in range(2):
                            bi = lb + par
                            mk.op("pe", lambda j=j, bi=bi: PE.matmul(bf(bi)[:, :], lhsT=MB[:, j * 128:(j + 1) * 128], rhs=negI[:, :, :].rearrange("p a b -> p (a b)"), start=False, stop=True, skip_group_check=True), reads=[MB, negI], writes=[banks[bi]])
                        ptv = (lambda pt=pt: pt[:, :, :].rearrange("p (a two) b -> p a two b", two=2))
                        mk.op("act", lambda lb=lb, ptv=ptv: A.activation(out=ptv()[:, :, 0, :], in_=bf(lb).rearrange("p (a b) -> p a b", a=4), func=AF.Exp, scale=0.125), reads=[banks[lb]], writes=[pt])
                        mk.op("act", lambda lb=lb, ptv=ptv: A.activation(out=ptv()[:, :, 1, :], in_=bf(lb + 1).rearrange("p (a b) -> p a b", a=4), func=AF.Exp, scale=0.125), reads=[banks[lb + 1]], writes=[pt])
                    if os.environ.get("MK_MERGE", "1") == "0":
                        mk.op("act", lambda lb=lb, pt=pt: A.activation(out=pt[:, 0:4, :], in_=bf(lb).rearrange("p (a b) -> p a b", a=4), func=AF.Exp, scale=0.125), reads=[banks[lb]], writes=[pt])
                        mk.op("act", lambda lb=lb, pt=pt: A.activation(out=pt[:, 4:8, :], in_=bf(lb + 1).rearrange("p (a b) -> p a b", a=4), func=AF.Exp, scale=0.125), reads=[banks[lb + 1]], writes=[pt])
                    for h in range(8):
                        bi = 6 + h // 4
                        mk.op("pe", lambda h=h, bi=bi, j=j, pt=pt: PE.matmul(bf(bi)[:, (h % 4) * 65:(h % 4) * 65 + 65], lhsT=pt[:, h, :], rhs=Vc[:, j, h, 0:65], start=(j == 0 and h % 4 == 0), stop=(j == i), skip_group_check=True), reads=[pt, Vc], writes=[banks[bi]])
                for hb in range(2):
                    ov = (lambda hb=hb: bf(6 + hb)[:, 0:260].rearrange("p (h e) -> p h e", h=4))
                    mk.op("dve", lambda hb=hb, ov=ov: V.reciprocal(out=rec8[:, hb * 4:hb * 4 + 4], in_=ov()[:, :, 64]), reads=[banks[6 + hb]], writes=[rec8])
                    mk.op("dve", lambda hb=hb, ov=ov: V.tensor_tensor(out=Y16[:, hb * 256:(hb + 1) * 256].rearrange("p (h e) -> p h e", h=4), in0=ov()[:, :, 0:64], in1=rec8[:, hb * 4:hb * 4 + 4].unsqueeze(2).to_broadcast([128, 4, 64]), op=ALU.mult), reads=[banks[6 + hb], rec8], writes=[Y16])

                if STOP < 7:
                    continue
                mk.op("pe", lambda: PE.transpose(out=bf(1)[0:16, 0:128], in_=PROJ[:, 2856:2872], identity=identf[:, :]), reads=[PROJ, identf], writes=[banks[1]])
                mk.op("act", lambda: A.copy(out=glrT[0:16, :], in_=bf(1)[0:16, 0:128]), reads=[banks[1]], writes=[glrT])
                mk.op("pe", lambda: PE.matmul(bf(2)[:, 0:256], lhsT=glrT[:, :], rhs=GU[:, :], start=True, stop=True), reads=[glrT, GU], writes=[banks[2]])
                mk.op("act", lambda: A.activation(out=Lg[:, :], in_=bf(2)[:, 0:256], func=AF.Exp, scale=-1.0), reads=[banks[2]], writes=[Lg])
                mk.op("act", lambda: A.activation(out=Lg[:, :], in_=Lg[:, :], func=AF.Ln, bias=1.0), reads=[Lg], writes=[Lg])
                for h in range(4):
                    mk.op("pe", lambda h=h: PE.matmul(bf(3)[0:64, h * 128:(h + 1) * 128], lhsT=Lg[:, h * 64:(h + 1) * 64], rhs=TRI[:, :], start=True, stop=True), reads=[Lg, TRI], writes=[banks[3]])
                mk.op("pe", lambda: PE.matmul(bf(4)[:, 0:256], lhsT=TRU[:, :], rhs=Lg[:, :], start=True, stop=True), reads=[Lg, TRU], writes=[banks[4]])
                b3v = (lambda: bf(3)[0:64, :].rearrange("p (a b) -> p a b", a=4))
                mk.op("act", lambda: A.activation(out=E1[:, :, :], in_=b3v(), func=AF.Exp), reads=[banks[3]], writes=[E1])
                mk.op("act", lambda: A.activation(out=E2[:, :, :], in_=b3v(), func=AF.Exp, scale=-1.0), reads=[banks[3]], writes=[E2])
                mk.op("act", lambda: A.activation(out=Ee[:, :], in_=bf(4)[:, 0:256], func=AF.Exp), reads=[banks[4]], writes=[Ee])
                mk.op("dve", lambda: V.scalar_tensor_tensor(out=qdT[:, :, :], in0=GQK[:, 0:4, :], scalar=0.125, in1=E1[:, :, :], op0=ALU.mult, op1=ALU.mult), reads=[GQK, E1], writes=[qdT])
                mk.op("dve", lambda: V.tensor_tensor(out=kiT[:, :, :], in0=GQK[:, 4:8, :], in1=E2[:, :, :], op=ALU.mult), reads=[GQK, E2], writes=[kiT])
                mk.op("dve", lambda: V.tensor_tensor(out=kte[:, :], in0=PROJ[:, 2088:2344], in1=Ee[:, :], op=ALU.mult), reads=[PROJ, Ee], writes=[kte])
                mk.op("act", lambda: A.copy(out=gvb[:, :], in_=PROJ[:, 2344:2856]), reads=[PROJ], writes=[gvb])
                for h in range(4):
                    mk.op("pe", lambda h=h: PE.matmul(bf(5)[:, h * 128:(h + 1) * 128], lhsT=kiT[:, h, :], rhs=qdT[:, h, :], start=True, stop=True), reads=[kiT, qdT], writes=[banks[5]])
                mk.op("dve", lambda: V.tensor_tensor(out=ATb[:, :, :], in0=bf(5).rearrange("p (a b) -> p a b", a=4), in1=MT[:, :].unsqueeze(1).to_broadcast([128, 4, 128]), op=ALU.mult), reads=[banks[5], MT], writes=[ATb])
                for h in range(4):
                    mk.op("pe", lambda h=h: PE.matmul(bf(1)[:, h * 128:(h + 1) * 128], lhsT=ATb[:, h, :], rhs=gvb[:, h * 128:(h + 1) * 128], start=True, stop=False), reads=[ATb, gvb], writes=[banks[1]])
                    mk.op("pe", lambda h=h: PE.matmul(bf(1)[:, h * 128:(h + 1) * 128], lhsT=qdT[:, h, :], rhs=Sb[:, h, :], start=False, stop=True), reads=[qdT, Sb], writes=[banks[1]])
                for h in range(4):
                    mk.op("pe", lambda h=h: PE.matmul(bf(2)[0:64, h * 128:(h + 1) * 128], lhsT=kte[:, h * 64:(h + 1) * 64], rhs=gvb[:, h * 128:(h + 1) * 128], start=True, stop=True), reads=[kte, gvb], writes=[banks[2]])
                for h in range(4):
                    mk.op("dve", lambda h=h: V.scalar_tensor_tensor(out=Sst[:, h, :], in0=Sst[:, h, :], scalar=E1[:, h, 127:128], in1=bf(2)[0:64, h * 128:(h + 1) * 128], op0=ALU.mult, op1=ALU.add), reads=[Sst, E1, banks[2]], writes=[Sst])
                mk.op("dve", lambda: V.tensor_copy(out=Sb[:, :, :], in_=Sst[:, :, :]), reads=[Sst], writes=[Sb])
                mk.op("act", lambda: A.copy(out=og[:, :, :], in_=bf(1).rearrange("p (a b) -> p a b", a=4)), reads=[banks[1]], writes=[og])
                mk.op("dve", lambda: V.tensor_tensor(out=sq[:, :], in0=og[:, :, :].rearrange("p a b -> p (a b)"), in1=og[:, :, :].rearrange("p a b -> p (a b)"), op=ALU.mult), reads=[og], writes=[sq])
                mk.op("dve", lambda: V.tensor_reduce(out=gst[:, 0:4], in_=sq[:, :].rearrange("p (a b) -> p a b", a=4), axis=AX.X, op=ALU.add), reads=[sq], writes=[gst])
                mk.op("dve", lambda: V.tensor_scalar(out=gst[:, 4:8], in0=gst[:, 0:4], scalar1=1.0 / 128, scalar2=1e-6, op0=ALU.mult, op1=ALU.add), reads=[gst], writes=[gst])
                mk.op("act", lambda: A.sqrt(out=gst[:, 8:12], in_=gst[:, 4:8]), reads=[gst], writes=[gst])
                mk.op("dve", lambda: V.reciprocal(out=gst[:, 12:16], in_=gst[:, 8:12]), reads=[gst], writes=[gst])
                mk.op("dve", lambda: V.tensor_tensor(out=og[:, :, :], in0=og[:, :, :], in1=gst[:, 12:16].unsqueeze(2).to_broadcast([128, 4, 128]), op=ALU.mult), reads=[og, gst], writes=[og])
                mk.op("dve", lambda: V.tensor_tensor(out=og[:, :, :], in0=og[:, :, :], in1=NG[:, :].unsqueeze(1).to_broadcast([128, 4, 128]), op=ALU.mult), reads=[og, NG], writes=[og])
                mk.op("act", lambda: A.activation(out=sil[:, :], in_=PROJ[:, 2872:3384], func=AF.Silu), reads=[PROJ], writes=[sil])
                mk.op("dve", lambda: V.tensor_tensor(out=Y16[:, 512:1024], in0=og[:, :, :].rearrange("p a b -> p (a b)"), in1=sil[:, :], op=ALU.mult), reads=[og, sil], writes=[Y16])

                if STOP < 8:
                    continue
                transpose8(Y16, lambda bk: mk.op("dve", lambda: V.tensor_copy(out=yT[:, :, :], in_=bb(0).rearrange("p (a b) -> p a b", a=8)), reads=[bk], writes=[yT]), 0)
                for hf in range(2):
                    bi = 1 + hf
                    for qd in range(2):
                        q4 = hf * 2 + qd
                        wt = WQ[q4 % 2]
                        for k in range(8):
                            mk.op("pe", lambda k=k, qd=qd, bi=bi, wt=wt: PE.matmul(bf(bi)[:, qd * 256:(qd + 1) * 256], lhsT=yT[:, k, :], rhs=wt[:, k, :], start=(k == 0), stop=(k == 7)), reads=[yT, wt], writes=[banks[bi]])
                        if q4 + 2 < 4:
                            wq_load(q4 + 2)
                    mk.op("dve", lambda hf=hf, bi=bi: V.scalar_tensor_tensor(out=rr[:, hf * 512:(hf + 1) * 512], in0=xs[:, hf * 512:(hf + 1) * 512], scalar=ALPHA, in1=bf(bi)[:, :], op0=ALU.mult, op1=ALU.add), reads=[xs, banks[bi]], writes=[rr])
                layer_norm(rr, g1, b1, h1, lnw)
                conv_issue((32 + NT - 1) // NT)
                if os.environ.get("MK_DBG") == "M":
                    mk.op("dve", lambda: V.tensor_copy(out=h1[:, 0:24], in_=tk[:, 0:24]), reads=[tk], writes=[h1])
                    mk.op("dve", lambda nk=nk: V.tensor_reduce(out=h1[:, 24:25], in_=MB[:, 0:nk], axis=AX.X, op=ALU.add), reads=[MB], writes=[h1])
                    mk.op("dve", lambda nk=nk: V.tensor_copy(out=h1[:, 32:32 + nk], in_=Ssc[:, 0:nk]), reads=[Ssc], writes=[h1])
                if os.environ.get("MK_DBG") == "Y":
                    mk.op("dve", lambda: V.tensor_copy(out=h1[:, :], in_=Y16[:, :]), reads=[Y16], writes=[h1])
                mk.dma("sp", lambda i=i: SP.dma_start(out=h1d[i * 128:(i + 1) * 128, :], in_=h1[:, :]), h1, reads=[h1], writes=[H1D])
            mk.barrier()
            mk.flush()


        if "B" in phases:
          with contextlib.ExitStack() as sbk:
            def sb(name, shape, dt):
                return mk.sb(name, shape, dt, st=sbk)
            conv_issue(64)
            wqb = sb("wqb", [128, 8, D], BF16); wob = sb("wob", [128, 8, D], BF16)
            wpqb = sb("wpqb", [128, 8, 2048], BF16)
            KmT = sb("KmT", [128, 8, 256], BF16); Vm = sb("Vm", [128, 2, D], BF16)
            skT = sb("skT", [128, 2, 128], BF16)
            g2, b2 = ln_consts(ln2g, ln2b, sbk, "ln2")
            g3, b3 = ln_consts(ln3g, ln3b, sbk, "ln3")
            ones = sb("ones", [128, 128], BF16)
            mk.op("pool", lambda: G.memset(ones[:, :], 1.0), writes=[ones])
            io_i = sb("io_i", [128, 16], I32); IO16 = sb("IO16", [128, 16], F32)
            mk.op("pool", lambda: G.iota(out=io_i[:, :], pattern=[[1, 16]], base=0, channel_multiplier=0), writes=[io_i])
            mk.op("dve", lambda: V.tensor_copy(out=IO16[:, :], in_=io_i[:, :]), reads=[io_i], writes=[IO16])
            dm2 = sb("dm2", [128, 8], F32)
            pro = contextlib.ExitStack()
            def sbp(name, shape, dt):
                return mk.sb(name, shape, dt, st=pro)
            wkb = sbp("wkb", [128, 8, D], BF16); wvb = sbp("wvb", [128, 8, D], BF16)
            for (wt, wd, wn) in ((wkb, wk, D), (wvb, wv, D), (wqb, wq, D), (wob, wo, D), (wpqb, wpq, 2048)):
                for k in range(8):
                    mk.dma("pool", lambda k=k, wt=wt, wd=wd: G.dma_start(out=wt[:, k, :], in_=wd[k * 128:(k + 1) * 128, :], max_dma_last_dim=4096), wt, writes=[wt])
            ms = sbp("ms", [128, 2, D], F32); mb = sbp("mb", [128, 2 * D], BF16)
            memT = sbp("memT", [128, 8, 256], BF16)
            mk.dma("sp", lambda: SP.dma_start(out=ms[:, :, :], in_=mem.rearrange("(a p) d -> p a d", p=128)), ms, writes=[ms])
            mk.op("dve", lambda: V.tensor_copy(out=mb[:, :], in_=ms[:, :, :].rearrange("p a d -> p (a d)")), reads=[ms], writes=[mb])
            for a in range(2):
                def ev_m(bk, a=a):
                    mk.op("dve", lambda: V.tensor_copy(out=memT[:, :, a * 128:(a + 1) * 128], in_=bb(0).rearrange("p (k b) -> p k b", k=8)), reads=[bk], writes=[memT])
                for k in range(8):
                    mk.op("pe", lambda k=k, a=a: PE.transpose(out=bb(0)[:, k * 128:(k + 1) * 128], in_=mb[:, a * D + k * 128:a * D + (k + 1) * 128], identity=ident[:, :]), reads=[mb, ident], writes=[banks[0]])
                ev_m(banks[0])
            for c in range(8):
                bi = 1 + c % 2
                for k in range(8):
                    mk.op("pe", lambda k=k, c=c, bi=bi: PE.matmul(bf(bi)[:, 0:256], lhsT=wkb[:, k, c * 128:(c + 1) * 128], rhs=memT[:, k, :], start=(k == 0), stop=(k == 7)), reads=[wkb, memT], writes=[banks[bi]])
                mk.op("dve", lambda c=c, bi=bi: V.tensor_copy(out=KmT[:, c, :], in_=bf(bi)[:, 0:256]), reads=[banks[bi]], writes=[KmT])
            for mt in range(2):
                for hf in range(2):
                    bi = 3 + hf
                    for k in range(8):
                        mk.op("pe", lambda k=k, mt=mt, hf=hf, bi=bi: PE.matmul(bf(bi)[:, :], lhsT=memT[:, k, mt * 128:(mt + 1) * 128], rhs=wvb[:, k, hf * 512:(hf + 1) * 512], start=(k == 0), stop=(k == 7)), reads=[wvb, memT], writes=[banks[bi]])
                    mk.op("dve", lambda mt=mt, hf=hf, bi=bi: V.tensor_copy(out=Vm[:, mt, hf * 512:(hf + 1) * 512], in_=bf(bi)[:, :]), reads=[banks[bi]], writes=[Vm])
            sks = sbp("sks", [128, 256], F32); skb = sbp("skb", [128, 256], BF16)
            mk.dma("sp", lambda: SP.dma_start(out=sks[:, 0:128], in_=sk1[:, :]), sks, writes=[sks])
            mk.dma("sp", lambda: SP.dma_start(out=sks[:, 128:256], in_=sk2[:, :]), sks, writes=[sks])
            mk.op("dve", lambda: V.tensor_copy(out=skb[:, :], in_=sks[:, :]), reads=[sks], writes=[skb])
            for a in range(2):
                mk.op("pe", lambda a=a: PE.transpose(out=bb(0)[:, a * 128:(a + 1) * 128], in_=skb[:, a * 128:(a + 1) * 128], identity=ident[:, :]), reads=[skb, ident], writes=[banks[0]])
            mk.op("dve", lambda: V.tensor_copy(out=skT[:, :, :], in_=bb(0)[:, 0:256].rearrange("p (a b) -> p a b", a=2)), reads=[banks[0]], writes=[skT])

            mk.barrier()
            mk.flush()
            pro.close()
            h1s = sb("h1s", [128, D], F32); hb = sb("hb", [128, D], BF16); hT = sb("hT", [128, 8, 128], BF16)
            qTb = sb("qTb", [128, 8, 128], BF16)
            PTx = sb("PTx", [128, 2, 4, 128], BF16)
            recs = sb("recs", [128, 4, 128], F32)
            oTn = sb("oTn", [128, 8, 128], BF16)
            rr2 = sb("rr2", [128, D], F32); H2L = [sb("h2_%d" % q, [128, D], F32) for q in range(2)]; lnw2 = sb("lnw2", [128, 32], F32)
            pqT = sb("pqT", [128, 16, 128], BF16)
            scr8 = sb("scr8", [128, 2048], F32)
            SCs = sb("SCs", [128, 16, 128], F32); SC2 = AV(scr8, lambda: scr8[:, :].rearrange("p (a b) -> p a b", a=16))
            V8 = sb("V8", [128, 16, 16], F32); I8 = sb("I8", [128, 16, 16], U32); I8f = sb("I8f", [128, 16, 16], F32)
            cand = AV(SCs, lambda: SCs[:, :, :].rearrange("p (h s) n -> p h (s n)", s=2)); cand2 = AV(scr8, lambda: scr8[:, :].rearrange("p (a b) -> p a b", a=8))
            T16 = sb("T16", [128, 8, 16], F32); P16 = sb("P16", [128, 8, 16], U32)
            Pa = sb("Pa", [128, 8, 16], U32); Pb = sb("Pb", [128, 8, 16], U32)
            Af = sb("Af", [128, 8, 16], F32); Bf = sb("Bf", [128, 8, 16], F32)
            OH = AV(scr8, lambda: scr8[:, :].rearrange("p (h r a) -> p h r a", h=8, r=16))
            i1s = sb("i1s", [128, 128], F32); i2s = sb("i2s", [128, 128], F32)
            EIDL = [sb("EID%d" % q, [128, 128], U32) for q in range(2)]
            GTL = [sb("GT%d" % q, [128, 8, 16], F32) for q in range(2)]; gz = sb("gz", [128, 16], F32)
            dots = sb("dots", [128, 128], F32); coef = sb("coef", [128, 128], F32)
            NB = 4
            GS = 4; NGB = 3
            UG = [sb("UG%d" % q, [128, 2 * D], BF16) for q in range(GS * NGB)]
            DG = [sb("DG%d" % q, [128, GS, 128], BF16) for q in range(NGB)]
            DSUB = [T(None, "dsub%d" % q) for q in range(128)]
            GLS = [T(None, "gls%d" % q) for q in range(128 // GS)]
            COEFS = [T(None, "coefs%d" % q) for q in range(128 // GS)]
            gl = sb("gl", [128, 128], F32)
            rr3 = rr2; yo = h1s

            def top16(src, srcT, scratch, scrT, vout, iout, tl):
                vt, it_ = tl
                mk.op("dve", lambda: V.max(out=vout[:, 0:8], in_=src), reads=[srcT], writes=[vt])
                mk.op("dve", lambda: V.max_index(out=iout[:, 0:8], in_max=vout[:, 0:8], in_values=src), reads=[srcT, vt], writes=[it_])
                mk.op("dve", lambda: V.match_replace(out=scratch, in_to_replace=vout[:, 0:8], in_values=src, imm_value=-1.0e30), reads=[srcT, vt], writes=[scrT])
                mk.op("dve", lambda: V.memset(dm2[:, 0:1], 0.0), writes=[dm2])
                mk.op("dve", lambda: V.max(out=vout[:, 8:16], in_=scratch), reads=[scrT], writes=[vt])
                mk.op("dve", lambda: V.max_index(out=iout[:, 8:16], in_max=vout[:, 8:16], in_values=scratch), reads=[scrT, vt], writes=[it_])

            def stage_P(i):
                h2 = H2L[i % 2]; EID = EIDL[i % 2]; GT = GTL[i % 2]
                mk.dma("sp", lambda i=i: SP.dma_start(out=h1s[:, :], in_=h1d[i * 128:(i + 1) * 128, :]), h1s, reads=[H1D], writes=[h1s])
                mk.op("act", lambda: A.copy(out=hb[:, :], in_=h1s[:, :]), reads=[h1s], writes=[hb])
                transpose8(hb, lambda bk: mk.op("dve", lambda: V.tensor_copy(out=hT[:, :, :], in_=bb(0).rearrange("p (a b) -> p a b", a=8)), reads=[bk], writes=[hT]), 0)
                for c in range(8):
                    bi = 1 + c // 4
                    for k in range(8):
                        mk.op("pe", lambda k=k, c=c, bi=bi: PE.matmul(bf(bi)[:, (c % 4) * 128:(c % 4 + 1) * 128], lhsT=wqb[:, k, c * 128:(c + 1) * 128], rhs=hT[:, k, :], start=(k == 0), stop=(k == 7)), reads=[wqb, hT], writes=[banks[bi]])
                for hf in range(2):
                    mk.op("act", lambda hf=hf: A.copy(out=qTb[:, hf * 4:(hf + 1) * 4, :], in_=bf(1 + hf).rearrange("p (a b) -> p a b", a=4)), reads=[banks[1 + hf]], writes=[qTb])
                for mt in range(2):
                    bi = 3 + mt
                    for h in range(4):
                        for cc in range(2):
                            c = 2 * h + cc
                            mk.op("pe", lambda h=h, c=c, cc=cc, mt=mt, bi=bi: PE.matmul(bf(bi)[:, h * 128:(h + 1) * 128], lhsT=KmT[:, c, mt * 128:(mt + 1) * 128], rhs=qTb[:, c, :], start=(cc == 0), stop=(cc == 1)), reads=[KmT, qTb], writes=[banks[bi]])
                    mk.op("act", lambda mt=mt, bi=bi: A.activation(out=PTx[:, mt, :, :], in_=bf(bi).rearrange("p (a b) -> p a b", a=4), func=AF.Exp, scale=1.0 / 16), reads=[banks[bi]], writes=[PTx])
                for c in range(8):
                    bi = (5, 1)[c // 4]
                    for mt in range(2):
                        mk.op("pe", lambda c=c, mt=mt, bi=bi: PE.matmul(bf(bi)[:, (c % 4) * 128:(c % 4 + 1) * 128], lhsT=Vm[:, mt, c * 128:(c + 1) * 128], rhs=PTx[:, mt, c // 2, :], start=(mt == 0), stop=(mt == 1)), reads=[Vm, PTx], writes=[banks[bi]])
                for h in range(4):
                    for mt in range(2):
                        mk.op("pe", lambda h=h, mt=mt: PE.matmul(bf(2)[:, h * 128:(h + 1) * 128], lhsT=ones[:, :], rhs=PTx[:, mt, h, :], start=(mt == 0), stop=(mt == 1)), reads=[ones, PTx], writes=[banks[2]])
                mk.op("dve", lambda: V.reciprocal(out=recs[:, :, :], in_=bf(2).rearrange("p (a b) -> p a b", a=4)), reads=[banks[2]], writes=[recs])
                for hf in range(2):
                    mk.op("dve", lambda hf=hf: V.tensor_tensor(out=oTn[:, hf * 4:(hf + 1) * 4, :].rearrange("p (h c) t -> p h c t", h=2), in0=bf((5, 1)[hf]).rearrange("p (h c t) -> p h c t", h=2, c=2), in1=recs[:, hf * 2:(hf + 1) * 2, :].unsqueeze(2).to_broadcast([128, 2, 2, 128]), op=ALU.mult), reads=[banks[(5, 1)[hf]], recs], writes=[oTn])
                for hf in range(2):
                    bi = 3 + hf
                    for k in range(8):
                        mk.op("pe", lambda k=k, hf=hf, bi=bi: PE.matmul(bf(bi)[:, :], lhsT=oTn[:, k, :], rhs=wob[:, k, hf * 512:(hf + 1) * 512], start=(k == 0), stop=(k == 7)), reads=[oTn, wob], writes=[banks[bi]])
                    mk.op("dve", lambda hf=hf, bi=bi: V.scalar_tensor_tensor(out=rr2[:, hf * 512:(hf + 1) * 512], in0=h1s[:, hf * 512:(hf + 1) * 512], scalar=ALPHA, in1=bf(bi)[:, :], op0=ALU.mult, op1=ALU.add), reads=[h1s, banks[bi]], writes=[rr2])
                layer_norm(rr2, g2, b2, h2, lnw2)
                if os.environ.get("MK_DBG") == "H2":
                    mk.dma("sp", lambda i=i: SP.dma_start(out=y[i * 128:(i + 1) * 128, :], in_=h2[:, :]), h2, reads=[h2])
                    return
                mk.op("act", lambda: A.copy(out=hb[:, :], in_=h2[:, :]), reads=[h2], writes=[hb])
                transpose8(hb, lambda bk: mk.op("dve", lambda: V.tensor_copy(out=hT[:, :, :], in_=bb(0).rearrange("p (a b) -> p a b", a=8)), reads=[bk], writes=[hT]), 0)
                for c in range(16):
                    bi = 1 + c // 4
                    for k in range(8):
                        mk.op("pe", lambda k=k, c=c, bi=bi: PE.matmul(bf(bi)[:, (c % 4) * 128:(c % 4 + 1) * 128], lhsT=wpqb[:, k, c * 128:(c + 1) * 128], rhs=hT[:, k, :], start=(k == 0), stop=(k == 7)), reads=[wpqb, hT], writes=[banks[bi]])
                for g4 in range(4):
                    mk.op("act", lambda g4=g4: A.copy(out=pqT[:, g4 * 4:(g4 + 1) * 4, :], in_=bf(1 + g4).rearrange("p (a b) -> p a b", a=4)), reads=[banks[1 + g4]], writes=[pqT])
                for c in range(16):
                    bi = (5, 0, 1, 2)[c // 4]
                    mk.op("pe", lambda c=c, bi=bi: PE.matmul(bf(bi)[:, (c % 4) * 128:(c % 4 + 1) * 128], lhsT=pqT[:, c, :], rhs=skT[:, c % 2, :], start=True, stop=True), reads=[pqT, skT], writes=[banks[bi]])
                    if c % 4 == 3:
                        g4 = c // 4
                        mk.op("dve", lambda g4=g4, bi=bi: V.tensor_copy(out=SCs[:, g4 * 4:(g4 + 1) * 4, :], in_=bf(bi).rearrange("p (a b) -> p a b", a=4)), reads=[banks[bi]], writes=[SCs])
                for c in range(16):
                    top16(SCs[:, c, :], SCs, SC2[:, c, :], SC2, V8[:, c, :], I8[:, c, :], (V8, I8))
                V8v = V8[:, :, :].rearrange("p (h s) a -> p h s a", s=2)
                mk.op("dve", lambda: V.tensor_tensor(out=cand[:, :, :].rearrange("p h (a b) -> p h a b", a=16), in0=V8v[:, :, 0, :].unsqueeze(3).to_broadcast([128, 8, 16, 16]), in1=V8v[:, :, 1, :].unsqueeze(2).to_broadcast([128, 8, 16, 16]), op=ALU.add), reads=[V8], writes=[cand])
                for h in range(8):
                    top16(cand[:, h, :], cand, cand2[:, h, :], cand2, T16[:, h, :], P16[:, h, :], (T16, P16))
                mk.op("dve", lambda: V.tensor_single_scalar(out=Pa[:, :, :], in_=P16[:, :, :], scalar=4, op=ALU.logical_shift_right), reads=[P16], writes=[Pa])
                mk.op("dve", lambda: V.tensor_single_scalar(out=Pb[:, :, :], in_=P16[:, :, :], scalar=15, op=ALU.bitwise_and), reads=[P16], writes=[Pb])
                mk.op("dve", lambda: V.tensor_copy(out=Af[:, :, :], in_=Pa[:, :, :]), reads=[Pa], writes=[Af])
                mk.op("dve", lambda: V.tensor_copy(out=Bf[:, :, :], in_=Pb[:, :, :]), reads=[Pb], writes=[Bf])
                mk.op("dve", lambda: V.tensor_copy(out=I8f[:, :, :], in_=I8[:, :, :]), reads=[I8], writes=[I8f])
                I8v = I8f[:, :, :].rearrange("p (h s) a -> p h s a", s=2)
                for (sel, which, dst) in ((Af, 0, i1s), (Bf, 1, i2s)):
                    mk.op("dve", lambda sel=sel: V.tensor_tensor(out=OH[:, :, :, :], in0=sel[:, :, :].unsqueeze(3).to_broadcast([128, 8, 16, 16]), in1=IO16[:, :].unsqueeze(1).unsqueeze(1).to_broadcast([128, 8, 16, 16]), op=ALU.is_equal), reads=[sel, IO16], writes=[OH])
                    mk.op("dve", lambda which=which: V.tensor_tensor(out=OH[:, :, :, :], in0=OH[:, :, :, :], in1=I8v[:, :, which, :].unsqueeze(2).to_broadcast([128, 8, 16, 16]), op=ALU.mult), reads=[OH, I8f], writes=[OH])
                    mk.op("dve", lambda dst=dst: V.tensor_reduce(out=dst[:, :], in_=OH[:, :, :, :].rearrange("p h r a -> p (h r) a"), axis=AX.X, op=ALU.add), reads=[OH], writes=[dst])
                mk.op("dve", lambda: V.scalar_tensor_tensor(out=i1s[:, :], in0=i1s[:, :], scalar=128.0, in1=i2s[:, :], op0=ALU.mult, op1=ALU.add), reads=[i1s, i2s], writes=[i1s])
                mk.op("dve", lambda: V.tensor_copy(out=EID[:, :], in_=i1s[:, :]), reads=[i1s], writes=[EID])
                mk.op("dve", lambda: V.tensor_tensor(out=GT[:, :, :], in0=T16[:, :, :], in1=T16[:, :, 0:1].to_broadcast([128, 8, 16]), op=ALU.subtract), reads=[T16], writes=[GT])
                mk.op("act", lambda: A.activation(out=GT[:, :, :], in_=GT[:, :, :], func=AF.Exp), reads=[GT], writes=[GT])
                mk.op("dve", lambda: V.tensor_reduce(out=gz[:, 0:8], in_=GT[:, :, :], axis=AX.X, op=ALU.add), reads=[GT], writes=[gz])
                mk.op("dve", lambda: V.reciprocal(out=gz[:, 8:16], in_=gz[:, 0:8]), reads=[gz], writes=[gz])
                mk.op("dve", lambda: V.tensor_tensor(out=GT[:, :, :], in0=GT[:, :, :], in1=gz[:, 8:16].unsqueeze(2).to_broadcast([128, 8, 16]), op=ALU.mult), reads=[GT, gz], writes=[GT])

            def stage_G(i, pend_ops):
                per_grp = (len(pend_ops) + 27) // 28
                h2 = H2L[i % 2]; EID = EIDL[i % 2]; GT = GTL[i % 2]
                GTf = GT[:, :, :].rearrange("p h r -> p (h r)")
                for g in range(128 // GS):
                    bufs = [UG[(g % NGB) * GS + j] for j in range(GS)]
                    dg = DG[g % NGB]
                    for j in range(GS):
                        s_ = g * GS + j
                        ug = bufs[j]
                        mk.dma("pool", lambda s_=s_, ug=ug: G.indirect_dma_start(out=ug[:, :], out_offset=None, in_=edu[:, :], in_offset=bass.IndirectOffsetOnAxis(ap=EID[:, s_:s_ + 1], axis=0)), ug, reads=[EID, EDU], writes=[ug])
                        mk.op("dve", lambda s_=s_, ug=ug: V.scalar_tensor_tensor(out=ug[:, 0:D], in0=ug[:, 0:D], scalar=1.0, in1=h2[:, :], op0=ALU.mult, op1=ALU.mult, accum_out=dots[:, s_:s_ + 1]), reads=[ug, h2], writes=[ug, DSUB[s_]])
                    dsl = [DSUB[g * GS + j] for j in range(GS)]
                    mk.op("dve", lambda: V.memset(dm2[:, 0:1], 0.0), writes=[dm2] + dsl)
                    mk.op("act", lambda g=g: A.activation(out=gl[:, g * GS:(g + 1) * GS], in_=dots[:, g * GS:(g + 1) * GS], func=AF.Gelu), reads=dsl, writes=[GLS[g]])
                    mk.op("dve", lambda g=g: V.tensor_tensor(out=coef[:, g * GS:(g + 1) * GS], in0=gl[:, g * GS:(g + 1) * GS], in1=GTf[:, g * GS:(g + 1) * GS], op=ALU.mult), reads=[GLS[g], GT], writes=[COEFS[g]])
                    mk.op("dve", lambda g=g, dg=dg: V.tensor_tensor(out=dg[:, :, :], in0=ident[:, :].unsqueeze(1).to_broadcast([128, GS, 128]), in1=coef[:, g * GS:(g + 1) * GS].unsqueeze(2).to_broadcast([128, GS, 128]), op=ALU.mult), reads=[ident, COEFS[g]], writes=[dg])
                    for j in range(GS):
                        s_ = g * GS + j
                        ug = bufs[j]
                        for hf in range(2):
                            mk.op("pe", lambda s_=s_, ug=ug, dg=dg, j=j, hf=hf: PE.matmul(bf(6 + hf)[:, :], lhsT=dg[:, j, :], rhs=ug[:, D + hf * 512:D + (hf + 1) * 512], start=(s_ == 0), stop=(s_ == 127)), reads=[dg, ug], writes=[banks[6 + hf]])
                    mk.replay(pend_ops, per_grp)
                mk.replay(pend_ops, len(pend_ops))
                for hf in range(2):
                    mk.op("dve", lambda hf=hf: V.scalar_tensor_tensor(out=rr3[:, hf * 512:(hf + 1) * 512], in0=h2[:, hf * 512:(hf + 1) * 512], scalar=ALPHA, in1=bf(6 + hf)[:, :], op0=ALU.mult, op1=ALU.add), reads=[h2, banks[6 + hf]], writes=[rr3])
                layer_norm(rr3, g3, b3, yo, lnw2)
                mk.dma("sp", lambda i=i: SP.dma_start(out=y[i * 128:(i + 1) * 128, :], in_=yo[:, :]), yo, reads=[yo])

            stage_P(0)
            for i in range(NT):
                pend_ops = []
                if i + 1 < NT:
                    mk.defer = pend_ops
                    stage_P(i + 1)
                    mk.defer = None
                stage_G(i, pend_ops)
            mk.barrier()
            mk.flush()
        if "B" not in phases:
            with contextlib.ExitStack() as sd:
                tmp = mk.sb("dbgt", [128, D], F32, st=sd)
                for i in range(NT):
                    mk.dma("sp", lambda i=i: SP.dma_start(out=tmp[:, :], in_=h1d[i * 128:(i + 1) * 128, :]), tmp, reads=[H1D], writes=[tmp])
                    mk.dma("sp", lambda i=i: SP.dma_start(out=y[i * 128:(i + 1) * 128, :], in_=tmp[:, :]), tmp, reads=[tmp])
                mk.barrier()
                mk.flush()
        nw = mk.flush(final=True)
    return nc


_NC_CACHE = {}


def _in_map(inp, b, S):
    NT = S // 128
    f = lambda a: np.ascontiguousarray(np.asarray(a, dtype=np.float32))
    return {
        "x": f(inp["x"][b]), "pos": np.ascontiguousarray(np.asarray(inp["positions"][b]).astype(np.int32).reshape(NT, 128).T),
        "mem": f(inp["mem"][b]), "w_in": f(inp["w_in"][0]), "gate_up": f(inp["gla_gate_up"][0]),
        "gate_bias": f(inp["gla_gate_bias"][0]).reshape(1, -1), "norm_g": f(inp["gla_norm_g"][0]).reshape(1, -1),
        "w_out": f(inp["w_out"][0]), "ln1g": f(inp["ln_mix_g"][0]).reshape(1, -1), "ln1b": f(inp["ln_mix_b"][0]).reshape(1, -1),
        "wq": f(inp["xattn_w_q"][0]), "wk": f(inp["xattn_w_k"][0]), "wv": f(inp["xattn_w_v"][0]), "wo": f(inp["xattn_w_o"][0]),
        "ln2g": f(inp["ln_mem_g"][0]).reshape(1, -1), "ln2b": f(inp["ln_mem_b"][0]).reshape(1, -1),
        "wpq": f(inp["peer_w_query"][0]), "sk1": f(inp["peer_sub_keys_1"][0]), "sk2": f(inp["peer_sub_keys_2"][0]),
        "edown": f(inp["peer_expert_down"][0]), "eup": f(inp["peer_expert_up"][0]),
        "ln3g": f(inp["ln_ffn_g"][0]).reshape(1, -1), "ln3b": f(inp["ln_ffn_b"][0]).reshape(1, -1),
    }


def run(inp, phases="AB", cores=None):
    B, S, _ = inp["x"].shape
    n_sel = min(256, S // 4)
    key = (S, n_sel, phases)
    if key not in _NC_CACHE:
        _NC_CACHE[key] = build(S, n_sel, phases)
    nc = _NC_CACHE[key]
    cores = list(range(B)) if cores is None else cores
    in_maps = [_in_map(inp, b, S) for b in cores]
    res = run_bass_kernel_spmd(nc, in_maps, core_ids=list(range(len(cores))))
    return np.stack([np.asarray(r["y"], dtype=np.float32) for r in res.results], axis=0)


def kernel(**inputs):
    return run(inputs, "AB")
```

```python
import contextlib
import numpy as np
import concourse.bass as bass
import concourse.mybir as mybir

F32 = mybir.dt.float32
BF16 = mybir.dt.bfloat16
I32 = mybir.dt.int32
U32 = mybir.dt.uint32
U16 = mybir.dt.uint16
ALU = mybir.AluOpType
AF = mybir.ActivationFunctionType
AX = mybir.AxisListType

from concourse.bass_utils import run_bass_kernel_spmd

ENGS = ("pe", "act", "dve", "pool", "sp")


class T:
    __slots__ = ("h", "name", "w", "r", "dl", "dlr", "sem", "dcnt")

    def __init__(self, h, name):
        self.h = h
        self.name = name
        self.w = None
        self.r = {}
        self.dl = {}
        self.dlr = {}
        self.sem = None
        self.dcnt = 0

    def __getitem__(self, k):
        return self.h[k]


class TV:
    def __init__(self, t, c0, c1):
        object.__setattr__(self, "_t", t); object.__setattr__(self, "_c0", c0); object.__setattr__(self, "_c1", c1)

    def __getattr__(self, k):
        return getattr(object.__getattribute__(self, "_t"), k)

    def __setattr__(self, k, v):
        setattr(object.__getattribute__(self, "_t"), k, v)

    def __getitem__(self, key):
        p, c = key
        c0 = object.__getattribute__(self, "_c0"); c1 = object.__getattribute__(self, "_c1")
        a = c0 + (c.start or 0); b = c0 + (c.stop if c.stop is not None else (c1 - c0))
        return object.__getattribute__(self, "_t").h[p, a:b]


class AV(TV):
    def __init__(self, t, fn):
        object.__setattr__(self, "_t", t); object.__setattr__(self, "_fn", fn)

    def __getitem__(self, key):
        return object.__getattribute__(self, "_fn")()[key]


class Op:
    __slots__ = ("eng", "fn", "waits", "dwaits", "idx", "need_inc", "dma_tile", "dma_val")

    def __init__(self, eng, fn):
        self.eng = eng
        self.fn = fn
        self.waits = []
        self.dwaits = []
        self.idx = None
        self.need_inc = False
        self.dma_tile = None
        self.dma_val = 0


class MK:
    def __init__(self, nc, st):
        self.nc = nc
        self.st = st
        self.ops = []
        self.cnt = {e: 0 for e in ENGS}
        self.eng_ops = {e: [] for e in ENGS}
        self.seen = {e: {o: -1 for o in ENGS} for e in ENGS}
        self.dseen = {e: {} for e in ENGS}
        self.tiles = []
        self.pend = {e: None for e in ENGS}
        self.defer = None

    def sb(self, name, shape, dt, st=None):
        h = (st or self.st).enter_context(self.nc.sbuf_tensor(name, list(shape), dt))
        t = T(h, name)
        self.tiles.append(t)
        return t

    def ps(self, name, shape, dt, st=None):
        h = (st or self.st).enter_context(self.nc.psum_tensor(name, list(shape), dt))
        t = T(h, name)
        self.tiles.append(t)
        return t

    def alias(self, name, h):
        t = T(h, name)
        self.tiles.append(t)
        return t

    def _dep(self, op, eng, tgt):
        if tgt is None:
            return
        te, ti = tgt
        if te == eng:
            return
        if self.seen[eng][te] >= ti:
            return
        op.waits.append((te, ti))
        self.seen[eng][te] = ti

    def barrier(self):
        for e in ENGS:
            w = [(o, self.cnt[o] - 1) for o in ENGS if o != e and self.cnt[o] > 0]
            d = [(t, t.dcnt) for t in self.tiles if t.sem is not None and t.dcnt > 0]
            self.pend[e] = (w, d)

    def _apply_pend(self, o, eng):
        p = self.pend[eng]
        if p is None:
            return
        self.pend[eng] = None
        for tgt in p[0]:
            self._dep_any(o, eng, tgt)
        for (t, v) in p[1]:
            if self.dseen[eng].get(id(t), 0) < v:
                o.dwaits.append((t, v))
                self.dseen[eng][id(t)] = v

    def replay(self, lst, n):
        assert self.defer is None
        for _ in range(min(n, len(lst))):
            it = lst.pop(0)
            if it[0] == "op":
                self.op(*it[1:])
            else:
                self.dma(*it[1:])

    def op(self, eng, fn, reads=(), writes=()):
        import os
        if self.defer is not None:
            self.defer.append(("op", eng, fn, list(reads), list(writes)))
            return None
        reads = [object.__getattribute__(t, "_t") if isinstance(t, TV) else t for t in reads]
        writes = [object.__getattribute__(t, "_t") if isinstance(t, TV) else t for t in writes]
        if len(self.ops) >= int(os.environ.get("MK_MAXOPS", "100000000")):
            return None
        o = Op(eng, fn)
        self._apply_pend(o, eng)
        for t in reads:
            if t.w is not None:
                if t.w[0] == eng:
                    if eng != "pe" and self.seen[eng][eng] < t.w[1]:
                        o.waits.append(t.w)
                        self.seen[eng][eng] = t.w[1]
                else:
                    self._dep(o, eng, t.w)
            self._ddep(o, eng, t, False)
        for t in writes:
            if t.w is not None and t.w[0] != eng:
                self._dep(o, eng, t.w)
            for re_, ri in t.r.items():
                if re_ != eng:
                    self._dep(o, eng, (re_, ri))
            self._ddep(o, eng, t)
        o.idx = self.cnt[eng]
        self.cnt[eng] += 1
        self.ops.append(o)
        self.eng_ops[eng].append(o)
        for t in reads:
            t.r[eng] = o.idx
        for t in writes:
            t.w = (eng, o.idx)
            t.r = {}
        return o

    def _ddep(self, o, eng, t, write=True):
        for d in ((t.dl, t.dlr) if write else (t.dl,)):
            for k, (stile, v) in d.items():
                if self.dseen[eng].get(k, 0) < v:
                    o.dwaits.append((stile, v))
                    self.dseen[eng][k] = v

    def _dep_any(self, o, eng, tgt):
        te, ti = tgt
        if self.seen[eng][te] >= ti:
            return
        o.waits.append((te, ti))
        self.seen[eng][te] = ti

    def dma(self, eng, fn, semtile, reads=(), writes=()):
        import os
        if self.defer is not None:
            self.defer.append(("dma", eng, fn, semtile, list(reads), list(writes)))
            return None
        if len(self.ops) >= int(os.environ.get("MK_MAXOPS", "100000000")):
            return None
        reads = [object.__getattribute__(t, "_t") if isinstance(t, TV) else t for t in reads]
        writes = [object.__getattribute__(t, "_t") if isinstance(t, TV) else t for t in writes]
        if isinstance(semtile, TV):
            semtile = object.__getattribute__(semtile, "_t")
        o = Op(eng, fn)
        self._apply_pend(o, eng)
        for t in reads:
            if t.w is not None:
                self._dep_any(o, eng, t.w)
            self._ddep(o, eng, t, False)
        for t in writes:
            if t.w is not None:
                self._dep_any(o, eng, t.w)
            for re_, ri in t.r.items():
                self._dep_any(o, eng, (re_, ri))
            self._ddep(o, eng, t)
        if semtile.sem is None:
            semtile.sem = self.st.enter_context(self.nc.semaphore("d_" + semtile.name))
        semtile.dcnt += 16
        o.dma_tile = semtile
        o.dma_val = semtile.dcnt
        self.ops.append(o)
        for t in reads:
            t.dlr[id(semtile)] = (semtile, semtile.dcnt)
        for t in writes:
            t.dl[id(semtile)] = (semtile, semtile.dcnt)
        for t in writes:
            t.w = None
            t.r = {}
        return o

    def flush(self, final=False):
        nc = self.nc
        if not hasattr(self, "esem"):
            self.esem = {e: self.st.enter_context(nc.semaphore("e_" + e)) for e in ENGS}
            self.ordinal = {e: [] for e in ENGS}
            self.eflushed = {e: 0 for e in ENGS}
            self.flushed = 0
            self.n_wait = 0
        E = {"pe": nc.tensor, "act": nc.scalar, "dve": nc.vector, "pool": nc.gpsimd, "sp": nc.sync}
        chunk = self.ops[self.flushed:]
        for e in ENGS:
            if len(self.eng_ops[e]) > self.eflushed[e]:
                self.eng_ops[e][-1].need_inc = True
        for o in chunk:
            for (te, ti) in o.waits:
                if ti >= self.eflushed[te]:
                    self.eng_ops[te][ti].need_inc = True
        for e in ENGS:
            c = self.ordinal[e][-1] if self.ordinal[e] else 0
            for o in self.eng_ops[e][self.eflushed[e]:]:
                if o.need_inc:
                    c += 1
                self.ordinal[e].append(c)
        for o in chunk:
            eng = E[o.eng]
            for (te, ti) in o.waits:
                v = self.ordinal[te][ti] if self.eng_ops[te][ti].need_inc else self.ordinal[te][ti] + 1
                eng.wait_ge(self.esem[te], v)
                self.n_wait += 1
            for (t, v) in o.dwaits:
                eng.wait_ge(t.sem, v)
                self.n_wait += 1
            inst = o.fn()
            if o.dma_tile is not None:
                inst.then_inc(o.dma_tile.sem, 16)
            elif o.need_inc:
                inst.then_inc(self.esem[o.eng], 1)
        self.flushed = len(self.ops)
        for e in ENGS:
            self.eflushed[e] = len(self.eng_ops[e])
        if final:
            for t in self.tiles:
                if t.sem is not None and t.dcnt > 0:
                    nc.sync.wait_ge(t.sem, t.dcnt)
        return self.n_wait

D = 1024
INW = 3384
ALPHA = 2.0 ** 0.25
THETA = 500000.0
NEG = -1.0e30


def build(S, n_sel, phases="AB", dbg=False):
    import os
    STOP = int(os.environ.get("MK_STOP", "99"))
    NT = S // 128
    nc = bass.Bass("TRN2", target_bir_lowering=False)
    V, A, G, PE, SP = nc.vector, nc.scalar, nc.gpsimd, nc.tensor, nc.sync

    def din(name, shape, dt=F32):
        return nc.dram_tensor(name, shape, dt, kind="ExternalInput").ap()

    x = din("x", [S, D]); pos = din("pos", [128, NT], I32); mem = din("mem", [256, D])
    w_in = din("w_in", [D, INW]); gate_up = din("gate_up", [16, 256]); gate_bias = din("gate_bias", [1, 256])
    norm_g = din("norm_g", [1, 128]); w_out = din("w_out", [D, D])
    ln1g = din("ln1g", [1, D]); ln1b = din("ln1b", [1, D])
    wq = din("wq", [D, D]); wk = din("wk", [D, D]); wv = din("wv", [D, D]); wo = din("wo", [D, D])
    ln2g = din("ln2g", [1, D]); ln2b = din("ln2b", [1, D])
    wpq = din("wpq", [D, 2048]); sk1 = din("sk1", [128, 128]); sk2 = din("sk2", [128, 128])
    edown = din("edown", [16384, D]); eup = din("eup", [16384, D])
    ln3g = din("ln3g", [1, D]); ln3b = din("ln3b", [1, D])
    y = nc.dram_tensor("y", [S, D], F32, kind="ExternalOutput").ap()
    h1d = nc.dram_tensor("h1d", [S, D], F32, kind="Internal").ap()
    woutd = nc.dram_tensor("woutd", [D, D], BF16, kind="Internal").ap()
    edu = nc.dram_tensor("edu", [16384, 2 * D], BF16, kind="Internal").ap()

    with contextlib.ExitStack() as st:
        mk = MK(nc, st)
        H1D = mk.alias("h1dT", None)
        EDU = mk.alias("eduT", None)
        WOD = mk.alias("woutdT", None)
        conv_jobs = [(c, src, off) for c in range(16) for (src, off) in ((edown, 0), (eup, D))]

        def conv_issue(n):
            for _ in range(n):
                if conv_jobs:
                    c, src, off = conv_jobs.pop(0)
                    mk.dma("pool", lambda c=c, src=src, off=off: G.dma_start(out=edu[c * 1024:(c + 1) * 1024, off:off + D].rearrange("(p a) d -> p a d", p=128), in_=src[c * 1024:(c + 1) * 1024, :].rearrange("(p a) d -> p a d", p=128), max_dma_last_dim=4096), EDU, writes=[EDU])
        banks = [mk.ps("bank%d" % i, [128, 512], F32) for i in range(8)]

        def bf(i):
            return banks[i][:, :]

        def bb(i):
            return banks[i][:, :].bitcast(BF16)

        identf = mk.sb("identf", [128, 128], F32)
        ident = mk.sb("ident", [128, 128], BF16)
        mk.op("pool", lambda: G.memset(identf[:, :], 1.0), writes=[identf])
        mk.op("pool", lambda: G.affine_select(out=identf[:, :], in_=identf[:, :], pattern=[[-1, 128]], compare_op=ALU.is_equal, fill=0.0, base=0, channel_multiplier=1), reads=[identf], writes=[identf])
        mk.op("dve", lambda: V.tensor_copy(out=ident[:, :], in_=identf[:, :]), reads=[identf], writes=[ident])
        negI = mk.sb("negI", [128, 4, 128], BF16)
        mk.op("dve", lambda: V.tensor_scalar(out=negI[:, :, :], in0=identf[:, :].unsqueeze(1).to_broadcast([128, 4, 128]), scalar1=-30000.0, scalar2=None, op0=ALU.mult), reads=[identf], writes=[negI])

        def ln_consts(gd, bd, stk, nm, dt=F32):
            g_t = mk.sb(nm + "gs", [128, D], dt, st=stk)
            b_t = mk.sb(nm + "bs", [128, D], dt, st=stk)
            if dt == F32:
                mk.dma("sp", lambda: SP.dma_start(out=g_t[:, :], in_=gd.broadcast_to([128, D])), g_t, writes=[g_t])
                mk.dma("sp", lambda: SP.dma_start(out=b_t[:, :], in_=bd.broadcast_to([128, D])), b_t, writes=[b_t])
            else:
                mk.dma("pool", lambda: G.dma_start(out=g_t[:, :], in_=gd.broadcast_to([128, D])), g_t, writes=[g_t])
                mk.dma("pool", lambda: G.dma_start(out=b_t[:, :], in_=bd.broadcast_to([128, D])), b_t, writes=[b_t])
            return g_t, b_t

        def layer_norm(r_t, g_t, b_t, out_t, wk_t):
            mk.op("dve", lambda: V.bn_stats(out=wk_t[:, 0:6], in_=r_t[:, 0:512]), reads=[r_t], writes=[wk_t])
            mk.op("dve", lambda: V.bn_stats(out=wk_t[:, 6:12], in_=r_t[:, 512:1024]), reads=[r_t], writes=[wk_t])
            mk.op("dve", lambda: V.bn_aggr(out=wk_t[:, 12:14], in_=wk_t[:, 0:12]), reads=[wk_t], writes=[wk_t])
            mk.op("dve", lambda: V.tensor_scalar_add(out=wk_t[:, 14:15], in0=wk_t[:, 13:14], scalar1=1e-5), reads=[wk_t], writes=[wk_t])
            mk.op("act", lambda: A.sqrt(out=wk_t[:, 15:16], in_=wk_t[:, 14:15]), reads=[wk_t], writes=[wk_t])
            mk.op("dve", lambda: V.reciprocal(out=wk_t[:, 16:17], in_=wk_t[:, 15:16]), reads=[wk_t], writes=[wk_t])
            mk.op("dve", lambda: V.tensor_scalar(out=r_t[:, :], in0=r_t[:, :], scalar1=wk_t[:, 12:13], scalar2=wk_t[:, 16:17], op0=ALU.subtract, op1=ALU.mult), reads=[r_t, wk_t], writes=[r_t])
            mk.op("pool", lambda: G.tensor_tensor(out=r_t[:, :], in0=r_t[:, :], in1=g_t[:, :], op=ALU.mult), reads=[r_t, g_t], writes=[r_t])
            mk.op("pool", lambda: G.tensor_tensor(out=out_t[:, :], in0=r_t[:, :], in1=b_t[:, :], op=ALU.add), reads=[r_t, b_t], writes=[out_t])

        def transpose8(src_bf, dst_fn, bank_i, nblk=8):
            bk = banks[bank_i]
            for k in range(nblk):
                mk.op("pe", lambda k=k: PE.transpose(out=bb(bank_i)[:, k * 128:(k + 1) * 128], in_=src_bf[:, k * 128:(k + 1) * 128], identity=ident[:, :]), reads=[src_bf, ident], writes=[bk])
            dst_fn(bk)

        if "A" in phases:
          with contextlib.ExitStack() as sa:
            def sb(name, shape, dt):
                return mk.sb(name, shape, dt, st=sa)
            winb = sb("winb", [128, 8, INW], BF16)
            WQ = [sb("WQ%d" % q, [128, 8, 256], BF16) for q in range(2)]
            for k in range(8):
                mk.dma("pool", lambda k=k: G.dma_start(out=woutd[k * 128:(k + 1) * 128, :], in_=w_out[k * 128:(k + 1) * 128, :], max_dma_last_dim=4096), WOD, writes=[WOD])

            def wq_load(q):
                wt = WQ[q % 2]
                mk.dma("sp", lambda q=q, wt=wt: SP.dma_start(out=wt[:, :, :], in_=woutd[:, q * 256:(q + 1) * 256].rearrange("(k p) n -> p k n", p=128)), wt, reads=[WOD], writes=[wt])
            for k in range(8):
                mk.dma("pool", lambda k=k: G.dma_start(out=winb[:, k, :], in_=w_in[k * 128:(k + 1) * 128, :], max_dma_last_dim=4096), winb, writes=[winb])
            g1, b1 = ln_consts(ln1g, ln1b, sa, "ln1", BF16)
            kTc = sb("kTc", [128, 4, S], BF16)
            Vc = sb("Vc", [128, NT, 8, 66], BF16)
            kidxT = sb("kidxT", [128, S], BF16)
            mk.op("pool", lambda: G.memset(Vc[:, :, :, :], 1.0), writes=[Vc])
            CB = sb("CB", [128, 128], F32)
            mk.op("pool", lambda: G.memset(CB[:, :], 0.0), writes=[CB])
            mk.op("pool", lambda: G.affine_select(out=CB[:, :], in_=CB[:, :], pattern=[[-1, 128]], compare_op=ALU.is_ge, fill=NEG, base=0, channel_multiplier=1), reads=[CB], writes=[CB])
            MT = sb("MT", [128, 128], F32)
            mk.op("pool", lambda: G.memset(MT[:, :], 1.0), writes=[MT])
            mk.op("pool", lambda: G.affine_select(out=MT[:, :], in_=MT[:, :], pattern=[[1, 128]], compare_op=ALU.is_ge, fill=0.0, base=0, channel_multiplier=-1), reads=[MT], writes=[MT])
            TRI = sb("TRI", [128, 128], F32)
            mk.op("pool", lambda: G.memset(TRI[:, :], -1.0 / 16), writes=[TRI])
            mk.op("pool", lambda: G.affine_select(out=TRI[:, :], in_=TRI[:, :], pattern=[[1, 128]], compare_op=ALU.is_ge, fill=0.0, base=0, channel_multiplier=-1), reads=[TRI], writes=[TRI])
            TRU = sb("TRU", [128, 128], F32)
            mk.op("pool", lambda: G.memset(TRU[:, :], -1.0 / 16), writes=[TRU])
            mk.op("pool", lambda: G.affine_select(out=TRU[:, :], in_=TRU[:, :], pattern=[[-1, 128]], compare_op=ALU.is_gt, fill=0.0, base=0, channel_multiplier=1), reads=[TRU], writes=[TRU])
            POW2 = sb("POW2", [128, 16], F32)
            for j in range(16):
                mk.op("pool", lambda j=j: G.memset(POW2[:, j:j + 1], 2.0 ** -(j + 1)), writes=[POW2])
            F12 = sb("F12", [128, 12], F32)
            for j in range(12):
                f = THETA ** (-(j / 8.0)) if j < 8 else THETA ** (-((j - 8) / 4.0))
                mk.op("pool", lambda j=j, f=f: G.memset(F12[:, j:j + 1], f / (2 * np.pi)), writes=[F12])
            posi = sb("posi", [128, NT], I32)
            posf = sb("posf", [128, NT], F32)
            PROJ = sb("PROJ", [128, INW], F32)
            class _V3:
                def __init__(self, c0, dt=None):
                    self.c0 = c0; self.dt = dt
                def __getitem__(self, key):
                    a = PROJ[:, self.c0:self.c0 + NT * 24]
                    if self.dt is not None:
                        a = a.bitcast(self.dt)
                    return a.rearrange("p (t f) -> p t f", t=NT)[key]
            SCt = _V3(0); SCn = _V3(NT * 24, I32); SCf = _V3(2 * NT * 24); SCm = _V3(3 * NT * 24)
            SC = sb("SC", [128, NT, 24], F32)
            mk.dma("sp", lambda: SP.dma_start(out=posi[:, :], in_=pos[:, :]), posi, writes=[posi])
            mk.op("dve", lambda: V.tensor_copy(out=posf[:, :], in_=posi[:, :]), reads=[posi], writes=[posf])
            mk.op("dve", lambda: V.tensor_tensor(out=SCt[:, :, 0:12], in0=posf[:, :].unsqueeze(2).to_broadcast([128, NT, 12]), in1=F12[:, :].unsqueeze(1).to_broadcast([128, NT, 12]), op=ALU.mult), reads=[posf, F12], writes=[PROJ])
            mk.op("dve", lambda: V.tensor_scalar_add(out=SCt[:, :, 12:24], in0=SCt[:, :, 0:12], scalar1=0.25), reads=[PROJ], writes=[PROJ])
            mk.op("dve", lambda: V.tensor_copy(out=SCn[:, :, :], in_=SCt[:, :, :]), reads=[PROJ], writes=[PROJ])
            mk.op("dve", lambda: V.tensor_copy(out=SCf[:, :, :], in_=SCn[:, :, :]), reads=[PROJ], writes=[PROJ])
            mk.op("dve", lambda: V.tensor_tensor(out=SCt[:, :, :], in0=SCt[:, :, :], in1=SCf[:, :, :], op=ALU.subtract), reads=[PROJ], writes=[PROJ])
            mk.op("dve", lambda: V.tensor_single_scalar(out=SCm[:, :, :], in_=SCt[:, :, :], scalar=0.5, op=ALU.is_gt), reads=[PROJ], writes=[PROJ])
            mk.op("dve", lambda: V.tensor_tensor(out=SCt[:, :, :], in0=SCt[:, :, :], in1=SCm[:, :, :], op=ALU.subtract), reads=[PROJ], writes=[PROJ])
            mk.op("dve", lambda: V.tensor_single_scalar(out=SCm[:, :, :], in_=SCt[:, :, :], scalar=-0.5, op=ALU.is_lt), reads=[PROJ], writes=[PROJ])
            mk.op("dve", lambda: V.tensor_tensor(out=SCt[:, :, :], in0=SCt[:, :, :], in1=SCm[:, :, :], op=ALU.add), reads=[PROJ], writes=[PROJ])
            mk.op("act", lambda: A.activation(out=SC[:, :, :], in_=SCt[:, :, :], func=AF.Sin, scale=2 * np.pi * (1 - 2e-6)), reads=[PROJ], writes=[SC])
            GU = sb("GU", [17, 256], F32)
            mk.dma("sp", lambda: SP.dma_start(out=GU[0:16, :], in_=gate_up[:, :]), GU, writes=[GU])
            mk.dma("sp", lambda: SP.dma_start(out=GU[16:17, :], in_=gate_bias[:, :]), GU, writes=[GU])
            NG = sb("NG", [128, 128], F32)
            mk.dma("sp", lambda: SP.dma_start(out=NG[:, :], in_=norm_g.broadcast_to([128, 128])), NG, writes=[NG])
            glrT = sb("glrT", [17, 128], F32)
            mk.op("pool", lambda: G.memset(glrT[:, :], 1.0), writes=[glrT])
            Sst = sb("Sst", [64, 4, 128], F32)
            Sb = sb("Sb", [64, 4, 128], BF16)
            mk.op("pool", lambda: G.memset(Sst[:, :, :], 0.0), writes=[Sst])
            mk.op("pool", lambda: G.memset(Sb[:, :, :], 0.0), writes=[Sb])

            xs = sb("xs", [128, D], F32)
            xb = sb("xb", [128, D], BF16)
            xT = sb("xT", [128, 8, 128], BF16)
            GQK = sb("GQK", [64, 8, 128], BF16)
            QK16 = xb
            IDX16 = sb("IDX16", [128, 384], BF16)
            qT = sb("qT", [128, 4, 128], BF16)
            qidxT = sb("qidxT", [128, 2, 128], BF16)
            wst = sb("wst", [128, 16], F32)
            Ssc = sb("Ssc", [128, max(S, 3200)], F32)
            MB = sb("MB", [128, S], BF16)
            rtmp = [sb("rtmp%d" % i, [128, 512], F32) for i in range(2)]
            t1, t2, t3, t4 = [AV(rtmp[1], (lambda q=q: rtmp[1][:, q * 128:(q + 1) * 128].rearrange("p (a b) -> p a b", a=16))) for q in range(4)]
            sq = rtmp[0]; sil = rtmp[1]
            Lg = AV(Ssc, lambda: Ssc[:, 0:256])
            E1 = AV(Ssc, lambda: Ssc[0:64, 256:768].rearrange("p (a b) -> p a b", a=4))
            E2 = AV(Ssc, lambda: Ssc[0:64, 768:1280].rearrange("p (a b) -> p a b", a=4))
            Ee = AV(Ssc, lambda: Ssc[:, 1280:1536])
            og = AV(Ssc, lambda: Ssc[:, 1536:2048].rearrange("p (a b) -> p a b", a=4))
            qdT = AV(Ssc, lambda: Ssc[0:64, 2048:2304].bitcast(BF16).rearrange("p (a b) -> p a b", a=4))
            kiT = AV(Ssc, lambda: Ssc[0:64, 2304:2560].bitcast(BF16).rearrange("p (a b) -> p a b", a=4))
            kte = AV(Ssc, lambda: Ssc[:, 2560:2688].bitcast(BF16))
            gvb = AV(Ssc, lambda: Ssc[:, 2688:2944].bitcast(BF16))
            ATb = AV(Ssc, lambda: Ssc[:, 2944:3200].bitcast(BF16).rearrange("p (a b) -> p a b", a=4))
            tk = sb("tk", [128, 64], F32)
            dmy = sb("dmy", [128, 8], F32)
            PT0 = sb("PT0", [128, 8, 128], BF16)
            PT = [PT0, PT0]
            Y16 = xb
            yT = xT
            rec8 = sb("rec8", [128, 8], F32)
            gst = sb("gst", [128, 16], F32)
            rr = TV(PROJ, 0, D)
            h1 = rr
            lnw = sb("lnw", [128, 32], F32)

            for i in range(NT):
                nk = (i + 1) * 128
                mk.dma("sp", lambda i=i: SP.dma_start(out=xs[:, :], in_=x[i * 128:(i + 1) * 128, :]), xs, writes=[xs])
                wq_load(0); wq_load(1)
                mk.op("act", lambda: A.copy(out=xb[:, :], in_=xs[:, :]), reads=[xs], writes=[xb])
                transpose8(xb, lambda bk: mk.op("dve", lambda: V.tensor_copy(out=xT[:, :, :], in_=bb(0).rearrange("p (a b) -> p a b", a=8)), reads=[bk], writes=[xT]), 0)
                if STOP < 1:
                    continue
                c0 = 0
                ci = 0
                while c0 < INW:
                    cw = min(512, INW - c0)
                    bi = 1 + (ci % 2)
                    for k in range(8):
                        mk.op("pe", lambda k=k, c0=c0, cw=cw, bi=bi: PE.matmul(bf(bi)[:, 0:cw], lhsT=xT[:, k, :], rhs=winb[:, k, c0:c0 + cw], start=(k == 0), stop=(k == 7)), reads=[xT, winb], writes=[banks[bi]])
                    if ci % 2 == 0:
                        mk.op("act", lambda c0=c0, cw=cw, bi=bi: A.copy(out=PROJ[:, c0:c0 + cw], in_=bf(bi)[:, 0:cw]), reads=[banks[bi]], writes=[PROJ])
                    else:
                        mk.op("dve", lambda c0=c0, cw=cw, bi=bi: V.tensor_copy(out=PROJ[:, c0:c0 + cw], in_=bf(bi)[:, 0:cw]), reads=[banks[bi]], writes=[PROJ])
                    c0 += cw
                    ci += 1
                for j in range(8):
                    col = (1832 if j < 4 else 2088) + (j % 4) * 64
                    bi = 3 + (j // 4)
                    for k in range(8):
                        mk.op("pe", lambda k=k, col=col, bi=bi, j=j: PE.matmul(bf(bi)[0:64, (j % 4) * 128:(j % 4 + 1) * 128], lhsT=winb[:, k, col:col + 64], rhs=xT[:, k, :], start=(k == 0), stop=(k == 7)), reads=[xT, winb], writes=[banks[bi]])
                mk.op("act", lambda: A.copy(out=GQK[:, 0:4, :], in_=bf(3)[0:64, :].rearrange("p (a b) -> p a b", a=4)), reads=[banks[3]], writes=[GQK])
                mk.op("act", lambda: A.copy(out=GQK[:, 4:8, :], in_=bf(4)[0:64, :].rearrange("p (a b) -> p a b", a=4)), reads=[banks[4]], writes=[GQK])

                if STOP < 2:
                    continue
                def rot(base, nh, hd, half, f0, i=i):
                    v3 = PROJ[:, base:base + nh * hd].rearrange("p (h d) -> p h d", h=nh)
                    x1 = v3[:, :, 0:half]; x2 = v3[:, :, half:2 * half]
                    sn = SC[:, i, f0:f0 + half].unsqueeze(1).to_broadcast([128, nh, half])
                    cs = SC[:, i, 12 + f0:12 + f0 + half].unsqueeze(1).to_broadcast([128, nh, half])
                    a1 = t1[:, 0:nh, 0:half]; a2 = t2[:, 0:nh, 0:half]; a3 = t3[:, 0:nh, 0:half]; a4 = t4[:, 0:nh, 0:half]
                    mk.op("dve", lambda: V.tensor_tensor(out=a1, in0=x1, in1=cs, op=ALU.mult), reads=[PROJ, SC], writes=[t1])
                    mk.op("dve", lambda: V.tensor_tensor(out=a2, in0=x2, in1=sn, op=ALU.mult), reads=[PROJ, SC], writes=[t2])
                    mk.op("dve", lambda: V.tensor_tensor(out=a3, in0=x2, in1=cs, op=ALU.mult), reads=[PROJ, SC], writes=[t3])
                    mk.op("dve", lambda: V.tensor_tensor(out=a4, in0=x1, in1=sn, op=ALU.mult), reads=[PROJ, SC], writes=[t4])
                    mk.op("dve", lambda: V.tensor_tensor(out=x1, in0=a1, in1=a2, op=ALU.subtract), reads=[t1, t2], writes=[PROJ])
                    mk.op("dve", lambda: V.tensor_tensor(out=x2, in0=a3, in1=a4, op=ALU.add), reads=[t3, t4], writes=[PROJ])
                rot(0, 16, 64, 8, 0)
                rot(1536, 9, 32, 4, 8)

                if i == 0:
                    print("OPS before stage3", len(mk.ops))
                if STOP < 3:
                    continue
                mk.op("act", lambda: A.copy(out=QK16[:, :], in_=PROJ[:, 0:1024]), reads=[PROJ], writes=[QK16])
                def ev_qk(bk, i=i):
                    mk.op("dve", lambda: V.tensor_copy(out=qT[:, :, :], in_=bb(0)[:, 0:512].rearrange("p (a b) -> p a b", a=4)), reads=[bk], writes=[qT])
                    mk.op("dve", lambda: V.tensor_copy(out=kTc[:, :, i * 128:(i + 1) * 128], in_=bb(0)[:, 512:1024].rearrange("p (a b) -> p a b", a=4)), reads=[bk], writes=[kTc])
                transpose8(QK16, ev_qk, 0)
                mk.op("act", lambda i=i: A.copy(out=Vc[:, i, :, 0:64], in_=PROJ[:, 1024:1536].rearrange("p (h d) -> p h d", h=8)), reads=[PROJ], writes=[Vc])
                mk.op("dve", lambda: V.tensor_copy(out=IDX16[:, 0:256], in_=PROJ[:, 1536:1792]), reads=[PROJ], writes=[IDX16])
                mk.op("dve", lambda: V.tensor_copy(out=IDX16[:, 256:384].rearrange("p (r d) -> p r d", r=4), in_=PROJ[:, 1792:1824].unsqueeze(1).to_broadcast([128, 4, 32])), reads=[PROJ], writes=[IDX16])
                def ev_idx(bk, i=i):
                    mk.op("dve", lambda: V.tensor_copy(out=qidxT[:, :, :], in_=bb(0)[:, 0:256].rearrange("p (a b) -> p a b", a=2)), reads=[bk], writes=[qidxT])
                    mk.op("dve", lambda: V.tensor_copy(out=kidxT[:, i * 128:(i + 1) * 128], in_=bb(0)[:, 256:384]), reads=[bk], writes=[kidxT])
                transpose8(IDX16, ev_idx, 0, nblk=3)
                mk.op("dve", lambda: V.tensor_scalar(out=wst[:, 8:16], in0=PROJ[:, 1824:1832], scalar1=0.0, scalar2=2.0, op0=ALU.is_gt, op1=ALU.mult), reads=[PROJ], writes=[wst])
                mk.op("dve", lambda: V.tensor_scalar_add(out=wst[:, 8:16], in0=wst[:, 8:16], scalar1=-1.0), reads=[wst], writes=[wst])
                mk.op("dve", lambda: V.tensor_tensor(out=wst[:, 0:8], in0=PROJ[:, 1824:1832], in1=wst[:, 8:16], op=ALU.mult), reads=[PROJ, wst], writes=[wst])

                if i == 0:
                    print("OPS before stage4", len(mk.ops))
                if STOP < 4:
                    continue
                nblk = (nk + 511) // 512
                for b in range(nblk):
                    k0 = b * 512
                    kw = min(512, nk - k0)
                    for h in range(8):
                        bi = 1 + (h % 2)
                        rt = rtmp[h % 2]
                        pb = 32 * (h % 4)
                        mk.op("pe", lambda h=h, bi=bi, pb=pb, k0=k0, kw=kw: PE.matmul(bf(bi)[:, 0:kw], lhsT=qidxT[pb:pb + 32, h // 4, :], rhs=kidxT[pb:pb + 32, k0:k0 + kw], start=True, stop=True, tile_position=(pb, 0)), reads=[qidxT, kidxT], writes=[banks[bi]])
                        mk.op("act", lambda h=h, bi=bi, kw=kw, rt=rt: A.activation(out=rt[:, 0:kw], in_=bf(bi)[:, 0:kw], func=AF.Relu, scale=wst[:, h:h + 1]), reads=[banks[bi], wst], writes=[rt])
                        if h == 0:
                            mk.op("dve", lambda h=h, k0=k0, kw=kw, rt=rt: V.tensor_scalar(out=Ssc[:, k0:k0 + kw], in0=rt[:, 0:kw], scalar1=wst[:, 8 + h:9 + h], scalar2=None, op0=ALU.mult), reads=[rt, wst], writes=[Ssc])
                        else:
                            mk.op("dve", lambda h=h, k0=k0, kw=kw, rt=rt: V.scalar_tensor_tensor(out=Ssc[:, k0:k0 + kw], in0=rt[:, 0:kw], scalar=wst[:, 8 + h:9 + h], in1=Ssc[:, k0:k0 + kw], op0=ALU.mult, op1=ALU.add), reads=[rt, wst, Ssc], writes=[Ssc])
                if STOP < 5:
                    continue
                use_topk = nk > n_sel
                if use_topk:
                    mk.op("dve", lambda nk=nk: V.tensor_reduce(out=tk[:, 0:1], in_=Ssc[:, 0:nk], axis=AX.X, op=ALU.max), reads=[Ssc], writes=[tk])
                    mk.op("dve", lambda nk=nk: V.tensor_reduce(out=tk[:, 1:2], in_=Ssc[:, 0:nk], axis=AX.X, op=ALU.min), reads=[Ssc], writes=[tk])
                mk.op("dve", lambda i=i: V.tensor_tensor(out=Ssc[:, i * 128:(i + 1) * 128], in0=Ssc[:, i * 128:(i + 1) * 128], in1=CB[:, :], op=ALU.add), reads=[Ssc, CB], writes=[Ssc])
                if use_topk:
                    mk.op("dve", lambda: V.tensor_tensor(out=tk[:, 2:3], in0=tk[:, 0:1], in1=tk[:, 1:2], op=ALU.subtract), reads=[tk], writes=[tk])
                    mk.op("dve", lambda: V.tensor_scalar(out=tk[:, 8:24], in0=POW2[:, :], scalar1=tk[:, 2:3], scalar2=None, op0=ALU.mult), reads=[tk, POW2], writes=[tk])
                    mk.op("dve", lambda: V.tensor_tensor(out=tk[:, 3:4], in0=tk[:, 1:2], in1=tk[:, 8:9], op=ALU.add), reads=[tk], writes=[tk])
                    for it in range(16):
                        mk.op("dve", lambda nk=nk: V.tensor_scalar(out=MB[:, 0:nk], in0=Ssc[:, 0:nk], scalar1=tk[:, 3:4], scalar2=None, op0=ALU.is_ge, op1=ALU.add, accum_out=tk[:, 4:5]), reads=[Ssc, tk], writes=[MB, tk])
                        mk.op("dve", lambda it=it: V.scalar_tensor_tensor(out=tk[:, 5:6], in0=tk[:, 4:5], scalar=float(n_sel), in1=tk[:, 8 + it:9 + it], op0=ALU.is_ge, op1=ALU.mult), reads=[tk], writes=[tk])
                        if it < 15:
                            mk.op("dve", lambda it=it: V.scalar_tensor_tensor(out=tk[:, 3:4], in0=tk[:, 5:6], scalar=tk[:, 9 + it:10 + it], in1=tk[:, 3:4], op0=ALU.subtract, op1=ALU.add), reads=[tk], writes=[tk])
                        else:
                            mk.op("dve", lambda: V.scalar_tensor_tensor(out=tk[:, 1:2], in0=tk[:, 5:6], scalar=tk[:, 23:24], in1=tk[:, 3:4], op0=ALU.subtract, op1=ALU.add), reads=[tk], writes=[tk])
                    mk.op("dve", lambda nk=nk: V.tensor_scalar(out=MB[:, 0:nk], in0=Ssc[:, 0:nk], scalar1=tk[:, 1:2], scalar2=None, op0=ALU.is_lt), reads=[Ssc, tk], writes=[MB])
                else:
                    mk.op("dve", lambda nk=nk: V.tensor_scalar(out=MB[:, 0:nk], in0=Ssc[:, 0:nk], scalar1=-1.0e29, scalar2=None, op0=ALU.is_lt), reads=[Ssc], writes=[MB])

                if STOP < 6:
                    continue
                for j in range(i + 1):
                    lb = 2 + 2 * (j % 2)
                    pt = PT[j % 2]
                    for h in range(8) if os.environ.get("MK_MERGE", "1") == "0" else []:
                        bi = lb + h // 4
                        pb = 64 * (h % 2)
                        o_ap = (lambda bi=bi, h=h: bf(bi)[:, (h % 4) * 128:(h % 4 + 1) * 128])
                        mk.op("pe", lambda h=h, pb=pb, j=j, o_ap=o_ap: PE.matmul(o_ap(), lhsT=kTc[pb:pb + 64, h // 2, j * 128:(j + 1) * 128], rhs=qT[pb:pb + 64, h // 2, :], start=True, stop=False, tile_position=(pb, 0)), reads=[kTc, qT], writes=[banks[bi]])
                        mk.op("pe", lambda j=j, o_ap=o_ap: PE.matmul(o_ap(), lhsT=MB[:, j * 128:(j + 1) * 128], rhs=negI[:, 0, :], start=False, stop=True), reads=[MB, negI], writes=[banks[bi]])
                    if os.environ.get("MK_MERGE", "1") == "1":
                        for r_ in range(4):
                            for par in range(2):
                                h = 2 * r_ + par
                                pb = 64 * par
                                bi = lb + par
                                mk.op("pe", lambda h=h, pb=pb, j=j, bi=bi, r_=r_: PE.matmul(bf(bi)[:, r_ * 128:(r_ + 1) * 128], lhsT=kTc[pb:pb + 64, h // 2, j * 128:(j + 1) * 128], rhs=qT[pb:pb + 64, h // 2, :], start=(r_ == 0), stop=False, tile_position=(pb, 0), skip_group_check=True), reads=[kTc, qT], writes=[banks[bi]])
                        for par in range(2):
                            bi = lb + par
                            mk.op("pe", lambda j=j, bi=bi: PE.matmul(bf(bi)[:, :], lhsT=MB[:, j * 128:(j + 1) * 128], rhs=negI[:, :, :].rearrange("p a b -> p (a b)"), start=False, stop=True, skip_group_check=True), reads=[MB, negI], writes=[banks[bi]])
                        ptv = (lambda pt=pt: pt[:, :, :].rearrange("p (a two) b -> p a two b", two=2))
                        mk.op("act", lambda lb=lb, ptv=ptv: A.activation(out=ptv()[:, :, 0, :], in_=bf(lb).rearrange("p (a b) -> p a b", a=4), func=AF.Exp, scale=0.125), reads=[banks[lb]], writes=[pt])
                        mk.op("act", lambda lb=lb, ptv=ptv: A.activation(out=ptv()[:, :, 1, :], in_=bf(lb + 1).rearrange("p (a b) -> p a b", a=4), func=AF.Exp, scale=0.125), reads=[banks[lb + 1]], writes=[pt])
                    if os.environ.get("MK_MERGE", "1") == "0":
                        mk.op("act", lambda lb=lb, pt=pt: A.activation(out=pt[:, 0:4, :], in_=bf(lb).rearrange("p (a b) -> p a b", a=4), func=AF.Exp, scale=0.125), reads=[banks[lb]], writes=[pt])
                        mk.op("act", lambda lb=lb, pt=pt: A.activation(out=pt[:, 4:8, :], in_=bf(lb + 1).rearrange("p (a b) -> p a b", a=4), func=AF.Exp, scale=0.125), reads=[banks[lb + 1]], writes=[pt])
                    for h in range(8):
                        bi = 6 + h // 4
                        mk.op("pe", lambda h=h, bi=bi, j=j, pt=pt: PE.matmul(bf(bi)[:, (h % 4) * 65:(h % 4) * 65 + 65], lhsT=pt[:, h, :], rhs=Vc[:, j, h, 0:65], start=(j == 0 and h % 4 == 0), stop=(j == i), skip_group_check=True), reads=[pt, Vc], writes=[banks[bi]])
                for hb in range(2):
                    ov = (lambda hb=hb: bf(6 + hb)[:, 0:260].rearrange("p (h e) -> p h e", h=4))
                    mk.op("dve", lambda hb=hb, ov=ov: V.reciprocal(out=rec8[:, hb * 4:hb * 4 + 4], in_=ov()[:, :, 64]), reads=[banks[6 + hb]], writes=[rec8])
                    mk.op("dve", lambda hb=hb, ov=ov: V.tensor_tensor(out=Y16[:, hb * 256:(hb + 1) * 256].rearrange("p (h e) -> p h e", h=4), in0=ov()[:, :, 0:64], in1=rec8[:, hb * 4:hb * 4 + 4].unsqueeze(2).to_broadcast([128, 4, 64]), op=ALU.mult), reads=[banks[6 + hb], rec8], writes=[Y16])

                if STOP < 7:
                    continue
                mk.op("pe", lambda: PE.transpose(out=bf(1)[0:16, 0:128], in_=PROJ[:, 2856:2872], identity=identf[:, :]), reads=[PROJ, identf], writes=[banks[1]])
                mk.op("act", lambda: A.copy(out=glrT[0:16, :], in_=bf(1)[0:16, 0:128]), reads=[banks[1]], writes=[glrT])
                mk.op("pe", lambda: PE.matmul(bf(2)[:, 0:256], lhsT=glrT[:, :], rhs=GU[:, :], start=True, stop=True), reads=[glrT, GU], writes=[banks[2]])
                mk.op("act", lambda: A.activation(out=Lg[:, :], in_=bf(2)[:, 0:256], func=AF.Exp, scale=-1.0), reads=[banks[2]], writes=[Lg])
                mk.op("act", lambda: A.activation(out=Lg[:, :], in_=Lg[:, :], func=AF.Ln, bias=1.0), reads=[Lg], writes=[Lg])
                for h in range(4):
                    mk.op("pe", lambda h=h: PE.matmul(bf(3)[0:64, h * 128:(h + 1) * 128], lhsT=Lg[:, h * 64:(h + 1) * 64], rhs=TRI[:, :], start=True, stop=True), reads=[Lg, TRI], writes=[banks[3]])
                mk.op("pe", lambda: PE.matmul(bf(4)[:, 0:256], lhsT=TRU[:, :], rhs=Lg[:, :], start=True, stop=True), reads=[Lg, TRU], writes=[banks[4]])
                b3v = (lambda: bf(3)[0:64, :].rearrange("p (a b) -> p a b", a=4))
                mk.op("act", lambda: A.activation(out=E1[:, :, :], in_=b3v(), func=AF.Exp), reads=[banks[3]], writes=[E1])
                mk.op("act", lambda: A.activation(out=E2[:, :, :], in_=b3v(), func=AF.Exp, scale=-1.0), reads=[banks[3]], writes=[E2])
                mk.op("act", lambda: A.activation(out=Ee[:, :], in_=bf(4)[:, 0:256], func=AF.Exp), reads=[banks[4]], writes=[Ee])
                mk.op("dve", lambda: V.scalar_tensor_tensor(out=qdT[:, :, :], in0=GQK[:, 0:4, :], scalar=0.125, in1=E1[:, :, :], op0=ALU.mult, op1=ALU.mult), reads=[GQK, E1], writes=[qdT])
                mk.op("dve", lambda: V.tensor_tensor(out=kiT[:, :, :], in0=GQK[:, 4:8, :], in1=E2[:, :, :], op=ALU.mult), reads=[GQK, E2], writes=[kiT])
                mk.op("dve", lambda: V.tensor_tensor(out=kte[:, :], in0=PROJ[:, 2088:2344], in1=Ee[:, :], op=ALU.mult), reads=[PROJ, Ee], writes=[kte])
                mk.op("act", lambda: A.copy(out=gvb[:, :], in_=PROJ[:, 2344:2856]), reads=[PROJ], writes=[gvb])
                for h in range(4):
                    mk.op("pe", lambda h=h: PE.matmul(bf(5)[:, h * 128:(h + 1) * 128], lhsT=kiT[:, h, :], rhs=qdT[:, h, :], start=True, stop=True), reads=[kiT, qdT], writes=[banks[5]])
                mk.op("dve", lambda: V.tensor_tensor(out=ATb[:, :, :], in0=bf(5).rearrange("p (a b) -> p a b", a=4), in1=MT[:, :].unsqueeze(1).to_broadcast([128, 4, 128]), op=ALU.mult), reads=[banks[5], MT], writes=[ATb])
                for h in range(4):
                    mk.op("pe", lambda h=h: PE.matmul(bf(1)[:, h * 128:(h + 1) * 128], lhsT=ATb[:, h, :], rhs=gvb[:, h * 128:(h + 1) * 128], start=True, stop=False), reads=[ATb, gvb], writes=[banks[1]])
                    mk.op("pe", lambda h=h: PE.matmul(bf(1)[:, h * 128:(h + 1) * 128], lhsT=qdT[:, h, :], rhs=Sb[:, h, :], start=False, stop=True), reads=[qdT, Sb], writes=[banks[1]])
                for h in range(4):
                    mk.op("pe", lambda h=h: PE.matmul(bf(2)[0:64, h * 128:(h + 1) * 128], lhsT=kte[:, h * 64:(h + 1) * 64], rhs=gvb[:, h * 128:(h + 1) * 128], start=True, stop=True), reads=[kte, gvb], writes=[banks[2]])
                for h in range(4):
                    mk.op("dve", lambda h=h: V.scalar_tensor_tensor(out=Sst[:, h, :], in0=Sst[:, h, :], scalar=E1[:, h, 127:128], in1=bf(2)[0:64, h * 128:(h + 1) * 128], op0=ALU.mult, op1=ALU.add), reads=[Sst, E1, banks[2]], writes=[Sst])
                mk.op("dve", lambda: V.tensor_copy(out=Sb[:, :, :], in_=Sst[:, :, :]), reads=[Sst], writes=[Sb])
                mk.op("act", lambda: A.copy(out=og[:, :, :], in_=bf(1).rearrange("p (a b) -> p a b", a=4)), reads=[banks[1]], writes=[og])
                mk.op("dve", lambda: V.tensor_tensor(out=sq[:, :], in0=og[:, :, :].rearrange("p a b -> p (a b)"), in1=og[:, :, :].rearrange("p a b -> p (a b)"), op=ALU.mult), reads=[og], writes=[sq])
                mk.op("dve", lambda: V.tensor_reduce(out=gst[:, 0:4], in_=sq[:, :].rearrange("p (a b) -> p a b", a=4), axis=AX.X, op=ALU.add), reads=[sq], writes=[gst])
                mk.op("dve", lambda: V.tensor_scalar(out=gst[:, 4:8], in0=gst[:, 0:4], scalar1=1.0 / 128, scalar2=1e-6, op0=ALU.mult, op1=ALU.add), reads=[gst], writes=[gst])
                mk.op("act", lambda: A.sqrt(out=gst[:, 8:12], in_=gst[:, 4:8]), reads=[gst], writes=[gst])
                mk.op("dve", lambda: V.reciprocal(out=gst[:, 12:16], in_=gst[:, 8:12]), reads=[gst], writes=[gst])
                mk.op("dve", lambda: V.tensor_tensor(out=og[:, :, :], in0=og[:, :, :], in1=gst[:, 12:16].unsqueeze(2).to_broadcast([128, 4, 128]), op=ALU.mult), reads=[og, gst], writes=[og])
                mk.op("dve", lambda: V.tensor_tensor(out=og[:, :, :], in0=og[:, :, :], in1=NG[:, :].unsqueeze(1).to_broadcast([128, 4, 128]), op=ALU.mult), reads=[og, NG], writes=[og])
                mk.op("act", lambda: A.activation(out=sil[:, :], in_=PROJ[:, 2872:3384], func=AF.Silu), reads=[PROJ], writes=[sil])
                mk.op("dve", lambda: V.tensor_tensor(out=Y16[:, 512:1024], in0=og[:, :, :].rearrange("p a b -> p (a b)"), in1=sil[:, :], op=ALU.mult), reads=[og, sil], writes=[Y16])

                if STOP < 8:
                    continue
                transpose8(Y16, lambda bk: mk.op("dve", lambda: V.tensor_copy(out=yT[:, :, :], in_=bb(0).rearrange("p (a b) -> p a b", a=8)), reads=[bk], writes=[yT]), 0)
                for hf in range(2):
                    bi = 1 + hf
                    for qd in range(2):
                        q4 = hf * 2 + qd
                        wt = WQ[q4 % 2]
                        for k in range(8):
                            mk.op("pe", lambda k=k, qd=qd, bi=bi, wt=wt: PE.matmul(bf(bi)[:, qd * 256:(qd + 1) * 256], lhsT=yT[:, k, :], rhs=wt[:, k, :], start=(k == 0), stop=(k == 7)), reads=[yT, wt], writes=[banks[bi]])
                        if q4 + 2 < 4:
                            wq_load(q4 + 2)
                    mk.op("dve", lambda hf=hf, bi=bi: V.scalar_tensor_tensor(out=rr[:, hf * 512:(hf + 1) * 512], in0=xs[:, hf * 512:(hf + 1) * 512], scalar=ALPHA, in1=bf(bi)[:, :], op0=ALU.mult, op1=ALU.add), reads=[xs, banks[bi]], writes=[rr])
                layer_norm(rr, g1, b1, h1, lnw)
                conv_issue((32 + NT - 1) // NT)
                if os.environ.get("MK_DBG") == "M":
                    mk.op("dve", lambda: V.tensor_copy(out=h1[:, 0:24], in_=tk[:, 0:24]), reads=[tk], writes=[h1])
                    mk.op("dve", lambda nk=nk: V.tensor_reduce(out=h1[:, 24:25], in_=MB[:, 0:nk], axis=AX.X, op=ALU.add), reads=[MB], writes=[h1])
                    mk.op("dve", lambda nk=nk: V.tensor_copy(out=h1[:, 32:32 + nk], in_=Ssc[:, 0:nk]), reads=[Ssc], writes=[h1])
                if os.environ.get("MK_DBG") == "Y":
                    mk.op("dve", lambda: V.tensor_copy(out=h1[:, :], in_=Y16[:, :]), reads=[Y16], writes=[h1])
                mk.dma("sp", lambda i=i: SP.dma_start(out=h1d[i * 128:(i + 1) * 128, :], in_=h1[:, :]), h1, reads=[h1], writes=[H1D])
            mk.barrier()
            mk.flush()


        if "B" in phases:
          with contextlib.ExitStack() as sbk:
            def sb(name, shape, dt):
                return mk.sb(name, shape, dt, st=sbk)
            conv_issue(64)
            wqb = sb("wqb", [128, 8, D], BF16); wob = sb("wob", [128, 8, D], BF16)
            wpqb = sb("wpqb", [128, 8, 2048], BF16)
            KmT = sb("KmT", [128, 8, 256], BF16); Vm = sb("Vm", [128, 2, D], BF16)
            skT = sb("skT", [128, 2, 128], BF16)
            g2, b2 = ln_consts(ln2g, ln2b, sbk, "ln2")
            g3, b3 = ln_consts(ln3g, ln3b, sbk, "ln3")
            ones = sb("ones", [128, 128], BF16)
            mk.op("pool", lambda: G.memset(ones[:, :], 1.0), writes=[ones])
            io_i = sb("io_i", [128, 16], I32); IO16 = sb("IO16", [128, 16], F32)
            mk.op("pool", lambda: G.iota(out=io_i[:, :], pattern=[[1, 16]], base=0, channel_multiplier=0), writes=[io_i])
            mk.op("dve", lambda: V.tensor_copy(out=IO16[:, :], in_=io_i[:, :]), reads=[io_i], writes=[IO16])
            dm2 = sb("dm2", [128, 8], F32)
            pro = contextlib.ExitStack()
            def sbp(name, shape, dt):
                return mk.sb(name, shape, dt, st=pro)
            wkb = sbp("wkb", [128, 8, D], BF16); wvb = sbp("wvb", [128, 8, D], BF16)
            for (wt, wd, wn) in ((wkb, wk, D), (wvb, wv, D), (wqb, wq, D), (wob, wo, D), (wpqb, wpq, 2048)):
                for k in range(8):
                    mk.dma("pool", lambda k=k, wt=wt, wd=wd: G.dma_start(out=wt[:, k, :], in_=wd[k * 128:(k + 1) * 128, :], max_dma_last_dim=4096), wt, writes=[wt])
            ms = sbp("ms", [128, 2, D], F32); mb = sbp("mb", [128, 2 * D], BF16)
            memT = sbp("memT", [128, 8, 256], BF16)
            mk.dma("sp", lambda: SP.dma_start(out=ms[:, :, :], in_=mem.rearrange("(a p) d -> p a d", p=128)), ms, writes=[ms])
            mk.op("dve", lambda: V.tensor_copy(out=mb[:, :], in_=ms[:, :, :].rearrange("p a d -> p (a d)")), reads=[ms], writes=[mb])
            for a in range(2):
                def ev_m(bk, a=a):
                    mk.op("dve", lambda: V.tensor_copy(out=memT[:, :, a * 128:(a + 1) * 128], in_=bb(0).rearrange("p (k b) -> p k b", k=8)), reads=[bk], writes=[memT])
                for k in range(8):
                    mk.op("pe", lambda k=k, a=a: PE.transpose(out=bb(0)[:, k * 128:(k + 1) * 128], in_=mb[:, a * D + k * 128:a * D + (k + 1) * 128], identity=ident[:, :]), reads=[mb, ident], writes=[banks[0]])
                ev_m(banks[0])
            for c in range(8):
                bi = 1 + c % 2
                for k in range(8):
                    mk.op("pe", lambda k=k, c=c, bi=bi: PE.matmul(bf(bi)[:, 0:256], lhsT=wkb[:, k, c * 128:(c + 1) * 128], rhs=memT[:, k, :], start=(k == 0), stop=(k == 7)), reads=[wkb, memT], writes=[banks[bi]])
                mk.op("dve", lambda c=c, bi=bi: V.tensor_copy(out=KmT[:, c, :], in_=bf(bi)[:, 0:256]), reads=[banks[bi]], writes=[KmT])
            for mt in range(2):
                for hf in range(2):
                    bi = 3 + hf
                    for k in range(8):
                        mk.op("pe", lambda k=k, mt=mt, hf=hf, bi=bi: PE.matmul(bf(bi)[:, :], lhsT=memT[:, k, mt * 128:(mt + 1) * 128], rhs=wvb[:, k, hf * 512:(hf + 1) * 512], start=(k == 0), stop=(k == 7)), reads=[wvb, memT], writes=[banks[bi]])
                    mk.op("dve", lambda mt=mt, hf=hf, bi=bi: V.tensor_copy(out=Vm[:, mt, hf * 512:(hf + 1) * 512], in_=bf(bi)[:, :]), reads=[banks[bi]], writes=[Vm])
            sks = sbp("sks", [128, 256], F32); skb = sbp("skb", [128, 256], BF16)
            mk.dma("sp", lambda: SP.dma_start(out=sks[:, 0:128], in_=sk1[:, :]), sks, writes=[sks])
            mk.dma("sp", lambda: SP.dma_start(out=sks[:, 128:256], in_=sk2[:, :]), sks, writes=[sks])
            mk.op("dve", lambda: V.tensor_copy(out=skb[:, :], in_=sks[:, :]), reads=[sks], writes=[skb])
            for a in range(2):
                mk.op("pe", lambda a=a: PE.transpose(out=bb(0)[:, a * 128:(a + 1) * 128], in_=skb[:, a * 128:(a + 1) * 128], identity=ident[:, :]), reads=[skb, ident], writes=[banks[0]])
            mk.op("dve", lambda: V.tensor_copy(out=skT[:, :, :], in_=bb(0)[:, 0:256].rearrange("p (a b) -> p a b", a=2)), reads=[banks[0]], writes=[skT])

            mk.barrier()
            mk.flush()
            pro.close()
            h1s = sb("h1s", [128, D], F32); hb = sb("hb", [128, D], BF16); hT = sb("hT", [128, 8, 128], BF16)
            qTb = sb("qTb", [128, 8, 128], BF16)
            PTx = sb("PTx", [128, 2, 4, 128], BF16)
            recs = sb("recs", [128, 4, 128], F32)
            oTn = sb("oTn", [128, 8, 128], BF16)
            rr2 = sb("rr2", [128, D], F32); H2L = [sb("h2_%d" % q, [128, D], F32) for q in range(2)]; lnw2 = sb("lnw2", [128, 32], F32)
            pqT = sb("pqT", [128, 16, 128], BF16)
            scr8 = sb("scr8", [128, 2048], F32)
            SCs = sb("SCs", [128, 16, 128], F32); SC2 = AV(scr8, lambda: scr8[:, :].rearrange("p (a b) -> p a b", a=16))
            V8 = sb("V8", [128, 16, 16], F32); I8 = sb("I8", [128, 16, 16], U32); I8f = sb("I8f", [128, 16, 16], F32)
            cand = AV(SCs, lambda: SCs[:, :, :].rearrange("p (h s) n -> p h (s n)", s=2)); cand2 = AV(scr8, lambda: scr8[:, :].rearrange("p (a b) -> p a b", a=8))
            T16 = sb("T16", [128, 8, 16], F32); P16 = sb("P16", [128, 8, 16], U32)
            Pa = sb("Pa", [128, 8, 16], U32); Pb = sb("Pb", [128, 8, 16], U32)
            Af = sb("Af", [128, 8, 16], F32); Bf = sb("Bf", [128, 8, 16], F32)
            OH = AV(scr8, lambda: scr8[:, :].rearrange("p (h r a) -> p h r a", h=8, r=16))
            i1s = sb("i1s", [128, 128], F32); i2s = sb("i2s", [128, 128], F32)
            EIDL = [sb("EID%d" % q, [128, 128], U32) for q in range(2)]
            GTL = [sb("GT%d" % q, [128, 8, 16], F32) for q in range(2)]; gz = sb("gz", [128, 16], F32)
            dots = sb("dots", [128, 128], F32); coef = sb("coef", [128, 128], F32)
            NB = 4
            GS = 4; NGB = 3
            UG = [sb("UG%d" % q, [128, 2 * D], BF16) for q in range(GS * NGB)]
            DG = [sb("DG%d" % q, [128, GS, 128], BF16) for q in range(NGB)]
            junk = sb("junk", [128, D], BF16)
            gl = sb("gl", [128, 128], F32)
            rr3 = rr2; yo = h1s

            def top16(src, srcT, scratch, scrT, vout, iout, tl):
                vt, it_ = tl
                mk.op("dve", lambda: V.max(out=vout[:, 0:8], in_=src), reads=[srcT], writes=[vt])
                mk.op("dve", lambda: V.max_index(out=iout[:, 0:8], in_max=vout[:, 0:8], in_values=src), reads=[srcT, vt], writes=[it_])
                mk.op("dve", lambda: V.match_replace(out=scratch, in_to_replace=vout[:, 0:8], in_values=src, imm_value=-1.0e30), reads=[srcT, vt], writes=[scrT])
                mk.op("dve", lambda: V.max(out=vout[:, 8:16], in_=scratch), reads=[scrT], writes=[vt])
                mk.op("dve", lambda: V.max_index(out=iout[:, 8:16], in_max=vout[:, 8:16], in_values=scratch), reads=[scrT, vt], writes=[it_])

            def stage_P(i):
                h2 = H2L[i % 2]; EID = EIDL[i % 2]; GT = GTL[i % 2]
                mk.dma("sp", lambda i=i: SP.dma_start(out=h1s[:, :], in_=h1d[i * 128:(i + 1) * 128, :]), h1s, reads=[H1D], writes=[h1s])
                mk.op("act", lambda: A.copy(out=hb[:, :], in_=h1s[:, :]), reads=[h1s], writes=[hb])
                transpose8(hb, lambda bk: mk.op("dve", lambda: V.tensor_copy(out=hT[:, :, :], in_=bb(0).rearrange("p (a b) -> p a b", a=8)), reads=[bk], writes=[hT]), 0)
                for c in range(8):
                    bi = 1 + c // 4
                    for k in range(8):
                        mk.op("pe", lambda k=k, c=c, bi=bi: PE.matmul(bf(bi)[:, (c % 4) * 128:(c % 4 + 1) * 128], lhsT=wqb[:, k, c * 128:(c + 1) * 128], rhs=hT[:, k, :], start=(k == 0), stop=(k == 7)), reads=[wqb, hT], writes=[banks[bi]])
                for hf in range(2):
                    mk.op("act", lambda hf=hf: A.copy(out=qTb[:, hf * 4:(hf + 1) * 4, :], in_=bf(1 + hf).rearrange("p (a b) -> p a b", a=4)), reads=[banks[1 + hf]], writes=[qTb])
                for mt in range(2):
                    bi = 3 + mt
                    for h in range(4):
                        for cc in range(2):
                            c = 2 * h + cc
                            mk.op("pe", lambda h=h, c=c, cc=cc, mt=mt, bi=bi: PE.matmul(bf(bi)[:, h * 128:(h + 1) * 128], lhsT=KmT[:, c, mt * 128:(mt + 1) * 128], rhs=qTb[:, c, :], start=(cc == 0), stop=(cc == 1)), reads=[KmT, qTb], writes=[banks[bi]])
                    mk.op("act", lambda mt=mt, bi=bi: A.activation(out=PTx[:, mt, :, :], in_=bf(bi).rearrange("p (a b) -> p a b", a=4), func=AF.Exp, scale=1.0 / 16), reads=[banks[bi]], writes=[PTx])
                for c in range(8):
                    bi = (5, 1)[c // 4]
                    for mt in range(2):
                        mk.op("pe", lambda c=c, mt=mt, bi=bi: PE.matmul(bf(bi)[:, (c % 4) * 128:(c % 4 + 1) * 128], lhsT=Vm[:, mt, c * 128:(c + 1) * 128], rhs=PTx[:, mt, c // 2, :], start=(mt == 0), stop=(mt == 1)), reads=[Vm, PTx], writes=[banks[bi]])
                for h in range(4):
                    for mt in range(2):
                        mk.op("pe", lambda h=h, mt=mt: PE.matmul(bf(2)[:, h * 128:(h + 1) * 128], lhsT=ones[:, :], rhs=PTx[:, mt, h, :], start=(mt == 0), stop=(mt == 1)), reads=[ones, PTx], writes=[banks[2]])
                mk.op("dve", lambda: V.reciprocal(out=recs[:, :, :], in_=bf(2).rearrange("p (a b) -> p a b", a=4)), reads=[banks[2]], writes=[recs])
                for hf in range(2):
                    mk.op("dve", lambda hf=hf: V.tensor_tensor(out=oTn[:, hf * 4:(hf + 1) * 4, :].rearrange("p (h c) t -> p h c t", h=2), in0=bf((5, 1)[hf]).rearrange("p (h c t) -> p h c t", h=2, c=2), in1=recs[:, hf * 2:(hf + 1) * 2, :].unsqueeze(2).to_broadcast([128, 2, 2, 128]), op=ALU.mult), reads=[banks[(5, 1)[hf]], recs], writes=[oTn])
                for hf in range(2):
                    bi = 3 + hf
                    for k in range(8):
                        mk.op("pe", lambda k=k, hf=hf, bi=bi: PE.matmul(bf(bi)[:, :], lhsT=oTn[:, k, :], rhs=wob[:, k, hf * 512:(hf + 1) * 512], start=(k == 0), stop=(k == 7)), reads=[oTn, wob], writes=[banks[bi]])
                    mk.op("dve", lambda hf=hf, bi=bi: V.scalar_tensor_tensor(out=rr2[:, hf * 512:(hf + 1) * 512], in0=h1s[:, hf * 512:(hf + 1) * 512], scalar=ALPHA, in1=bf(bi)[:, :], op0=ALU.mult, op1=ALU.add), reads=[h1s, banks[bi]], writes=[rr2])
                layer_norm(rr2, g2, b2, h2, lnw2)
                if os.environ.get("MK_DBG") == "H2":
                    mk.dma("sp", lambda i=i: SP.dma_start(out=y[i * 128:(i + 1) * 128, :], in_=h2[:, :]), h2, reads=[h2])
                    return
                mk.op("act", lambda: A.copy(out=hb[:, :], in_=h2[:, :]), reads=[h2], writes=[hb])
                transpose8(hb, lambda bk: mk.op("dve", lambda: V.tensor_copy(out=hT[:, :, :], in_=bb(0).rearrange("p (a b) -> p a b", a=8)), reads=[bk], writes=[hT]), 0)
                for c in range(16):
                    bi = 1 + c // 4
                    for k in range(8):
                        mk.op("pe", lambda k=k, c=c, bi=bi: PE.matmul(bf(bi)[:, (c % 4) * 128:(c % 4 + 1) * 128], lhsT=wpqb[:, k, c * 128:(c + 1) * 128], rhs=hT[:, k, :], start=(k == 0), stop=(k == 7)), reads=[wpqb, hT], writes=[banks[bi]])
                for g4 in range(4):
                    mk.op("act", lambda g4=g4: A.copy(out=pqT[:, g4 * 4:(g4 + 1) * 4, :], in_=bf(1 + g4).rearrange("p (a b) -> p a b", a=4)), reads=[banks[1 + g4]], writes=[pqT])
                for c in range(16):
                    bi = (5, 0, 1, 2)[c // 4]
                    mk.op("pe", lambda c=c, bi=bi: PE.matmul(bf(bi)[:, (c % 4) * 128:(c % 4 + 1) * 128], lhsT=pqT[:, c, :], rhs=skT[:, c % 2, :], start=True, stop=True), reads=[pqT, skT], writes=[banks[bi]])
                    if c % 4 == 3:
                        g4 = c // 4
                        mk.op("dve", lambda g4=g4, bi=bi: V.tensor_copy(out=SCs[:, g4 * 4:(g4 + 1) * 4, :], in_=bf(bi).rearrange("p (a b) -> p a b", a=4)), reads=[banks[bi]], writes=[SCs])
                for c in range(16):
                    top16(SCs[:, c, :], SCs, SC2[:, c, :], SC2, V8[:, c, :], I8[:, c, :], (V8, I8))
                V8v = V8[:, :, :].rearrange("p (h s) a -> p h s a", s=2)
                mk.op("dve", lambda: V.tensor_tensor(out=cand[:, :, :].rearrange("p h (a b) -> p h a b", a=16), in0=V8v[:, :, 0, :].unsqueeze(3).to_broadcast([128, 8, 16, 16]), in1=V8v[:, :, 1, :].unsqueeze(2).to_broadcast([128, 8, 16, 16]), op=ALU.add), reads=[V8], writes=[cand])
                for h in range(8):
                    top16(cand[:, h, :], cand, cand2[:, h, :], cand2, T16[:, h, :], P16[:, h, :], (T16, P16))
                mk.op("dve", lambda: V.tensor_single_scalar(out=Pa[:, :, :], in_=P16[:, :, :], scalar=4, op=ALU.logical_shift_right), reads=[P16], writes=[Pa])
                mk.op("dve", lambda: V.tensor_single_scalar(out=Pb[:, :, :], in_=P16[:, :, :], scalar=15, op=ALU.bitwise_and), reads=[P16], writes=[Pb])
                mk.op("dve", lambda: V.tensor_copy(out=Af[:, :, :], in_=Pa[:, :, :]), reads=[Pa], writes=[Af])
                mk.op("dve", lambda: V.tensor_copy(out=Bf[:, :, :], in_=Pb[:, :, :]), reads=[Pb], writes=[Bf])
                mk.op("dve", lambda: V.tensor_copy(out=I8f[:, :, :], in_=I8[:, :, :]), reads=[I8], writes=[I8f])
                I8v = I8f[:, :, :].rearrange("p (h s) a -> p h s a", s=2)
                for (sel, which, dst) in ((Af, 0, i1s), (Bf, 1, i2s)):
                    mk.op("dve", lambda sel=sel: V.tensor_tensor(out=OH[:, :, :, :], in0=sel[:, :, :].unsqueeze(3).to_broadcast([128, 8, 16, 16]), in1=IO16[:, :].unsqueeze(1).unsqueeze(1).to_broadcast([128, 8, 16, 16]), op=ALU.is_equal), reads=[sel, IO16], writes=[OH])
                    mk.op("dve", lambda which=which: V.tensor_tensor(out=OH[:, :, :, :], in0=OH[:, :, :, :], in1=I8v[:, :, which, :].unsqueeze(2).to_broadcast([128, 8, 16, 16]), op=ALU.mult), reads=[OH, I8f], writes=[OH])
                    mk.op("dve", lambda dst=dst: V.tensor_reduce(out=dst[:, :], in_=OH[:, :, :, :].rearrange("p h r a -> p (h r) a"), axis=AX.X, op=ALU.add), reads=[OH], writes=[dst])
                mk.op("dve", lambda: V.scalar_tensor_tensor(out=i1s[:, :], in0=i1s[:, :], scalar=128.0, in1=i2s[:, :], op0=ALU.mult, op1=ALU.add), reads=[i1s, i2s], writes=[i1s])
                mk.op("dve", lambda: V.tensor_copy(out=EID[:, :], in_=i1s[:, :]), reads=[i1s], writes=[EID])
                mk.op("dve", lambda: V.tensor_tensor(out=GT[:, :, :], in0=T16[:, :, :], in1=T16[:, :, 0:1].to_broadcast([128, 8, 16]), op=ALU.subtract), reads=[T16], writes=[GT])
                mk.op("act", lambda: A.activation(out=GT[:, :, :], in_=GT[:, :, :], func=AF.Exp), reads=[GT], writes=[GT])
                mk.op("dve", lambda: V.tensor_reduce(out=gz[:, 0:8], in_=GT[:, :, :], axis=AX.X, op=ALU.add), reads=[GT], writes=[gz])
                mk.op("dve", lambda: V.reciprocal(out=gz[:, 8:16], in_=gz[:, 0:8]), reads=[gz], writes=[gz])
                mk.op("dve", lambda: V.tensor_tensor(out=GT[:, :, :], in0=GT[:, :, :], in1=gz[:, 8:16].unsqueeze(2).to_broadcast([128, 8, 16]), op=ALU.mult), reads=[GT, gz], writes=[GT])

            def stage_G(i, pend_ops):
                per_grp = (len(pend_ops) + 27) // 28
                h2 = H2L[i % 2]; EID = EIDL[i % 2]; GT = GTL[i % 2]
                GTf = GT[:, :, :].rearrange("p h r -> p (h r)")
                for g in range(128 // GS):
                    bufs = [UG[(g % NGB) * GS + j] for j in range(GS)]
                    dg = DG[g % NGB]
                    for j in range(GS):
                        s_ = g * GS + j
                        ug = bufs[j]
                        mk.dma("pool", lambda s_=s_, ug=ug: G.indirect_dma_start(out=ug[:, :], out_offset=None, in_=edu[:, :], in_offset=bass.IndirectOffsetOnAxis(ap=EID[:, s_:s_ + 1], axis=0)), ug, reads=[EID, EDU], writes=[ug])
                        mk.op("dve", lambda s_=s_, ug=ug: V.scalar_tensor_tensor(out=junk[:, :], in0=ug[:, 0:D], scalar=1.0, in1=h2[:, :], op0=ALU.mult, op1=ALU.mult, accum_out=dots[:, s_:s_ + 1]), reads=[ug, h2], writes=[junk, dots])
                    mk.op("act", lambda g=g: A.activation(out=gl[:, g * GS:(g + 1) * GS], in_=dots[:, g * GS:(g + 1) * GS], func=AF.Gelu), reads=[dots], writes=[gl])
                    mk.op("dve", lambda g=g: V.tensor_tensor(out=coef[:, g * GS:(g + 1) * GS], in0=gl[:, g * GS:(g + 1) * GS], in1=GTf[:, g * GS:(g + 1) * GS], op=ALU.mult), reads=[gl, GT], writes=[coef])
                    mk.op("dve", lambda g=g, dg=dg: V.tensor_tensor(out=dg[:, :, :], in0=ident[:, :].unsqueeze(1).to_broadcast([128, GS, 128]), in1=coef[:, g * GS:(g + 1) * GS].unsqueeze(2).to_broadcast([128, GS, 128]), op=ALU.mult), reads=[ident, coef], writes=[dg])
                    for j in range(GS):
                        s_ = g * GS + j
                        ug = bufs[j]
                        for hf in range(2):
                            mk.op("pe", lambda s_=s_, ug=ug, dg=dg, j=j, hf=hf: PE.matmul(bf(6 + hf)[:, :], lhsT=dg[:, j, :], rhs=ug[:, D + hf * 512:D + (hf + 1) * 512], start=(s_ == 0), stop=(s_ == 127)), reads=[dg, ug], writes=[banks[6 + hf]])
                    mk.replay(pend_ops, per_grp)
                mk.replay(pend_ops, len(pend_ops))
                for hf in range(2):
                    mk.op("dve", lambda hf=hf: V.scalar_tensor_tensor(out=rr3[:, hf * 512:(hf + 1) * 512], in0=h2[:, hf * 512:(hf + 1) * 512], scalar=ALPHA, in1=bf(6 + hf)[:, :], op0=ALU.mult, op1=ALU.add), reads=[h2, banks[6 + hf]], writes=[rr3])
                layer_norm(rr3, g3, b3, yo, lnw2)
                mk.dma("sp", lambda i=i: SP.dma_start(out=y[i * 128:(i + 1) * 128, :], in_=yo[:, :]), yo, reads=[yo])

            stage_P(0)
            for i in range(NT):
                pend_ops = []
                if i + 1 < NT:
                    mk.defer = pend_ops
                    stage_P(i + 1)
                    mk.defer = None
                stage_G(i, pend_ops)
            mk.barrier()
            mk.flush()
        if "B" not in phases:
            with contextlib.ExitStack() as sd:
                tmp = mk.sb("dbgt", [128, D], F32, st=sd)
                for i in range(NT):
                    mk.dma("sp", lambda i=i: SP.dma_start(out=tmp[:, :], in_=h1d[i * 128:(i + 1) * 128, :]), tmp, reads=[H1D], writes=[tmp])
                    mk.dma("sp", lambda i=i: SP.dma_start(out=y[i * 128:(i + 1) * 128, :], in_=tmp[:, :]), tmp, reads=[tmp])
                mk.barrier()
                mk.flush()
        nw = mk.flush(final=True)
    return nc


_NC_CACHE = {}


def _in_map(inp, b, S):
    NT = S // 128
    f = lambda a: np.ascontiguousarray(np.asarray(a, dtype=np.float32))
    return {
        "x": f(inp["x"][b]), "pos": np.ascontiguousarray(np.asarray(inp["positions"][b]).astype(np.int32).reshape(NT, 128).T),
        "mem": f(inp["mem"][b]), "w_in": f(inp["w_in"][0]), "gate_up": f(inp["gla_gate_up"][0]),
        "gate_bias": f(inp["gla_gate_bias"][0]).reshape(1, -1), "norm_g": f(inp["gla_norm_g"][0]).reshape(1, -1),
        "w_out": f(inp["w_out"][0]), "ln1g": f(inp["ln_mix_g"][0]).reshape(1, -1), "ln1b": f(inp["ln_mix_b"][0]).reshape(1, -1),
        "wq": f(inp["xattn_w_q"][0]), "wk": f(inp["xattn_w_k"][0]), "wv": f(inp["xattn_w_v"][0]), "wo": f(inp["xattn_w_o"][0]),
        "ln2g": f(inp["ln_mem_g"][0]).reshape(1, -1), "ln2b": f(inp["ln_mem_b"][0]).reshape(1, -1),
        "wpq": f(inp["peer_w_query"][0]), "sk1": f(inp["peer_sub_keys_1"][0]), "sk2": f(inp["peer_sub_keys_2"][0]),
        "edown": f(inp["peer_expert_down"][0]), "eup": f(inp["peer_expert_up"][0]),
        "ln3g": f(inp["ln_ffn_g"][0]).reshape(1, -1), "ln3b": f(inp["ln_ffn_b"][0]).reshape(1, -1),
    }


def run(inp, phases="AB", cores=None):
    B, S, _ = inp["x"].shape
    n_sel = min(256, S // 4)
    key = (S, n_sel, phases)
    if key not in _NC_CACHE:
        _NC_CACHE[key] = build(S, n_sel, phases)
    nc = _NC_CACHE[key]
    cores = list(range(B)) if cores is None else cores
    in_maps = [_in_map(inp, b, S) for b in cores]
    res = run_bass_kernel_spmd(nc, in_maps, core_ids=list(range(len(cores))))
    return np.stack([np.asarray(r["y"], dtype=np.float32) for r in res.results], axis=0)


def kernel(**inputs):
    return run(inputs, "AB")
```

```python
import contextlib
import numpy as np
import concourse.bass as bass
import concourse.mybir as mybir

F32 = mybir.dt.float32
BF16 = mybir.dt.bfloat16
I32 = mybir.dt.int32
U32 = mybir.dt.uint32
U16 = mybir.dt.uint16
ALU = mybir.AluOpType
AF = mybir.ActivationFunctionType
AX = mybir.AxisListType

from concourse.bass_utils import run_bass_kernel_spmd

ENGS = ("pe", "act", "dve", "pool", "sp")


class T:
    __slots__ = ("h", "name", "w", "r", "dl", "dlr", "sem", "dcnt")

    def __init__(self, h, name):
        self.h = h
        self.name = name
        self.w = None
        self.r = {}
        self.dl = {}
        self.dlr = {}
        self.sem = None
        self.dcnt = 0

    def __getitem__(self, k):
        return self.h[k]


class TV:
    def __init__(self, t, c0, c1):
        object.__setattr__(self, "_t", t); object.__setattr__(self, "_c0", c0); object.__setattr__(self, "_c1", c1)

    def __getattr__(self, k):
        return getattr(object.__getattribute__(self, "_t"), k)

    def __setattr__(self, k, v):
        setattr(object.__getattribute__(self, "_t"), k, v)

    def __getitem__(self, key):
        p, c = key
        c0 = object.__getattribute__(self, "_c0"); c1 = object.__getattribute__(self, "_c1")
        a = c0 + (c.start or 0); b = c0 + (c.stop if c.stop is not None else (c1 - c0))
        return object.__getattribute__(self, "_t").h[p, a:b]


class AV(TV):
    def __init__(self, t, fn):
        object.__setattr__(self, "_t", t); object.__setattr__(self, "_fn", fn)

    def __getitem__(self, key):
        return object.__getattribute__(self, "_fn")()[key]


class Op:
    __slots__ = ("eng", "fn", "waits", "dwaits", "idx", "need_inc", "dma_tile", "dma_val")

    def __init__(self, eng, fn):
        self.eng = eng
        self.fn = fn
        self.waits = []
        self.dwaits = []
        self.idx = None
        self.need_inc = False
        self.dma_tile = None
        self.dma_val = 0


class MK:
    def __init__(self, nc, st):
        self.nc = nc
        self.st = st
        self.ops = []
        self.cnt = {e: 0 for e in ENGS}
        self.eng_ops = {e: [] for e in ENGS}
        self.seen = {e: {o: -1 for o in ENGS} for e in ENGS}
        self.dseen = {e: {} for e in ENGS}
        self.tiles = []
        self.pend = {e: None for e in ENGS}
        self.defer = None

    def sb(self, name, shape, dt, st=None):
        h = (st or self.st).enter_context(self.nc.sbuf_tensor(name, list(shape), dt))
        t = T(h, name)
        self.tiles.append(t)
        return t

    def ps(self, name, shape, dt, st=None):
        h = (st or self.st).enter_context(self.nc.psum_tensor(name, list(shape), dt))
        t = T(h, name)
        self.tiles.append(t)
        return t

    def alias(self, name, h):
        t = T(h, name)
        self.tiles.append(t)
        return t

    def _dep(self, op, eng, tgt):
        if tgt is None:
            return
        te, ti = tgt
        if te == eng:
            return
        if self.seen[eng][te] >= ti:
            return
        op.waits.append((te, ti))
        self.seen[eng][te] = ti

    def barrier(self):
        for e in ENGS:
            w = [(o, self.cnt[o] - 1) for o in ENGS if o != e and self.cnt[o] > 0]
            d = [(t, t.dcnt) for t in self.tiles if t.sem is not None and t.dcnt > 0]
            self.pend[e] = (w, d)

    def _apply_pend(self, o, eng):
        p = self.pend[eng]
        if p is None:
            return
        self.pend[eng] = None
        for tgt in p[0]:
            self._dep_any(o, eng, tgt)
        for (t, v) in p[1]:
            if self.dseen[eng].get(id(t), 0) < v:
                o.dwaits.append((t, v))
                self.dseen[eng][id(t)] = v

    def replay(self, lst, n):
        assert self.defer is None
        for _ in range(min(n, len(lst))):
            it = lst.pop(0)
            if it[0] == "op":
                self.op(*it[1:])
            else:
                self.dma(*it[1:])

    def op(self, eng, fn, reads=(), writes=()):
        import os
        if self.defer is not None:
            self.defer.append(("op", eng, fn, list(reads), list(writes)))
            return None
        reads = [object.__getattribute__(t, "_t") if isinstance(t, TV) else t for t in reads]
        writes = [object.__getattribute__(t, "_t") if isinstance(t, TV) else t for t in writes]
        if len(self.ops) >= int(os.environ.get("MK_MAXOPS", "100000000")):
            return None
        o = Op(eng, fn)
        self._apply_pend(o, eng)
        for t in reads:
            if t.w is not None:
                if t.w[0] == eng:
                    if eng != "pe" and self.seen[eng][eng] < t.w[1]:
                        o.waits.append(t.w)
                        self.seen[eng][eng] = t.w[1]
                else:
                    self._dep(o, eng, t.w)
            self._ddep(o, eng, t, False)
        for t in writes:
            if t.w is not None and t.w[0] != eng:
                self._dep(o, eng, t.w)
            for re_, ri in t.r.items():
                if re_ != eng:
                    self._dep(o, eng, (re_, ri))
            self._ddep(o, eng, t)
        o.idx = self.cnt[eng]
        self.cnt[eng] += 1
        self.ops.append(o)
        self.eng_ops[eng].append(o)
        for t in reads:
            t.r[eng] = o.idx
        for t in writes:
            t.w = (eng, o.idx)
            t.r = {}
        return o

    def _ddep(self, o, eng, t, write=True):
        for d in ((t.dl, t.dlr) if write else (t.dl,)):
            for k, (stile, v) in d.items():
                if self.dseen[eng].get(k, 0) < v:
                    o.dwaits.append((stile, v))
                    self.dseen[eng][k] = v

    def _dep_any(self, o, eng, tgt):
        te, ti = tgt
        if self.seen[eng][te] >= ti:
            return
        o.waits.append((te, ti))
        self.seen[eng][te] = ti

    def dma(self, eng, fn, semtile, reads=(), writes=()):
        import os
        if self.defer is not None:
            self.defer.append(("dma", eng, fn, semtile, list(reads), list(writes)))
            return None
        if len(self.ops) >= int(os.environ.get("MK_MAXOPS", "100000000")):
            return None
        reads = [object.__getattribute__(t, "_t") if isinstance(t, TV) else t for t in reads]
        writes = [object.__getattribute__(t, "_t") if isinstance(t, TV) else t for t in writes]
        if isinstance(semtile, TV):
            semtile = object.__getattribute__(semtile, "_t")
        o = Op(eng, fn)
        self._apply_pend(o, eng)
        for t in reads:
            if t.w is not None:
                self._dep_any(o, eng, t.w)
            self._ddep(o, eng, t, False)
        for t in writes:
            if t.w is not None:
                self._dep_any(o, eng, t.w)
            for re_, ri in t.r.items():
                self._dep_any(o, eng, (re_, ri))
            self._ddep(o, eng, t)
        if semtile.sem is None:
            semtile.sem = self.st.enter_context(self.nc.semaphore("d_" + semtile.name))
        semtile.dcnt += 16
        o.dma_tile = semtile
        o.dma_val = semtile.dcnt
        self.ops.append(o)
        for t in reads:
            t.dlr[id(semtile)] = (semtile, semtile.dcnt)
        for t in writes:
            t.dl[id(semtile)] = (semtile, semtile.dcnt)
        for t in writes:
            t.w = None
            t.r = {}
        return o

    def flush(self, final=False):
        nc = self.nc
        if not hasattr(self, "esem"):
            self.esem = {e: self.st.enter_context(nc.semaphore("e_" + e)) for e in ENGS}
            self.ordinal = {e: [] for e in ENGS}
            self.eflushed = {e: 0 for e in ENGS}
            self.flushed = 0
            self.n_wait = 0
        E = {"pe": nc.tensor, "act": nc.scalar, "dve": nc.vector, "pool": nc.gpsimd, "sp": nc.sync}
        chunk = self.ops[self.flushed:]
        for e in ENGS:
            if len(self.eng_ops[e]) > self.eflushed[e]:
                self.eng_ops[e][-1].need_inc = True
        for o in chunk:
            for (te, ti) in o.waits:
                if ti >= self.eflushed[te]:
                    self.eng_ops[te][ti].need_inc = True
        for e in ENGS:
            c = self.ordinal[e][-1] if self.ordinal[e] else 0
            for o in self.eng_ops[e][self.eflushed[e]:]:
                if o.need_inc:
                    c += 1
                self.ordinal[e].append(c)
        for o in chunk:
            eng = E[o.eng]
            for (te, ti) in o.waits:
                v = self.ordinal[te][ti] if self.eng_ops[te][ti].need_inc else self.ordinal[te][ti] + 1
                eng.wait_ge(self.esem[te], v)
                self.n_wait += 1
            for (t, v) in o.dwaits:
                eng.wait_ge(t.sem, v)
                self.n_wait += 1
            inst = o.fn()
            if o.dma_tile is not None:
                inst.then_inc(o.dma_tile.sem, 16)
            elif o.need_inc:
                inst.then_inc(self.esem[o.eng], 1)
        self.flushed = len(self.ops)
        for e in ENGS:
            self.eflushed[e] = len(self.eng_ops[e])
        if final:
            for t in self.tiles:
                if t.sem is not None and t.dcnt > 0:
                    nc.sync.wait_ge(t.sem, t.dcnt)
        return self.n_wait

D = 1024
INW = 3384
ALPHA = 2.0 ** 0.25
THETA = 500000.0
NEG = -1.0e30


def build(S, n_sel, phases="AB", dbg=False):
    import os
    STOP = int(os.environ.get("MK_STOP", "99"))
    NT = S // 128
    nc = bass.Bass("TRN2", target_bir_lowering=False)
    V, A, G, PE, SP = nc.vector, nc.scalar, nc.gpsimd, nc.tensor, nc.sync

    def din(name, shape, dt=F32):
        return nc.dram_tensor(name, shape, dt, kind="ExternalInput").ap()

    x = din("x", [S, D]); pos = din("pos", [128, NT], I32); mem = din("mem", [256, D])
    w_in = din("w_in", [D, INW]); gate_up = din("gate_up", [16, 256]); gate_bias = din("gate_bias", [1, 256])
    norm_g = din("norm_g", [1, 128]); w_out = din("w_out", [D, D])
    ln1g = din("ln1g", [1, D]); ln1b = din("ln1b", [1, D])
    wq = din("wq", [D, D]); wk = din("wk", [D, D]); wv = din("wv", [D, D]); wo = din("wo", [D, D])
    ln2g = din("ln2g", [1, D]); ln2b = din("ln2b", [1, D])
    wpq = din("wpq", [D, 2048]); sk1 = din("sk1", [128, 128]); sk2 = din("sk2", [128, 128])
    edown = din("edown", [16384, D]); eup = din("eup", [16384, D])
    ln3g = din("ln3g", [1, D]); ln3b = din("ln3b", [1, D])
    y = nc.dram_tensor("y", [S, D], F32, kind="ExternalOutput").ap()
    h1d = nc.dram_tensor("h1d", [S, D], F32, kind="Internal").ap()
    woutd = nc.dram_tensor("woutd", [D, D], BF16, kind="Internal").ap()
    edu = nc.dram_tensor("edu", [16384, 2 * D], BF16, kind="Internal").ap()

    with contextlib.ExitStack() as st:
        mk = MK(nc, st)
        H1D = mk.alias("h1dT", None)
        EDU = mk.alias("eduT", None)
        WOD = mk.alias("woutdT", None)
        conv_jobs = [(c, src, off) for c in range(16) for (src, off) in ((edown, 0), (eup, D))]

        def conv_issue(n):
            for _ in range(n):
                if conv_jobs:
                    c, src, off = conv_jobs.pop(0)
                    mk.dma("pool", lambda c=c, src=src, off=off: G.dma_start(out=edu[c * 1024:(c + 1) * 1024, off:off + D].rearrange("(p a) d -> p a d", p=128), in_=src[c * 1024:(c + 1) * 1024, :].rearrange("(p a) d -> p a d", p=128), max_dma_last_dim=4096), EDU, writes=[EDU])
        banks = [mk.ps("bank%d" % i, [128, 512], F32) for i in range(8)]

        def bf(i):
            return banks[i][:, :]

        def bb(i):
            return banks[i][:, :].bitcast(BF16)

        identf = mk.sb("identf", [128, 128], F32)
        ident = mk.sb("ident", [128, 128], BF16)
        mk.op("pool", lambda: G.memset(identf[:, :], 1.0), writes=[identf])
        mk.op("pool", lambda: G.affine_select(out=identf[:, :], in_=identf[:, :], pattern=[[-1, 128]], compare_op=ALU.is_equal, fill=0.0, base=0, channel_multiplier=1), reads=[identf], writes=[identf])
        mk.op("dve", lambda: V.tensor_copy(out=ident[:, :], in_=identf[:, :]), reads=[identf], writes=[ident])
        negI = mk.sb("negI", [128, 4, 128], BF16)
        mk.op("dve", lambda: V.tensor_scalar(out=negI[:, :, :], in0=identf[:, :].unsqueeze(1).to_broadcast([128, 4, 128]), scalar1=-30000.0, scalar2=None, op0=ALU.mult), reads=[identf], writes=[negI])

        def ln_consts(gd, bd, stk, nm, dt=F32):
            g_t = mk.sb(nm + "gs", [128, D], dt, st=stk)
            b_t = mk.sb(nm + "bs", [128, D], dt, st=stk)
            if dt == F32:
                mk.dma("sp", lambda: SP.dma_start(out=g_t[:, :], in_=gd.broadcast_to([128, D])), g_t, writes=[g_t])
                mk.dma("sp", lambda: SP.dma_start(out=b_t[:, :], in_=bd.broadcast_to([128, D])), b_t, writes=[b_t])
            else:
                mk.dma("pool", lambda: G.dma_start(out=g_t[:, :], in_=gd.broadcast_to([128, D])), g_t, writes=[g_t])
                mk.dma("pool", lambda: G.dma_start(out=b_t[:, :], in_=bd.broadcast_to([128, D])), b_t, writes=[b_t])
            return g_t, b_t

        def layer_norm(r_t, g_t, b_t, out_t, wk_t):
            mk.op("dve", lambda: V.bn_stats(out=wk_t[:, 0:6], in_=r_t[:, 0:512]), reads=[r_t], writes=[wk_t])
            mk.op("dve", lambda: V.bn_stats(out=wk_t[:, 6:12], in_=r_t[:, 512:1024]), reads=[r_t], writes=[wk_t])
            mk.op("dve", lambda: V.bn_aggr(out=wk_t[:, 12:14], in_=wk_t[:, 0:12]), reads=[wk_t], writes=[wk_t])
            mk.op("dve", lambda: V.tensor_scalar_add(out=wk_t[:, 14:15], in0=wk_t[:, 13:14], scalar1=1e-5), reads=[wk_t], writes=[wk_t])
            mk.op("act", lambda: A.sqrt(out=wk_t[:, 15:16], in_=wk_t[:, 14:15]), reads=[wk_t], writes=[wk_t])
            mk.op("dve", lambda: V.reciprocal(out=wk_t[:, 16:17], in_=wk_t[:, 15:16]), reads=[wk_t], writes=[wk_t])
            mk.op("dve", lambda: V.tensor_scalar(out=r_t[:, :], in0=r_t[:, :], scalar1=wk_t[:, 12:13], scalar2=wk_t[:, 16:17], op0=ALU.subtract, op1=ALU.mult), reads=[r_t, wk_t], writes=[r_t])
            mk.op("pool", lambda: G.tensor_tensor(out=r_t[:, :], in0=r_t[:, :], in1=g_t[:, :], op=ALU.mult), reads=[r_t, g_t], writes=[r_t])
            mk.op("pool", lambda: G.tensor_tensor(out=out_t[:, :], in0=r_t[:, :], in1=b_t[:, :], op=ALU.add), reads=[r_t, b_t], writes=[out_t])

        def transpose8(src_bf, dst_fn, bank_i, nblk=8):
            bk = banks[bank_i]
            for k in range(nblk):
                mk.op("pe", lambda k=k: PE.transpose(out=bb(bank_i)[:, k * 128:(k + 1) * 128], in_=src_bf[:, k * 128:(k + 1) * 128], identity=ident[:, :]), reads=[src_bf, ident], writes=[bk])
            dst_fn(bk)

        if "A" in phases:
          with contextlib.ExitStack() as sa:
            def sb(name, shape, dt):
                return mk.sb(name, shape, dt, st=sa)
            winb = sb("winb", [128, 8, INW], BF16)
            WQ = [sb("WQ%d" % q, [128, 8, 256], BF16) for q in range(2)]
            for k in range(8):
                mk.dma("pool", lambda k=k: G.dma_start(out=woutd[k * 128:(k + 1) * 128, :], in_=w_out[k * 128:(k + 1) * 128, :], max_dma_last_dim=4096), WOD, writes=[WOD])

            def wq_load(q):
                wt = WQ[q % 2]
                mk.dma("sp", lambda q=q, wt=wt: SP.dma_start(out=wt[:, :, :], in_=woutd[:, q * 256:(q + 1) * 256].rearrange("(k p) n -> p k n", p=128)), wt, reads=[WOD], writes=[wt])
            for k in range(8):
                mk.dma("pool", lambda k=k: G.dma_start(out=winb[:, k, :], in_=w_in[k * 128:(k + 1) * 128, :], max_dma_last_dim=4096), winb, writes=[winb])
            g1, b1 = ln_consts(ln1g, ln1b, sa, "ln1", BF16)
            kTc = sb("kTc", [128, 4, S], BF16)
            Vc = sb("Vc", [128, NT, 8, 66], BF16)
            kidxT = sb("kidxT", [128, S], BF16)
            mk.op("pool", lambda: G.memset(Vc[:, :, :, :], 1.0), writes=[Vc])
            CB = sb("CB", [128, 128], F32)
            mk.op("pool", lambda: G.memset(CB[:, :], 0.0), writes=[CB])
            mk.op("pool", lambda: G.affine_select(out=CB[:, :], in_=CB[:, :], pattern=[[-1, 128]], compare_op=ALU.is_ge, fill=NEG, base=0, channel_multiplier=1), reads=[CB], writes=[CB])
            TRI = sb("TRI", [128, 128], F32)
            mk.op("pool", lambda: G.memset(TRI[:, :], -1.0 / 16), writes=[TRI])
            mk.op("pool", lambda: G.affine_select(out=TRI[:, :], in_=TRI[:, :], pattern=[[1, 128]], compare_op=ALU.is_ge, fill=0.0, base=0, channel_multiplier=-1), reads=[TRI], writes=[TRI])
            TRU = sb("TRU", [128, 128], F32)
            mk.op("pool", lambda: G.memset(TRU[:, :], -1.0 / 16), writes=[TRU])
            mk.op("pool", lambda: G.affine_select(out=TRU[:, :], in_=TRU[:, :], pattern=[[-1, 128]], compare_op=ALU.is_gt, fill=0.0, base=0, channel_multiplier=1), reads=[TRU], writes=[TRU])
            POW2 = sb("POW2", [128, 16], F32)
            for j in range(16):
                mk.op("pool", lambda j=j: G.memset(POW2[:, j:j + 1], 2.0 ** -(j + 1)), writes=[POW2])
            F12 = sb("F12", [128, 12], F32)
            for j in range(12):
                f = THETA ** (-(j / 8.0)) if j < 8 else THETA ** (-((j - 8) / 4.0))
                mk.op("pool", lambda j=j, f=f: G.memset(F12[:, j:j + 1], f / (2 * np.pi)), writes=[F12])
            posi = sb("posi", [128, NT], I32)
            posf = sb("posf", [128, NT], F32)
            PROJ = sb("PROJ", [128, INW], F32)
            class _V3:
                def __init__(self, c0, dt=None):
                    self.c0 = c0; self.dt = dt
                def __getitem__(self, key):
                    a = PROJ[:, self.c0:self.c0 + NT * 24]
                    if self.dt is not None:
                        a = a.bitcast(self.dt)
                    return a.rearrange("p (t f) -> p t f", t=NT)[key]
            SCt = _V3(0); SCn = _V3(NT * 24, I32); SCf = _V3(2 * NT * 24); SCm = _V3(3 * NT * 24)
            SC = sb("SC", [128, NT, 24], F32)
            mk.dma("sp", lambda: SP.dma_start(out=posi[:, :], in_=pos[:, :]), posi, writes=[posi])
            mk.op("dve", lambda: V.tensor_copy(out=posf[:, :], in_=posi[:, :]), reads=[posi], writes=[posf])
            mk.op("dve", lambda: V.tensor_tensor(out=SCt[:, :, 0:12], in0=posf[:, :].unsqueeze(2).to_broadcast([128, NT, 12]), in1=F12[:, :].unsqueeze(1).to_broadcast([128, NT, 12]), op=ALU.mult), reads=[posf, F12], writes=[PROJ])
            mk.op("dve", lambda: V.tensor_scalar_add(out=SCt[:, :, 12:24], in0=SCt[:, :, 0:12], scalar1=0.25), reads=[PROJ], writes=[PROJ])
            mk.op("dve", lambda: V.tensor_copy(out=SCn[:, :, :], in_=SCt[:, :, :]), reads=[PROJ], writes=[PROJ])
            mk.op("dve", lambda: V.tensor_copy(out=SCf[:, :, :], in_=SCn[:, :, :]), reads=[PROJ], writes=[PROJ])
            mk.op("dve", lambda: V.tensor_tensor(out=SCt[:, :, :], in0=SCt[:, :, :], in1=SCf[:, :, :], op=ALU.subtract), reads=[PROJ], writes=[PROJ])
            mk.op("dve", lambda: V.tensor_single_scalar(out=SCm[:, :, :], in_=SCt[:, :, :], scalar=0.5, op=ALU.is_gt), reads=[PROJ], writes=[PROJ])
            mk.op("dve", lambda: V.tensor_tensor(out=SCt[:, :, :], in0=SCt[:, :, :], in1=SCm[:, :, :], op=ALU.subtract), reads=[PROJ], writes=[PROJ])
            mk.op("dve", lambda: V.tensor_single_scalar(out=SCm[:, :, :], in_=SCt[:, :, :], scalar=-0.5, op=ALU.is_lt), reads=[PROJ], writes=[PROJ])
            mk.op("dve", lambda: V.tensor_tensor(out=SCt[:, :, :], in0=SCt[:, :, :], in1=SCm[:, :, :], op=ALU.add), reads=[PROJ], writes=[PROJ])
            mk.op("act", lambda: A.activation(out=SC[:, :, :], in_=SCt[:, :, :], func=AF.Sin, scale=2 * np.pi * (1 - 2e-6)), reads=[PROJ], writes=[SC])
            GU = sb("GU", [17, 256], F32)
            mk.dma("sp", lambda: SP.dma_start(out=GU[0:16, :], in_=gate_up[:, :]), GU, writes=[GU])
            mk.dma("sp", lambda: SP.dma_start(out=GU[16:17, :], in_=gate_bias[:, :]), GU, writes=[GU])
            NG = sb("NG", [128, 128], F32)
            mk.dma("sp", lambda: SP.dma_start(out=NG[:, :], in_=norm_g.broadcast_to([128, 128])), NG, writes=[NG])
            glrT = sb("glrT", [17, 128], F32)
            mk.op("pool", lambda: G.memset(glrT[:, :], 1.0), writes=[glrT])
            Sst = sb("Sst", [64, 4, 128], F32)
            Sb = sb("Sb", [64, 4, 128], BF16)
            mk.op("pool", lambda: G.memset(Sst[:, :, :], 0.0), writes=[Sst])
            mk.op("pool", lambda: G.memset(Sb[:, :, :], 0.0), writes=[Sb])

            xs = sb("xs", [128, D], F32)
            xb = sb("xb", [128, D], BF16)
            xT = sb("xT", [128, 8, 128], BF16)
            GQK = sb("GQK", [64, 8, 128], BF16)
            QK16 = xb
            IDX16 = TV(xb, 0, 384)
            qT = sb("qT", [128, 4, 128], BF16)
            qidxT = sb("qidxT", [128, 2, 128], BF16)
            wst = sb("wst", [128, 16], F32)
            Ssc = sb("Ssc", [128, max(S, 3200)], F32)
            MB = sb("MB", [128, S], BF16)
            rtmp = [sb("rtmp%d" % i, [128, 512], F32) for i in range(2)]
            t1, t2, t3, t4 = [AV(rtmp[1], (lambda q=q: rtmp[1][:, q * 128:(q + 1) * 128].rearrange("p (a b) -> p a b", a=16))) for q in range(4)]
            sq = rtmp[0]; sil = rtmp[1]
            Lg = AV(Ssc, lambda: Ssc[:, 0:256])
            E1 = AV(Ssc, lambda: Ssc[0:64, 256:768].rearrange("p (a b) -> p a b", a=4))
            E2 = AV(Ssc, lambda: Ssc[0:64, 768:1280].rearrange("p (a b) -> p a b", a=4))
            Ee = AV(Ssc, lambda: Ssc[:, 1280:1536])
            og = AV(Ssc, lambda: Ssc[:, 1536:2048].rearrange("p (a b) -> p a b", a=4))
            qdT = AV(Ssc, lambda: Ssc[0:64, 2048:2304].bitcast(BF16).rearrange("p (a b) -> p a b", a=4))
            kiT = AV(Ssc, lambda: Ssc[0:64, 2304:2560].bitcast(BF16).rearrange("p (a b) -> p a b", a=4))
            kte = AV(Ssc, lambda: Ssc[:, 2560:2688].bitcast(BF16))
            gvb = AV(Ssc, lambda: Ssc[:, 2688:2944].bitcast(BF16))
            ATb = AV(Ssc, lambda: Ssc[:, 2944:3200].bitcast(BF16).rearrange("p (a b) -> p a b", a=4))
            tk = sb("tk", [128, 64], F32)
            dmy = sb("dmy", [128, 8], F32)
            PT0 = sb("PT0", [128, 8, 128], BF16)
            PT = [PT0, sb("PT1", [128, 8, 128], BF16)]
            Y16 = xb
            yT = xT
            rec8 = sb("rec8", [128, 8], F32)
            gst = sb("gst", [128, 16], F32)
            rr = TV(PROJ, 0, D)
            h1 = rr
            lnw = sb("lnw", [128, 32], F32)

            for i in range(NT):
                nk = (i + 1) * 128
                mk.dma("sp", lambda i=i: SP.dma_start(out=xs[:, :], in_=x[i * 128:(i + 1) * 128, :]), xs, writes=[xs])
                wq_load(0); wq_load(1)
                mk.op("act", lambda: A.copy(out=xb[:, :], in_=xs[:, :]), reads=[xs], writes=[xb])
                transpose8(xb, lambda bk: mk.op("dve", lambda: V.tensor_copy(out=xT[:, :, :], in_=bb(0).rearrange("p (a b) -> p a b", a=8)), reads=[bk], writes=[xT]), 0)
                if STOP < 1:
                    continue
                c0 = 0
                ci = 0
                while c0 < INW:
                    cw = min(512, INW - c0)
                    bi = 1 + (ci % 2)
                    for k in range(8):
                        mk.op("pe", lambda k=k, c0=c0, cw=cw, bi=bi: PE.matmul(bf(bi)[:, 0:cw], lhsT=xT[:, k, :], rhs=winb[:, k, c0:c0 + cw], start=(k == 0), stop=(k == 7)), reads=[xT, winb], writes=[banks[bi]])
                    if ci % 2 == 0:
                        mk.op("act", lambda c0=c0, cw=cw, bi=bi: A.copy(out=PROJ[:, c0:c0 + cw], in_=bf(bi)[:, 0:cw]), reads=[banks[bi]], writes=[PROJ])
                    else:
                        mk.op("dve", lambda c0=c0, cw=cw, bi=bi: V.tensor_copy(out=PROJ[:, c0:c0 + cw], in_=bf(bi)[:, 0:cw]), reads=[banks[bi]], writes=[PROJ])
                    c0 += cw
                    ci += 1
                for j in range(8):
                    col = (1832 if j < 4 else 2088) + (j % 4) * 64
                    bi = 3 + (j // 4)
                    for k in range(8):
                        mk.op("pe", lambda k=k, col=col, bi=bi, j=j: PE.matmul(bf(bi)[0:64, (j % 4) * 128:(j % 4 + 1) * 128], lhsT=winb[:, k, col:col + 64], rhs=xT[:, k, :], start=(k == 0), stop=(k == 7)), reads=[xT, winb], writes=[banks[bi]])
                mk.op("act", lambda: A.copy(out=GQK[:, 0:4, :], in_=bf(3)[0:64, :].rearrange("p (a b) -> p a b", a=4)), reads=[banks[3]], writes=[GQK])
                mk.op("act", lambda: A.copy(out=GQK[:, 4:8, :], in_=bf(4)[0:64, :].rearrange("p (a b) -> p a b", a=4)), reads=[banks[4]], writes=[GQK])

                if STOP < 2:
                    continue
                def rot(base, nh, hd, half, f0, i=i):
                    v3 = PROJ[:, base:base + nh * hd].rearrange("p (h d) -> p h d", h=nh)
                    x1 = v3[:, :, 0:half]; x2 = v3[:, :, half:2 * half]
                    sn = SC[:, i, f0:f0 + half].unsqueeze(1).to_broadcast([128, nh, half])
                    cs = SC[:, i, 12 + f0:12 + f0 + half].unsqueeze(1).to_broadcast([128, nh, half])
                    a1 = t1[:, 0:nh, 0:half]; a2 = t2[:, 0:nh, 0:half]; a3 = t3[:, 0:nh, 0:half]; a4 = t4[:, 0:nh, 0:half]
                    mk.op("dve", lambda: V.tensor_tensor(out=a1, in0=x1, in1=cs, op=ALU.mult), reads=[PROJ, SC], writes=[t1])
                    mk.op("dve", lambda: V.tensor_tensor(out=a2, in0=x2, in1=sn, op=ALU.mult), reads=[PROJ, SC], writes=[t2])
                    mk.op("dve", lambda: V.tensor_tensor(out=a3, in0=x2, in1=cs, op=ALU.mult), reads=[PROJ, SC], writes=[t3])
                    mk.op("dve", lambda: V.tensor_tensor(out=a4, in0=x1, in1=sn, op=ALU.mult), reads=[PROJ, SC], writes=[t4])
                    mk.op("dve", lambda: V.tensor_tensor(out=x1, in0=a1, in1=a2, op=ALU.subtract), reads=[t1, t2], writes=[PROJ])
                    mk.op("dve", lambda: V.tensor_tensor(out=x2, in0=a3, in1=a4, op=ALU.add), reads=[t3, t4], writes=[PROJ])
                rot(0, 16, 64, 8, 0)
                rot(1536, 9, 32, 4, 8)

                if i == 0:
                    print("OPS before stage3", len(mk.ops))
                if STOP < 3:
                    continue
                mk.op("act", lambda: A.copy(out=QK16[:, :], in_=PROJ[:, 0:1024]), reads=[PROJ], writes=[QK16])
                def ev_qk(bk, i=i):
                    mk.op("dve", lambda: V.tensor_copy(out=qT[:, :, :], in_=bb(0)[:, 0:512].rearrange("p (a b) -> p a b", a=4)), reads=[bk], writes=[qT])
                    mk.op("dve", lambda: V.tensor_copy(out=kTc[:, :, i * 128:(i + 1) * 128], in_=bb(0)[:, 512:1024].rearrange("p (a b) -> p a b", a=4)), reads=[bk], writes=[kTc])
                transpose8(QK16, ev_qk, 0)
                mk.op("act", lambda i=i: A.copy(out=Vc[:, i, :, 0:64], in_=PROJ[:, 1024:1536].rearrange("p (h d) -> p h d", h=8)), reads=[PROJ], writes=[Vc])
                mk.op("dve", lambda: V.tensor_copy(out=IDX16[:, 0:256], in_=PROJ[:, 1536:1792]), reads=[PROJ], writes=[IDX16])
                mk.op("dve", lambda: V.tensor_copy(out=IDX16[:, 256:384].rearrange("p (r d) -> p r d", r=4), in_=PROJ[:, 1792:1824].unsqueeze(1).to_broadcast([128, 4, 32])), reads=[PROJ], writes=[IDX16])
                def ev_idx(bk, i=i):
                    mk.op("dve", lambda: V.tensor_copy(out=qidxT[:, :, :], in_=bb(0)[:, 0:256].rearrange("p (a b) -> p a b", a=2)), reads=[bk], writes=[qidxT])
                    mk.op("dve", lambda: V.tensor_copy(out=kidxT[:, i * 128:(i + 1) * 128], in_=bb(0)[:, 256:384]), reads=[bk], writes=[kidxT])
                transpose8(IDX16, ev_idx, 0, nblk=3)
                mk.op("dve", lambda: V.tensor_scalar(out=wst[:, 8:16], in0=PROJ[:, 1824:1832], scalar1=0.0, scalar2=2.0, op0=ALU.is_gt, op1=ALU.mult), reads=[PROJ], writes=[wst])
                mk.op("dve", lambda: V.tensor_scalar_add(out=wst[:, 8:16], in0=wst[:, 8:16], scalar1=-1.0), reads=[wst], writes=[wst])
                mk.op("dve", lambda: V.tensor_tensor(out=wst[:, 0:8], in0=PROJ[:, 1824:1832], in1=wst[:, 8:16], op=ALU.mult), reads=[PROJ, wst], writes=[wst])

                if i == 0:
                    print("OPS before stage4", len(mk.ops))
                if STOP < 4:
                    continue
                nblk = (nk + 511) // 512
                for b in range(nblk):
                    k0 = b * 512
                    kw = min(512, nk - k0)
                    for h in range(8):
                        bi = 1 + (h % 2)
                        rt = rtmp[h % 2]
                        pb = 32 * (h % 4)
                        mk.op("pe", lambda h=h, bi=bi, pb=pb, k0=k0, kw=kw: PE.matmul(bf(bi)[:, 0:kw], lhsT=qidxT[pb:pb + 32, h // 4, :], rhs=kidxT[pb:pb + 32, k0:k0 + kw], start=True, stop=True, tile_position=(pb, 0)), reads=[qidxT, kidxT], writes=[banks[bi]])
                        mk.op("act", lambda h=h, bi=bi, kw=kw, rt=rt: A.activation(out=rt[:, 0:kw], in_=bf(bi)[:, 0:kw], func=AF.Relu, scale=wst[:, h:h + 1]), reads=[banks[bi], wst], writes=[rt])
                        if h == 0:
                            mk.op("dve", lambda h=h, k0=k0, kw=kw, rt=rt: V.tensor_scalar(out=Ssc[:, k0:k0 + kw], in0=rt[:, 0:kw], scalar1=wst[:, 8 + h:9 + h], scalar2=None, op0=ALU.mult), reads=[rt, wst], writes=[Ssc])
                        else:
                            mk.op("dve", lambda h=h, k0=k0, kw=kw, rt=rt: V.scalar_tensor_tensor(out=Ssc[:, k0:k0 + kw], in0=rt[:, 0:kw], scalar=wst[:, 8 + h:9 + h], in1=Ssc[:, k0:k0 + kw], op0=ALU.mult, op1=ALU.add), reads=[rt, wst, Ssc], writes=[Ssc])
                if STOP < 5:
                    continue
                use_topk = nk > n_sel
                if use_topk:
                    mk.op("dve", lambda nk=nk: V.tensor_reduce(out=tk[:, 0:1], in_=Ssc[:, 0:nk], axis=AX.X, op=ALU.max), reads=[Ssc], writes=[tk])
                    mk.op("dve", lambda nk=nk: V.tensor_reduce(out=tk[:, 1:2], in_=Ssc[:, 0:nk], axis=AX.X, op=ALU.min), reads=[Ssc], writes=[tk])
                mk.op("dve", lambda i=i: V.tensor_tensor(out=Ssc[:, i * 128:(i + 1) * 128], in0=Ssc[:, i * 128:(i + 1) * 128], in1=CB[:, :], op=ALU.add), reads=[Ssc, CB], writes=[Ssc])
                if use_topk:
                    mk.op("dve", lambda: V.tensor_tensor(out=tk[:, 2:3], in0=tk[:, 0:1], in1=tk[:, 1:2], op=ALU.subtract), reads=[tk], writes=[tk])
                    mk.op("dve", lambda: V.tensor_scalar(out=tk[:, 8:24], in0=POW2[:, :], scalar1=tk[:, 2:3], scalar2=None, op0=ALU.mult), reads=[tk, POW2], writes=[tk])
                    mk.op("dve", lambda: V.tensor_tensor(out=tk[:, 3:4], in0=tk[:, 1:2], in1=tk[:, 8:9], op=ALU.add), reads=[tk], writes=[tk])
                    for it in range(16):
                        mk.op("dve", lambda nk=nk: V.tensor_scalar(out=MB[:, 0:nk], in0=Ssc[:, 0:nk], scalar1=tk[:, 3:4], scalar2=None, op0=ALU.is_ge, op1=ALU.add, accum_out=tk[:, 4:5]), reads=[Ssc, tk], writes=[MB, tk])
                        mk.op("dve", lambda: V.memset(dmy[:, 0:1], 0.0), writes=[dmy])
                        mk.op("dve", lambda it=it: V.scalar_tensor_tensor(out=tk[:, 5:6], in0=tk[:, 4:5], scalar=float(n_sel), in1=tk[:, 8 + it:9 + it], op0=ALU.is_ge, op1=ALU.mult), reads=[tk], writes=[tk])
                        if it < 15:
                            mk.op("dve", lambda it=it: V.scalar_tensor_tensor(out=tk[:, 3:4], in0=tk[:, 5:6], scalar=tk[:, 9 + it:10 + it], in1=tk[:, 3:4], op0=ALU.subtract, op1=ALU.add), reads=[tk], writes=[tk])
                        else:
                            mk.op("dve", lambda: V.scalar_tensor_tensor(out=tk[:, 1:2], in0=tk[:, 5:6], scalar=tk[:, 23:24], in1=tk[:, 3:4], op0=ALU.subtract, op1=ALU.add), reads=[tk], writes=[tk])
                    mk.op("dve", lambda nk=nk: V.tensor_scalar(out=MB[:, 0:nk], in0=Ssc[:, 0:nk], scalar1=tk[:, 1:2], scalar2=None, op0=ALU.is_lt), reads=[Ssc, tk], writes=[MB])
                else:
                    mk.op("dve", lambda nk=nk: V.tensor_scalar(out=MB[:, 0:nk], in0=Ssc[:, 0:nk], scalar1=-1.0e29, scalar2=None, op0=ALU.is_lt), reads=[Ssc], writes=[MB])

                if STOP < 6:
                    continue
                for j in range(i + 1):
                    lb = 2 + 2 * (j % 2)
                    pt = PT[j % 2]
                    for h in range(8) if os.environ.get("MK_MERGE", "1") == "0" else []:
                        bi = lb + h // 4
                        pb = 64 * (h % 2)
                        o_ap = (lambda bi=bi, h=h: bf(bi)[:, (h % 4) * 128:(h % 4 + 1) * 128])
                        mk.op("pe", lambda h=h, pb=pb, j=j, o_ap=o_ap: PE.matmul(o_ap(), lhsT=kTc[pb:pb + 64, h // 2, j * 128:(j + 1) * 128], rhs=qT[pb:pb + 64, h // 2, :], start=True, stop=False, tile_position=(pb, 0)), reads=[kTc, qT], writes=[banks[bi]])
                        mk.op("pe", lambda j=j, o_ap=o_ap: PE.matmul(o_ap(), lhsT=MB[:, j * 128:(j + 1) * 128], rhs=negI[:, 0, :], start=False, stop=True), reads=[MB, negI], writes=[banks[bi]])
                    if os.environ.get("MK_MERGE", "1") == "1":
                        for r_ in range(4):
                            for par in range(2):
                                h = 2 * r_ + par
                                pb = 64 * par
                                bi = lb + par
                                mk.op("pe", lambda h=h, pb=pb, j=j, bi=bi, r_=r_: PE.matmul(bf(bi)[:, r_ * 128:(r_ + 1) * 128], lhsT=kTc[pb:pb + 64, h // 2, j * 128:(j + 1) * 128], rhs=qT[pb:pb + 64, h // 2, :], start=(r_ == 0), stop=False, tile_position=(pb, 0), skip_group_check=True), reads=[kTc, qT], writes=[banks[bi]])
                        for par in range(2):
                            bi = lb + par
                            mk.op("pe", lambda j=j, bi=bi: PE.matmul(bf(bi)[:, :], lhsT=MB[:, j * 128:(j + 1) * 128], rhs=negI[:, :, :].rearrange("p a b -> p (a b)"), start=False, stop=True, skip_group_check=True), reads=[MB, negI], writes=[banks[bi]])
                        ptv = (lambda pt=pt: pt[:, :, :].rearrange("p (a two) b -> p a two b", two=2))
                        mk.op("act", lambda lb=lb, ptv=ptv: A.activation(out=ptv()[:, :, 0, :], in_=bf(lb).rearrange("p (a b) -> p a b", a=4), func=AF.Exp, scale=0.125), reads=[banks[lb]], writes=[pt])
                        mk.op("act", lambda lb=lb, ptv=ptv: A.activation(out=ptv()[:, :, 1, :], in_=bf(lb + 1).rearrange("p (a b) -> p a b", a=4), func=AF.Exp, scale=0.125), reads=[banks[lb + 1]], writes=[pt])
                    if os.environ.get("MK_MERGE", "1") == "0":
                        mk.op("act", lambda lb=lb, pt=pt: A.activation(out=pt[:, 0:4, :], in_=bf(lb).rearrange("p (a b) -> p a b", a=4), func=AF.Exp, scale=0.125), reads=[banks[lb]], writes=[pt])
                        mk.op("act", lambda lb=lb, pt=pt: A.activation(out=pt[:, 4:8, :], in_=bf(lb + 1).rearrange("p (a b) -> p a b", a=4), func=AF.Exp, scale=0.125), reads=[banks[lb + 1]], writes=[pt])
                    for h in range(8):
                        bi = 6 + h // 4
                        mk.op("pe", lambda h=h, bi=bi, j=j, pt=pt: PE.matmul(bf(bi)[:, (h % 4) * 65:(h % 4) * 65 + 65], lhsT=pt[:, h, :], rhs=Vc[:, j, h, 0:65], start=(j == 0 and h % 4 == 0), stop=(j == i), skip_group_check=True), reads=[pt, Vc], writes=[banks[bi]])
                for hb in range(2):
                    ov = (lambda hb=hb: bf(6 + hb)[:, 0:260].rearrange("p (h e) -> p h e", h=4))
                    mk.op("dve", lambda hb=hb, ov=ov: V.reciprocal(out=rec8[:, hb * 4:hb * 4 + 4], in_=ov()[:, :, 64]), reads=[banks[6 + hb]], writes=[rec8])
                    mk.op("dve", lambda hb=hb, ov=ov: V.tensor_tensor(out=Y16[:, hb * 256:(hb + 1) * 256].rearrange("p (h e) -> p h e", h=4), in0=ov()[:, :, 0:64], in1=rec8[:, hb * 4:hb * 4 + 4].unsqueeze(2).to_broadcast([128, 4, 64]), op=ALU.mult), reads=[banks[6 + hb], rec8], writes=[Y16])

                if STOP < 7:
                    continue
                mk.op("pe", lambda: PE.transpose(out=bf(1)[0:16, 0:128], in_=PROJ[:, 2856:2872], identity=identf[:, :]), reads=[PROJ, identf], writes=[banks[1]])
                mk.op("act", lambda: A.copy(out=glrT[0:16, :], in_=bf(1)[0:16, 0:128]), reads=[banks[1]], writes=[glrT])
                mk.op("pe", lambda: PE.matmul(bf(2)[:, 0:256], lhsT=glrT[:, :], rhs=GU[:, :], start=True, stop=True), reads=[glrT, GU], writes=[banks[2]])
                mk.op("act", lambda: A.activation(out=Lg[:, :], in_=bf(2)[:, 0:256], func=AF.Exp, scale=-1.0), reads=[banks[2]], writes=[Lg])
                mk.op("act", lambda: A.activation(out=Lg[:, :], in_=Lg[:, :], func=AF.Ln, bias=1.0), reads=[Lg], writes=[Lg])
                for h in range(4):
                    mk.op("pe", lambda h=h: PE.matmul(bf(3)[0:64, h * 128:(h + 1) * 128], lhsT=Lg[:, h * 64:(h + 1) * 64], rhs=TRI[:, :], start=True, stop=True), reads=[Lg, TRI], writes=[banks[3]])
                mk.op("pe", lambda: PE.matmul(bf(4)[:, 0:256], lhsT=TRU[:, :], rhs=Lg[:, :], start=True, stop=True), reads=[Lg, TRU], writes=[banks[4]])
                b3v = (lambda: bf(3)[0:64, :].rearrange("p (a b) -> p a b", a=4))
                mk.op("act", lambda: A.activation(out=E1[:, :, :], in_=b3v(), func=AF.Exp), reads=[banks[3]], writes=[E1])
                mk.op("act", lambda: A.activation(out=E2[:, :, :], in_=b3v(), func=AF.Exp, scale=-1.0), reads=[banks[3]], writes=[E2])
                mk.op("act", lambda: A.activation(out=Ee[:, :], in_=bf(4)[:, 0:256], func=AF.Exp), reads=[banks[4]], writes=[Ee])
                mk.op("dve", lambda: V.scalar_tensor_tensor(out=qdT[:, :, :], in0=GQK[:, 0:4, :], scalar=0.125, in1=E1[:, :, :], op0=ALU.mult, op1=ALU.mult), reads=[GQK, E1], writes=[qdT])
                mk.op("dve", lambda: V.tensor_tensor(out=kiT[:, :, :], in0=GQK[:, 4:8, :], in1=E2[:, :, :], op=ALU.mult), reads=[GQK, E2], writes=[kiT])
                mk.op("dve", lambda: V.tensor_tensor(out=kte[:, :], in0=PROJ[:, 2088:2344], in1=Ee[:, :], op=ALU.mult), reads=[PROJ, Ee], writes=[kte])
                mk.op("act", lambda: A.copy(out=gvb[:, :], in_=PROJ[:, 2344:2856]), reads=[PROJ], writes=[gvb])
                for h in range(4):
                    mk.op("pe", lambda h=h: PE.matmul(bf(5)[:, h * 128:(h + 1) * 128], lhsT=kiT[:, h, :], rhs=qdT[:, h, :], start=True, stop=True), reads=[kiT, qdT], writes=[banks[5]])
                mk.op("dve", lambda: V.scalar_tensor_tensor(out=ATb[:, :, :], in0=bf(5).rearrange("p (a b) -> p a b", a=4), scalar=-16.0, in1=TRI[:, :].unsqueeze(1).to_broadcast([128, 4, 128]), op0=ALU.mult, op1=ALU.mult), reads=[banks[5], TRI], writes=[ATb])
                for h in range(4):
                    mk.op("pe", lambda h=h: PE.matmul(bf(1)[:, h * 128:(h + 1) * 128], lhsT=ATb[:, h, :], rhs=gvb[:, h * 128:(h + 1) * 128], start=True, stop=False), reads=[ATb, gvb], writes=[banks[1]])
                    mk.op("pe", lambda h=h: PE.matmul(bf(1)[:, h * 128:(h + 1) * 128], lhsT=qdT[:, h, :], rhs=Sb[:, h, :], start=False, stop=True), reads=[qdT, Sb], writes=[banks[1]])
                for h in range(4):
                    mk.op("pe", lambda h=h: PE.matmul(bf(2)[0:64, h * 128:(h + 1) * 128], lhsT=kte[:, h * 64:(h + 1) * 64], rhs=gvb[:, h * 128:(h + 1) * 128], start=True, stop=True), reads=[kte, gvb], writes=[banks[2]])
                for h in range(4):
                    mk.op("dve", lambda h=h: V.scalar_tensor_tensor(out=Sst[:, h, :], in0=Sst[:, h, :], scalar=E1[:, h, 127:128], in1=bf(2)[0:64, h * 128:(h + 1) * 128], op0=ALU.mult, op1=ALU.add), reads=[Sst, E1, banks[2]], writes=[Sst])
                mk.op("dve", lambda: V.tensor_copy(out=Sb[:, :, :], in_=Sst[:, :, :]), reads=[Sst], writes=[Sb])
                mk.op("act", lambda: A.copy(out=og[:, :, :], in_=bf(1).rearrange("p (a b) -> p a b", a=4)), reads=[banks[1]], writes=[og])
                mk.op("dve", lambda: V.tensor_tensor(out=sq[:, :], in0=og[:, :, :].rearrange("p a b -> p (a b)"), in1=og[:, :, :].rearrange("p a b -> p (a b)"), op=ALU.mult), reads=[og], writes=[sq])
                mk.op("dve", lambda: V.tensor_reduce(out=gst[:, 0:4], in_=sq[:, :].rearrange("p (a b) -> p a b", a=4), axis=AX.X, op=ALU.add), reads=[sq], writes=[gst])
                mk.op("dve", lambda: V.tensor_scalar(out=gst[:, 4:8], in0=gst[:, 0:4], scalar1=1.0 / 128, scalar2=1e-6, op0=ALU.mult, op1=ALU.add), reads=[gst], writes=[gst])
                mk.op("act", lambda: A.sqrt(out=gst[:, 8:12], in_=gst[:, 4:8]), reads=[gst], writes=[gst])
                mk.op("dve", lambda: V.reciprocal(out=gst[:, 12:16], in_=gst[:, 8:12]), reads=[gst], writes=[gst])
                mk.op("dve", lambda: V.tensor_tensor(out=og[:, :, :], in0=og[:, :, :], in1=gst[:, 12:16].unsqueeze(2).to_broadcast([128, 4, 128]), op=ALU.mult), reads=[og, gst], writes=[og])
                mk.op("dve", lambda: V.tensor_tensor(out=og[:, :, :], in0=og[:, :, :], in1=NG[:, :].unsqueeze(1).to_broadcast([128, 4, 128]), op=ALU.mult), reads=[og, NG], writes=[og])
                mk.op("act", lambda: A.activation(out=sil[:, :], in_=PROJ[:, 2872:3384], func=AF.Silu), reads=[PROJ], writes=[sil])
                mk.op("dve", lambda: V.tensor_tensor(out=Y16[:, 512:1024], in0=og[:, :, :].rearrange("p a b -> p (a b)"), in1=sil[:, :], op=ALU.mult), reads=[og, sil], writes=[Y16])

                if STOP < 8:
                    continue
                transpose8(Y16, lambda bk: mk.op("dve", lambda: V.tensor_copy(out=yT[:, :, :], in_=bb(0).rearrange("p (a b) -> p a b", a=8)), reads=[bk], writes=[yT]), 0)
                for hf in range(2):
                    bi = 1 + hf
                    for qd in range(2):
                        q4 = hf * 2 + qd
                        wt = WQ[q4 % 2]
                        for k in range(8):
                            mk.op("pe", lambda k=k, qd=qd, bi=bi, wt=wt: PE.matmul(bf(bi)[:, qd * 256:(qd + 1) * 256], lhsT=yT[:, k, :], rhs=wt[:, k, :], start=(k == 0), stop=(k == 7)), reads=[yT, wt], writes=[banks[bi]])
                        if q4 + 2 < 4:
                            wq_load(q4 + 2)
                    mk.op("dve", lambda hf=hf, bi=bi: V.scalar_tensor_tensor(out=rr[:, hf * 512:(hf + 1) * 512], in0=xs[:, hf * 512:(hf + 1) * 512], scalar=ALPHA, in1=bf(bi)[:, :], op0=ALU.mult, op1=ALU.add), reads=[xs, banks[bi]], writes=[rr])
                layer_norm(rr, g1, b1, h1, lnw)
                conv_issue((32 + NT - 1) // NT)
                if os.environ.get("MK_DBG") == "M":
                    mk.op("dve", lambda: V.tensor_copy(out=h1[:, 0:24], in_=tk[:, 0:24]), reads=[tk], writes=[h1])
                    mk.op("dve", lambda nk=nk: V.tensor_reduce(out=h1[:, 24:25], in_=MB[:, 0:nk], axis=AX.X, op=ALU.add), reads=[MB], writes=[h1])
                    mk.op("dve", lambda nk=nk: V.tensor_copy(out=h1[:, 32:32 + nk], in_=Ssc[:, 0:nk]), reads=[Ssc], writes=[h1])
                if os.environ.get("MK_DBG") == "Y":
                    mk.op("dve", lambda: V.tensor_copy(out=h1[:, :], in_=Y16[:, :]), reads=[Y16], writes=[h1])
                mk.dma("sp", lambda i=i: SP.dma_start(out=h1d[i * 128:(i + 1) * 128, :], in_=h1[:, :]), h1, reads=[h1], writes=[H1D])
            mk.barrier()
            mk.flush()


        if "B" in phases:
          with contextlib.ExitStack() as sbk:
            def sb(name, shape, dt):
                return mk.sb(name, shape, dt, st=sbk)
            conv_issue(64)
            wqb = sb("wqb", [128, 8, D], BF16); wob = sb("wob", [128, 8, D], BF16)
            wpqb = sb("wpqb", [128, 8, 2048], BF16)
            KmT = sb("KmT", [128, 8, 256], BF16); Vm = sb("Vm", [128, 2, D], BF16)
            skT = sb("skT", [128, 2, 128], BF16)
            g2, b2 = ln_consts(ln2g, ln2b, sbk, "ln2")
            g3, b3 = ln_consts(ln3g, ln3b, sbk, "ln3")
            ones = sb("ones", [128, 128], BF16)
            mk.op("pool", lambda: G.memset(ones[:, :], 1.0), writes=[ones])
            io_i = sb("io_i", [128, 16], I32); IO16 = sb("IO16", [128, 16], F32)
            mk.op("pool", lambda: G.iota(out=io_i[:, :], pattern=[[1, 16]], base=0, channel_multiplier=0), writes=[io_i])
            mk.op("dve", lambda: V.tensor_copy(out=IO16[:, :], in_=io_i[:, :]), reads=[io_i], writes=[IO16])
            dm2 = sb("dm2", [128, 8], F32)
            pro = contextlib.ExitStack()
            def sbp(name, shape, dt):
                return mk.sb(name, shape, dt, st=pro)
            wkb = sbp("wkb", [128, 8, D], BF16); wvb = sbp("wvb", [128, 8, D], BF16)
            for (wt, wd, wn) in ((wkb, wk, D), (wvb, wv, D), (wqb, wq, D), (wob, wo, D), (wpqb, wpq, 2048)):
                for k in range(8):
                    mk.dma("pool", lambda k=k, wt=wt, wd=wd: G.dma_start(out=wt[:, k, :], in_=wd[k * 128:(k + 1) * 128, :], max_dma_last_dim=4096), wt, writes=[wt])
            ms = sbp("ms", [128, 2, D], F32); mb = sbp("mb", [128, 2 * D], BF16)
            memT = sbp("memT", [128, 8, 256], BF16)
            mk.dma("sp", lambda: SP.dma_start(out=ms[:, :, :], in_=mem.rearrange("(a p) d -> p a d", p=128)), ms, writes=[ms])
            mk.op("dve", lambda: V.tensor_copy(out=mb[:, :], in_=ms[:, :, :].rearrange("p a d -> p (a d)")), reads=[ms], writes=[mb])
            for a in range(2):
                def ev_m(bk, a=a):
                    mk.op("dve", lambda: V.tensor_copy(out=memT[:, :, a * 128:(a + 1) * 128], in_=bb(0).rearrange("p (k b) -> p k b", k=8)), reads=[bk], writes=[memT])
                for k in range(8):
                    mk.op("pe", lambda k=k, a=a: PE.transpose(out=bb(0)[:, k * 128:(k + 1) * 128], in_=mb[:, a * D + k * 128:a * D + (k + 1) * 128], identity=ident[:, :]), reads=[mb, ident], writes=[banks[0]])
                ev_m(banks[0])
            for c in range(8):
                bi = 1 + c % 2
                for k in range(8):
                    mk.op("pe", lambda k=k, c=c, bi=bi: PE.matmul(bf(bi)[:, 0:256], lhsT=wkb[:, k, c * 128:(c + 1) * 128], rhs=memT[:, k, :], start=(k == 0), stop=(k == 7)), reads=[wkb, memT], writes=[banks[bi]])
                mk.op("dve", lambda c=c, bi=bi: V.tensor_copy(out=KmT[:, c, :], in_=bf(bi)[:, 0:256]), reads=[banks[bi]], writes=[KmT])
            for mt in range(2):
                for hf in range(2):
                    bi = 3 + hf
                    for k in range(8):
                        mk.op("pe", lambda k=k, mt=mt, hf=hf, bi=bi: PE.matmul(bf(bi)[:, :], lhsT=memT[:, k, mt * 128:(mt + 1) * 128], rhs=wvb[:, k, hf * 512:(hf + 1) * 512], start=(k == 0), stop=(k == 7)), reads=[wvb, memT], writes=[banks[bi]])
                    mk.op("dve", lambda mt=mt, hf=hf, bi=bi: V.tensor_copy(out=Vm[:, mt, hf * 512:(hf + 1) * 512], in_=bf(bi)[:, :]), reads=[banks[bi]], writes=[Vm])
            sks = sbp("sks", [128, 256], F32); skb = sbp("skb", [128, 256], BF16)
            mk.dma("sp", lambda: SP.dma_start(out=sks[:, 0:128], in_=sk1[:, :]), sks, writes=[sks])
            mk.dma("sp", lambda: SP.dma_start(out=sks[:, 128:256], in_=sk2[:, :]), sks, writes=[sks])
            mk.op("dve", lambda: V.tensor_copy(out=skb[:, :], in_=sks[:, :]), reads=[sks], writes=[skb])
            for a in range(2):
                mk.op("pe", lambda a=a: PE.transpose(out=bb(0)[:, a * 128:(a + 1) * 128], in_=skb[:, a * 128:(a + 1) * 128], identity=ident[:, :]), reads=[skb, ident], writes=[banks[0]])
            mk.op("dve", lambda: V.tensor_copy(out=skT[:, :, :], in_=bb(0)[:, 0:256].rearrange("p (a b) -> p a b", a=2)), reads=[banks[0]], writes=[skT])

            mk.barrier()
            mk.flush()
            pro.close()
            h1s = sb("h1s", [128, D], F32); hb = sb("hb", [128, D], BF16); hT = sb("hT", [128, 8, 128], BF16)
            qTb = sb("qTb", [128, 8, 128], BF16)
            PTx = sb("PTx", [128, 2, 4, 128], BF16)
            recs = sb("recs", [128, 4, 128], F32)
            oTn = sb("oTn", [128, 8, 128], BF16)
            rr2 = sb("rr2", [128, D], F32); H2L = [sb("h2_%d" % q, [128, D], F32) for q in range(2)]; lnw2 = sb("lnw2", [128, 32], F32)
            pqT = sb("pqT", [128, 16, 128], BF16)
            scr8 = sb("scr8", [128, 2048], F32)
            SCs = sb("SCs", [128, 16, 128], F32); SC2 = AV(scr8, lambda: scr8[:, :].rearrange("p (a b) -> p a b", a=16))
            V8 = sb("V8", [128, 16, 16], F32); I8 = sb("I8", [128, 16, 16], U32); I8f = sb("I8f", [128, 16, 16], F32)
            cand = AV(SCs, lambda: SCs[:, :, :].rearrange("p (h s) n -> p h (s n)", s=2)); cand2 = AV(scr8, lambda: scr8[:, :].rearrange("p (a b) -> p a b", a=8))
            T16 = sb("T16", [128, 8, 16], F32); P16 = sb("P16", [128, 8, 16], U32)
            Pa = sb("Pa", [128, 8, 16], U32); Pb = sb("Pb", [128, 8, 16], U32)
            Af = sb("Af", [128, 8, 16], F32); Bf = sb("Bf", [128, 8, 16], F32)
            OH = AV(scr8, lambda: scr8[:, :].rearrange("p (h r a) -> p h r a", h=8, r=16))
            i1s = sb("i1s", [128, 128], F32); i2s = sb("i2s", [128, 128], F32)
            EIDL = [sb("EID%d" % q, [128, 128], U32) for q in range(2)]
            GTL = [sb("GT%d" % q, [128, 8, 16], F32) for q in range(2)]; gz = sb("gz", [128, 16], F32)
            dots = sb("dots", [128, 128], F32); coef = sb("coef", [128, 128], F32)
            NB = 4
            GS = 4; NGB = 3
            UG = [sb("UG%d" % q, [128, 2 * D], BF16) for q in range(GS * NGB)]
            DG = [sb("DG%d" % q, [128, GS, 128], BF16) for q in range(NGB)]
            junk = sb("junk", [128, D], BF16)
            gl = sb("gl", [128, 128], F32)
            rr3 = rr2; yo = h1s

            def top16(src, srcT, scratch, scrT, vout, iout, tl):
                vt, it_ = tl
                mk.op("dve", lambda: V.max(out=vout[:, 0:8], in_=src), reads=[srcT], writes=[vt])
                mk.op("dve", lambda: V.max_index(out=iout[:, 0:8], in_max=vout[:, 0:8], in_values=src), reads=[srcT, vt], writes=[it_])
                mk.op("dve", lambda: V.match_replace(out=scratch, in_to_replace=vout[:, 0:8], in_values=src, imm_value=-1.0e30), reads=[srcT, vt], writes=[scrT])
                mk.op("dve", lambda: V.memset(dm2[:, 0:1], 0.0), writes=[dm2])
                mk.op("dve", lambda: V.max(out=vout[:, 8:16], in_=scratch), reads=[scrT], writes=[vt])
                mk.op("dve", lambda: V.max_index(out=iout[:, 8:16], in_max=vout[:, 8:16], in_values=scratch), reads=[scrT, vt], writes=[it_])

            def stage_P(i):
                h2 = H2L[i % 2]; EID = EIDL[i % 2]; GT = GTL[i % 2]
                mk.dma("sp", lambda i=i: SP.dma_start(out=h1s[:, :], in_=h1d[i * 128:(i + 1) * 128, :]), h1s, reads=[H1D], writes=[h1s])
                mk.op("act", lambda: A.copy(out=hb[:, :], in_=h1s[:, :]), reads=[h1s], writes=[hb])
                transpose8(hb, lambda bk: mk.op("dve", lambda: V.tensor_copy(out=hT[:, :, :], in_=bb(0).rearrange("p (a b) -> p a b", a=8)), reads=[bk], writes=[hT]), 0)
                for c in range(8):
                    bi = 1 + c // 4
                    for k in range(8):
                        mk.op("pe", lambda k=k, c=c, bi=bi: PE.matmul(bf(bi)[:, (c % 4) * 128:(c % 4 + 1) * 128], lhsT=wqb[:, k, c * 128:(c + 1) * 128], rhs=hT[:, k, :], start=(k == 0), stop=(k == 7)), reads=[wqb, hT], writes=[banks[bi]])
                for hf in range(2):
                    mk.op("act", lambda hf=hf: A.copy(out=qTb[:, hf * 4:(hf + 1) * 4, :], in_=bf(1 + hf).rearrange("p (a b) -> p a b", a=4)), reads=[banks[1 + hf]], writes=[qTb])
                for mt in range(2):
                    bi = 3 + mt
                    for h in range(4):
                        for cc in range(2):
                            c = 2 * h + cc
                            mk.op("pe", lambda h=h, c=c, cc=cc, mt=mt, bi=bi: PE.matmul(bf(bi)[:, h * 128:(h + 1) * 128], lhsT=KmT[:, c, mt * 128:(mt + 1) * 128], rhs=qTb[:, c, :], start=(cc == 0), stop=(cc == 1)), reads=[KmT, qTb], writes=[banks[bi]])
                    mk.op("act", lambda mt=mt, bi=bi: A.activation(out=PTx[:, mt, :, :], in_=bf(bi).rearrange("p (a b) -> p a b", a=4), func=AF.Exp, scale=1.0 / 16), reads=[banks[bi]], writes=[PTx])
                for c in range(8):
                    bi = (5, 1)[c // 4]
                    for mt in range(2):
                        mk.op("pe", lambda c=c, mt=mt, bi=bi: PE.matmul(bf(bi)[:, (c % 4) * 128:(c % 4 + 1) * 128], lhsT=Vm[:, mt, c * 128:(c + 1) * 128], rhs=PTx[:, mt, c // 2, :], start=(mt == 0), stop=(mt == 1)), reads=[Vm, PTx], writes=[banks[bi]])
                for h in range(4):
                    for mt in range(2):
                        mk.op("pe", lambda h=h, mt=mt: PE.matmul(bf(2)[:, h * 128:(h + 1) * 128], lhsT=ones[:, :], rhs=PTx[:, mt, h, :], start=(mt == 0), stop=(mt == 1)), reads=[ones, PTx], writes=[banks[2]])
                mk.op("dve", lambda: V.reciprocal(out=recs[:, :, :], in_=bf(2).rearrange("p (a b) -> p a b", a=4)), reads=[banks[2]], writes=[recs])
                for hf in range(2):
                    mk.op("dve", lambda hf=hf: V.tensor_tensor(out=oTn[:, hf * 4:(hf + 1) * 4, :].rearrange("p (h c) t -> p h c t", h=2), in0=bf((5, 1)[hf]).rearrange("p (h c t) -> p h c t", h=2, c=2), in1=recs[:, hf * 2:(hf + 1) * 2, :].unsqueeze(2).to_broadcast([128, 2, 2, 128]), op=ALU.mult), reads=[banks[(5, 1)[hf]], recs], writes=[oTn])
                for hf in range(2):
                    bi = 3 + hf
                    for k in range(8):
                        mk.op("pe", lambda k=k, hf=hf, bi=bi: PE.matmul(bf(bi)[:, :], lhsT=oTn[:, k, :], rhs=wob[:, k, hf * 512:(hf + 1) * 512], start=(k == 0), stop=(k == 7)), reads=[oTn, wob], writes=[banks[bi]])
                    mk.op("dve", lambda hf=hf, bi=bi: V.scalar_tensor_tensor(out=rr2[:, hf * 512:(hf + 1) * 512], in0=h1s[:, hf * 512:(hf + 1) * 512], scalar=ALPHA, in1=bf(bi)[:, :], op0=ALU.mult, op1=ALU.add), reads=[h1s, banks[bi]], writes=[rr2])
                layer_norm(rr2, g2, b2, h2, lnw2)
                if os.environ.get("MK_DBG") == "H2":
                    mk.dma("sp", lambda i=i: SP.dma_start(out=y[i * 128:(i + 1) * 128, :], in_=h2[:, :]), h2, reads=[h2])
                    return
                mk.op("act", lambda: A.copy(out=hb[:, :], in_=h2[:, :]), reads=[h2], writes=[hb])
                transpose8(hb, lambda bk: mk.op("dve", lambda: V.tensor_copy(out=hT[:, :, :], in_=bb(0).rearrange("p (a b) -> p a b", a=8)), reads=[bk], writes=[hT]), 0)
                for c in range(16):
                    bi = 1 + c // 4
                    for k in range(8):
                        mk.op("pe", lambda k=k, c=c, bi=bi: PE.matmul(bf(bi)[:, (c % 4) * 128:(c % 4 + 1) * 128], lhsT=wpqb[:, k, c * 128:(c + 1) * 128], rhs=hT[:, k, :], start=(k == 0), stop=(k == 7)), reads=[wpqb, hT], writes=[banks[bi]])
                for g4 in range(4):
                    mk.op("act", lambda g4=g4: A.copy(out=pqT[:, g4 * 4:(g4 + 1) * 4, :], in_=bf(1 + g4).rearrange("p (a b) -> p a b", a=4)), reads=[banks[1 + g4]], writes=[pqT])
                for c in range(16):
                    bi = (5, 0, 1, 2)[c // 4]
                    mk.op("pe", lambda c=c, bi=bi: PE.matmul(bf(bi)[:, (c % 4) * 128:(c % 4 + 1) * 128], lhsT=pqT[:, c, :], rhs=skT[:, c % 2, :], start=True, stop=True), reads=[pqT, skT], writes=[banks[bi]])
                    if c % 4 == 3:
                        g4 = c // 4
                        mk.op("dve", lambda g4=g4, bi=bi: V.tensor_copy(out=SCs[:, g4 * 4:(g4 + 1) * 4, :], in_=bf(bi).rearrange("p (a b) -> p a b", a=4)), reads=[banks[bi]], writes=[SCs])
                for c in range(16):
                    top16(SCs[:, c, :], SCs, SC2[:, c, :], SC2, V8[:, c, :], I8[:, c, :], (V8, I8))
                V8v = V8[:, :, :].rearrange("p (h s) a -> p h s a", s=2)
                mk.op("dve", lambda: V.tensor_tensor(out=cand[:, :, :].rearrange("p h (a b) -> p h a b", a=16), in0=V8v[:, :, 0, :].unsqueeze(3).to_broadcast([128, 8, 16, 16]), in1=V8v[:, :, 1, :].unsqueeze(2).to_broadcast([128, 8, 16, 16]), op=ALU.add), reads=[V8], writes=[cand])
                for h in range(8):
                    top16(cand[:, h, :], cand, cand2[:, h, :], cand2, T16[:, h, :], P16[:, h, :], (T16, P16))
                mk.op("dve", lambda: V.tensor_single_scalar(out=Pa[:, :, :], in_=P16[:, :, :], scalar=4, op=ALU.logical_shift_right), reads=[P16], writes=[Pa])
                mk.op("dve", lambda: V.tensor_single_scalar(out=Pb[:, :, :], in_=P16[:, :, :], scalar=15, op=ALU.bitwise_and), reads=[P16], writes=[Pb])
                mk.op("dve", lambda: V.tensor_copy(out=Af[:, :, :], in_=Pa[:, :, :]), reads=[Pa], writes=[Af])
                mk.op("dve", lambda: V.tensor_copy(out=Bf[:, :, :], in_=Pb[:, :, :]), reads=[Pb], writes=[Bf])
                mk.op("dve", lambda: V.tensor_copy(out=I8f[:, :, :], in_=I8[:, :, :]), reads=[I8], writes=[I8f])
                I8v = I8f[:, :, :].rearrange("p (h s) a -> p h s a", s=2)
                for (sel, which, dst) in ((Af, 0, i1s), (Bf, 1, i2s)):
                    mk.op("dve", lambda sel=sel: V.tensor_tensor(out=OH[:, :, :, :], in0=sel[:, :, :].unsqueeze(3).to_broadcast([128, 8, 16, 16]), in1=IO16[:, :].unsqueeze(1).unsqueeze(1).to_broadcast([128, 8, 16, 16]), op=ALU.is_equal), reads=[sel, IO16], writes=[OH])
                    mk.op("dve", lambda which=which: V.tensor_tensor(out=OH[:, :, :, :], in0=OH[:, :, :, :], in1=I8v[:, :, which, :].unsqueeze(2).to_broadcast([128, 8, 16, 16]), op=ALU.mult), reads=[OH, I8f], writes=[OH])
                    mk.op("dve", lambda dst=dst: V.tensor_reduce(out=dst[:, :], in_=OH[:, :, :, :].rearrange("p h r a -> p (h r) a"), axis=AX.X, op=ALU.add), reads=[OH], writes=[dst])
                mk.op("dve", lambda: V.scalar_tensor_tensor(out=i1s[:, :], in0=i1s[:, :], scalar=128.0, in1=i2s[:, :], op0=ALU.mult, op1=ALU.add), reads=[i1s, i2s], writes=[i1s])
                mk.op("dve", lambda: V.tensor_copy(out=EID[:, :], in_=i1s[:, :]), reads=[i1s], writes=[EID])
                mk.op("dve", lambda: V.tensor_tensor(out=GT[:, :, :], in0=T16[:, :, :], in1=T16[:, :, 0:1].to_broadcast([128, 8, 16]), op=ALU.subtract), reads=[T16], writes=[GT])
                mk.op("act", lambda: A.activation(out=GT[:, :, :], in_=GT[:, :, :], func=AF.Exp), reads=[GT], writes=[GT])
                mk.op("dve", lambda: V.tensor_reduce(out=gz[:, 0:8], in_=GT[:, :, :], axis=AX.X, op=ALU.add), reads=[GT], writes=[gz])
                mk.op("dve", lambda: V.reciprocal(out=gz[:, 8:16], in_=gz[:, 0:8]), reads=[gz], writes=[gz])
                mk.op("dve", lambda: V.tensor_tensor(out=GT[:, :, :], in0=GT[:, :, :], in1=gz[:, 8:16].unsqueeze(2).to_broadcast([128, 8, 16]), op=ALU.mult), reads=[GT, gz], writes=[GT])

            def stage_G(i, pend_ops):
                per_grp = (len(pend_ops) + 27) // 28
                h2 = H2L[i % 2]; EID = EIDL[i % 2]; GT = GTL[i % 2]
                GTf = GT[:, :, :].rearrange("p h r -> p (h r)")
                for g in range(128 // GS):
                    bufs = [UG[(g % NGB) * GS + j] for j in range(GS)]
                    dg = DG[g % NGB]
                    for j in range(GS):
                        s_ = g * GS + j
                        ug = bufs[j]
                        mk.dma("pool", lambda s_=s_, ug=ug: G.indirect_dma_start(out=ug[:, :], out_offset=None, in_=edu[:, :], in_offset=bass.IndirectOffsetOnAxis(ap=EID[:, s_:s_ + 1], axis=0)), ug, reads=[EID, EDU], writes=[ug])
                        mk.op("dve", lambda s_=s_, ug=ug: V.scalar_tensor_tensor(out=junk[:, :], in0=ug[:, 0:D], scalar=1.0, in1=h2[:, :], op0=ALU.mult, op1=ALU.mult, accum_out=dots[:, s_:s_ + 1]), reads=[ug, h2], writes=[junk, dots])
                    mk.op("dve", lambda: V.memset(dm2[:, 0:1], 0.0), writes=[dm2, dots])
                    mk.op("act", lambda g=g: A.activation(out=gl[:, g * GS:(g + 1) * GS], in_=dots[:, g * GS:(g + 1) * GS], func=AF.Gelu), reads=[dots], writes=[gl])
                    mk.op("dve", lambda g=g: V.tensor_tensor(out=coef[:, g * GS:(g + 1) * GS], in0=gl[:, g * GS:(g + 1) * GS], in1=GTf[:, g * GS:(g + 1) * GS], op=ALU.mult), reads=[gl, GT], writes=[coef])
                    mk.op("dve", lambda g=g, dg=dg: V.tensor_tensor(out=dg[:, :, :], in0=ident[:, :].unsqueeze(1).to_broadcast([128, GS, 128]), in1=coef[:, g * GS:(g + 1) * GS].unsqueeze(2).to_broadcast([128, GS, 128]), op=ALU.mult), reads=[ident, coef], writes=[dg])
                    for j in range(GS):
                        s_ = g * GS + j
                        ug = bufs[j]
                        for hf in range(2):
                            mk.op("pe", lambda s_=s_, ug=ug, dg=dg, j=j, hf=hf: PE.matmul(bf(6 + hf)[:, :], lhsT=dg[:, j, :], rhs=ug[:, D + hf * 512:D + (hf + 1) * 512], start=(s_ == 0), stop=(s_ == 127)), reads=[dg, ug], writes=[banks[6 + hf]])
                    mk.replay(pend_ops, per_grp)
                mk.replay(pend_ops, len(pend_ops))
                for hf in range(2):
                    mk.op("dve", lambda hf=hf: V.scalar_tensor_tensor(out=rr3[:, hf * 512:(hf + 1) * 512], in0=h2[:, hf * 512:(hf + 1) * 512], scalar=ALPHA, in1=bf(6 + hf)[:, :], op0=ALU.mult, op1=ALU.add), reads=[h2, banks[6 + hf]], writes=[rr3])
                layer_norm(rr3, g3, b3, yo, lnw2)
                mk.dma("sp", lambda i=i: SP.dma_start(out=y[i * 128:(i + 1) * 128, :], in_=yo[:, :]), yo, reads=[yo])

            stage_P(0)
            for i in range(NT):
                pend_ops = []
                if i + 1 < NT:
                    mk.defer = pend_ops
                    stage_P(i + 1)
                    mk.defer = None
                stage_G(i, pend_ops)
            mk.barrier()
            mk.flush()
        if "B" not in phases:
            with contextlib.ExitStack() as sd:
                tmp = mk.sb("dbgt", [128, D], F32, st=sd)
                for i in range(NT):
                    mk.dma("sp", lambda i=i: SP.dma_start(out=tmp[:, :], in_=h1d[i * 128:(i + 1) * 128, :]), tmp, reads=[H1D], writes=[tmp])
                    mk.dma("sp", lambda i=i: SP.dma_start(out=y[i * 128:(i + 1) * 128, :], in_=tmp[:, :]), tmp, reads=[tmp])
                mk.barrier()
                mk.flush()
        nw = mk.flush(final=True)
    return nc


_NC_CACHE = {}


def _in_map(inp, b, S):
    NT = S // 128
    f = lambda a: np.ascontiguousarray(np.asarray(a, dtype=np.float32))
    return {
        "x": f(inp["x"][b]), "pos": np.ascontiguousarray(np.asarray(inp["positions"][b]).astype(np.int32).reshape(NT, 128).T),
        "mem": f(inp["mem"][b]), "w_in": f(inp["w_in"][0]), "gate_up": f(inp["gla_gate_up"][0]),
        "gate_bias": f(inp["gla_gate_bias"][0]).reshape(1, -1), "norm_g": f(inp["gla_norm_g"][0]).reshape(1, -1),
        "w_out": f(inp["w_out"][0]), "ln1g": f(inp["ln_mix_g"][0]).reshape(1, -1), "ln1b": f(inp["ln_mix_b"][0]).reshape(1, -1),
        "wq": f(inp["xattn_w_q"][0]), "wk": f(inp["xattn_w_k"][0]), "wv": f(inp["xattn_w_v"][0]), "wo": f(inp["xattn_w_o"][0]),
        "ln2g": f(inp["ln_mem_g"][0]).reshape(1, -1), "ln2b": f(inp["ln_mem_b"][0]).reshape(1, -1),
        "wpq": f(inp["peer_w_query"][0]), "sk1": f(inp["peer_sub_keys_1"][0]), "sk2": f(inp["peer_sub_keys_2"][0]),
        "edown": f(inp["peer_expert_down"][0]), "eup": f(inp["peer_expert_up"][0]),
        "ln3g": f(inp["ln_ffn_g"][0]).reshape(1, -1), "ln3b": f(inp["ln_ffn_b"][0]).reshape(1, -1),
    }


def run(inp, phases="AB", cores=None):
    B, S, _ = inp["x"].shape
    n_sel = min(256, S // 4)
    key = (S, n_sel, phases)
    if key not in _NC_CACHE:
        _NC_CACHE[key] = build(S, n_sel, phases)
    nc = _NC_CACHE[key]
    cores = list(range(B)) if cores is None else cores
    in_maps = [_in_map(inp, b, S) for b in cores]
    res = run_bass_kernel_spmd(nc, in_maps, core_ids=list(range(len(cores))))
    return np.stack([np.asarray(r["y"], dtype=np.float32) for r in res.results], axis=0)


def kernel(**inputs):
    return run(inputs, "AB")
```

```python
import contextlib
import numpy as np
import concourse.bass as bass
import concourse.mybir as mybir

F32 = mybir.dt.float32
BF16 = mybir.dt.bfloat16
I32 = mybir.dt.int32
U32 = mybir.dt.uint32
U16 = mybir.dt.uint16
ALU = mybir.AluOpType
AF = mybir.ActivationFunctionType
AX = mybir.AxisListType

from concourse.bass_utils import run_bass_kernel_spmd

ENGS = ("pe", "act", "dve", "pool", "sp")


class T:
    __slots__ = ("h", "name", "w", "r", "dl", "dlr", "sem", "dcnt")

    def __init__(self, h, name):
        self.h = h
        self.name = name
        self.w = None
        self.r = {}
        self.dl = {}
        self.dlr = {}
        self.sem = None
        self.dcnt = 0

    def __getitem__(self, k):
        return self.h[k]


class TV:
    def __init__(self, t, c0, c1):
        object.__setattr__(self, "_t", t); object.__setattr__(self, "_c0", c0); object.__setattr__(self, "_c1", c1)

    def __getattr__(self, k):
        return getattr(object.__getattribute__(self, "_t"), k)

    def __setattr__(self, k, v):
        setattr(object.__getattribute__(self, "_t"), k, v)

    def __getitem__(self, key):
        p, c = key
        c0 = object.__getattribute__(self, "_c0"); c1 = object.__getattribute__(self, "_c1")
        a = c0 + (c.start or 0); b = c0 + (c.stop if c.stop is not None else (c1 - c0))
        return object.__getattribute__(self, "_t").h[p, a:b]


class AV(TV):
    def __init__(self, t, fn):
        object.__setattr__(self, "_t", t); object.__setattr__(self, "_fn", fn)

    def __getitem__(self, key):
        return object.__getattribute__(self, "_fn")()[key]


class Op:
    __slots__ = ("eng", "fn", "waits", "dwaits", "idx", "need_inc", "dma_tile", "dma_val")

    def __init__(self, eng, fn):
        self.eng = eng
        self.fn = fn
        self.waits = []
        self.dwaits = []
        self.idx = None
        self.need_inc = False
        self.dma_tile = None
        self.dma_val = 0


class MK:
    def __init__(self, nc, st):
        self.nc = nc
        self.st = st
        self.ops = []
        self.cnt = {e: 0 for e in ENGS}
        self.eng_ops = {e: [] for e in ENGS}
        self.seen = {e: {o: -1 for o in ENGS} for e in ENGS}
        self.dseen = {e: {} for e in ENGS}
        self.tiles = []
        self.pend = {e: None for e in ENGS}
        self.defer = None

    def sb(self, name, shape, dt, st=None):
        h = (st or self.st).enter_context(self.nc.sbuf_tensor(name, list(shape), dt))
        t = T(h, name)
        self.tiles.append(t)
        return t

    def ps(self, name, shape, dt, st=None):
        h = (st or self.st).enter_context(self.nc.psum_tensor(name, list(shape), dt))
        t = T(h, name)
        self.tiles.append(t)
        return t

    def alias(self, name, h):
        t = T(h, name)
        self.tiles.append(t)
        return t

    def _dep(self, op, eng, tgt):
        if tgt is None:
            return
        te, ti = tgt
        if te == eng:
            return
        if self.seen[eng][te] >= ti:
            return
        op.waits.append((te, ti))
        self.seen[eng][te] = ti

    def barrier(self):
        for e in ENGS:
            w = [(o, self.cnt[o] - 1) for o in ENGS if o != e and self.cnt[o] > 0]
            d = [(t, t.dcnt) for t in self.tiles if t.sem is not None and t.dcnt > 0]
            self.pend[e] = (w, d)

    def _apply_pend(self, o, eng):
        p = self.pend[eng]
        if p is None:
            return
        self.pend[eng] = None
        for tgt in p[0]:
            self._dep_any(o, eng, tgt)
        for (t, v) in p[1]:
            if self.dseen[eng].get(id(t), 0) < v:
                o.dwaits.append((t, v))
                self.dseen[eng][id(t)] = v

    def replay(self, lst, n):
        assert self.defer is None
        for _ in range(min(n, len(lst))):
            it = lst.pop(0)
            if it[0] == "op":
                self.op(*it[1:])
            else:
                self.dma(*it[1:])

    def op(self, eng, fn, reads=(), writes=()):
        import os
        if self.defer is not None:
            self.defer.append(("op", eng, fn, list(reads), list(writes)))
            return None
        reads = [object.__getattribute__(t, "_t") if isinstance(t, TV) else t for t in reads]
        writes = [object.__getattribute__(t, "_t") if isinstance(t, TV) else t for t in writes]
        if len(self.ops) >= int(os.environ.get("MK_MAXOPS", "100000000")):
            return None
        o = Op(eng, fn)
        self._apply_pend(o, eng)
        for t in reads:
            if t.w is not None:
                if t.w[0] == eng:
                    if eng != "pe" and self.seen[eng][eng] < t.w[1]:
                        o.waits.append(t.w)
                        self.seen[eng][eng] = t.w[1]
                else:
                    self._dep(o, eng, t.w)
            self._ddep(o, eng, t, False)
        for t in writes:
            if t.w is not None and t.w[0] != eng:
                self._dep(o, eng, t.w)
            for re_, ri in t.r.items():
                if re_ != eng:
                    self._dep(o, eng, (re_, ri))
            self._ddep(o, eng, t)
        o.idx = self.cnt[eng]
        self.cnt[eng] += 1
        self.ops.append(o)
        self.eng_ops[eng].append(o)
        for t in reads:
            t.r[eng] = o.idx
        for t in writes:
            t.w = (eng, o.idx)
            t.r = {}
        return o

    def _ddep(self, o, eng, t, write=True):
        for d in ((t.dl, t.dlr) if write else (t.dl,)):
            for k, (stile, v) in d.items():
                if self.dseen[eng].get(k, 0) < v:
                    o.dwaits.append((stile, v))
                    self.dseen[eng][k] = v

    def _dep_any(self, o, eng, tgt):
        te, ti = tgt
        if self.seen[eng][te] >= ti:
            return
        o.waits.append((te, ti))
        self.seen[eng][te] = ti

    def dma(self, eng, fn, semtile, reads=(), writes=()):
        import os
        if self.defer is not None:
            self.defer.append(("dma", eng, fn, semtile, list(reads), list(writes)))
            return None
        if len(self.ops) >= int(os.environ.get("MK_MAXOPS", "100000000")):
            return None
        reads = [object.__getattribute__(t, "_t") if isinstance(t, TV) else t for t in reads]
        writes = [object.__getattribute__(t, "_t") if isinstance(t, TV) else t for t in writes]
        if isinstance(semtile, TV):
            semtile = object.__getattribute__(semtile, "_t")
        o = Op(eng, fn)
        self._apply_pend(o, eng)
        for t in reads:
            if t.w is not None:
                self._dep_any(o, eng, t.w)
            self._ddep(o, eng, t, False)
        for t in writes:
            if t.w is not None:
                self._dep_any(o, eng, t.w)
            for re_, ri in t.r.items():
                self._dep_any(o, eng, (re_, ri))
            self._ddep(o, eng, t)
        if semtile.sem is None:
            semtile.sem = self.st.enter_context(self.nc.semaphore("d_" + semtile.name))
        semtile.dcnt += 16
        o.dma_tile = semtile
        o.dma_val = semtile.dcnt
        self.ops.append(o)
        for t in reads:
            t.dlr[id(semtile)] = (semtile, semtile.dcnt)
        for t in writes:
            t.dl[id(semtile)] = (semtile, semtile.dcnt)
        for t in writes:
            t.w = None
            t.r = {}
        return o

    def flush(self, final=False):
        nc = self.nc
        if not hasattr(self, "esem"):
            self.esem = {e: self.st.enter_context(nc.semaphore("e_" + e)) for e in ENGS}
            self.ordinal = {e: [] for e in ENGS}
            self.eflushed = {e: 0 for e in ENGS}
            self.flushed = 0
            self.n_wait = 0
        E = {"pe": nc.tensor, "act": nc.scalar, "dve": nc.vector, "pool": nc.gpsimd, "sp": nc.sync}
        chunk = self.ops[self.flushed:]
        for e in ENGS:
            if len(self.eng_ops[e]) > self.eflushed[e]:
                self.eng_ops[e][-1].need_inc = True
        for o in chunk:
            for (te, ti) in o.waits:
                if ti >= self.eflushed[te]:
                    self.eng_ops[te][ti].need_inc = True
        for e in ENGS:
            c = self.ordinal[e][-1] if self.ordinal[e] else 0
            for o in self.eng_ops[e][self.eflushed[e]:]:
                if o.need_inc:
                    c += 1
                self.ordinal[e].append(c)
        for o in chunk:
            eng = E[o.eng]
            for (te, ti) in o.waits:
                v = self.ordinal[te][ti] if self.eng_ops[te][ti].need_inc else self.ordinal[te][ti] + 1
                eng.wait_ge(self.esem[te], v)
                self.n_wait += 1
            for (t, v) in o.dwaits:
                eng.wait_ge(t.sem, v)
                self.n_wait += 1
            inst = o.fn()
            if o.dma_tile is not None:
                inst.then_inc(o.dma_tile.sem, 16)
            elif o.need_inc:
                inst.then_inc(self.esem[o.eng], 1)
        self.flushed = len(self.ops)
        for e in ENGS:
            self.eflushed[e] = len(self.eng_ops[e])
        if final:
            for t in self.tiles:
                if t.sem is not None and t.dcnt > 0:
                    nc.sync.wait_ge(t.sem, t.dcnt)
        return self.n_wait

D = 1024
INW = 3384
ALPHA = 2.0 ** 0.25
THETA = 500000.0
NEG = -1.0e30


def build(S, n_sel, phases="AB", dbg=False):
    import os
    STOP = int(os.environ.get("MK_STOP", "99"))
    NT = S // 128
    nc = bass.Bass("TRN2", target_bir_lowering=False)
    V, A, G, PE, SP = nc.vector, nc.scalar, nc.gpsimd, nc.tensor, nc.sync

    def din(name, shape, dt=F32):
        return nc.dram_tensor(name, shape, dt, kind="ExternalInput").ap()

    x = din("x", [S, D]); pos = din("pos", [128, NT], I32); mem = din("mem", [256, D])
    w_in = din("w_in", [D, INW]); gate_up = din("gate_up", [16, 256]); gate_bias = din("gate_bias", [1, 256])
    norm_g = din("norm_g", [1, 128]); w_out = din("w_out", [D, D])
    ln1g = din("ln1g", [1, D]); ln1b = din("ln1b", [1, D])
    wq = din("wq", [D, D]); wk = din("wk", [D, D]); wv = din("wv", [D, D]); wo = din("wo", [D, D])
    ln2g = din("ln2g", [1, D]); ln2b = din("ln2b", [1, D])
    wpq = din("wpq", [D, 2048]); sk1 = din("sk1", [128, 128]); sk2 = din("sk2", [128, 128])
    edown = din("edown", [16384, D]); eup = din("eup", [16384, D])
    ln3g = din("ln3g", [1, D]); ln3b = din("ln3b", [1, D])
    y = nc.dram_tensor("y", [S, D], F32, kind="ExternalOutput").ap()
    h1d = nc.dram_tensor("h1d", [S, D], F32, kind="Internal").ap()
    woutd = nc.dram_tensor("woutd", [D, D], BF16, kind="Internal").ap()
    edu = nc.dram_tensor("edu", [16384, 2 * D], BF16, kind="Internal").ap()

    with contextlib.ExitStack() as st:
        mk = MK(nc, st)
        H1D = mk.alias("h1dT", None)
        EDU = mk.alias("eduT", None)
        WOD = mk.alias("woutdT", None)
        conv_jobs = [(c, src, off) for c in range(16) for (src, off) in ((edown, 0), (eup, D))]

        def conv_issue(n):
            for _ in range(n):
                if conv_jobs:
                    c, src, off = conv_jobs.pop(0)
                    mk.dma("pool", lambda c=c, src=src, off=off: G.dma_start(out=edu[c * 1024:(c + 1) * 1024, off:off + D].rearrange("(p a) d -> p a d", p=128), in_=src[c * 1024:(c + 1) * 1024, :].rearrange("(p a) d -> p a d", p=128), max_dma_last_dim=4096), EDU, writes=[EDU])
        banks = [mk.ps("bank%d" % i, [128, 512], F32) for i in range(8)]

        def bf(i):
            return banks[i][:, :]

        def bb(i):
            return banks[i][:, :].bitcast(BF16)

        identf = mk.sb("identf", [128, 128], F32)
        ident = mk.sb("ident", [128, 128], BF16)
        mk.op("pool", lambda: G.memset(identf[:, :], 1.0), writes=[identf])
        mk.op("pool", lambda: G.affine_select(out=identf[:, :], in_=identf[:, :], pattern=[[-1, 128]], compare_op=ALU.is_equal, fill=0.0, base=0, channel_multiplier=1), reads=[identf], writes=[identf])
        mk.op("dve", lambda: V.tensor_copy(out=ident[:, :], in_=identf[:, :]), reads=[identf], writes=[ident])
        negI = mk.sb("negI", [128, 4, 128], BF16)
        mk.op("dve", lambda: V.tensor_scalar(out=negI[:, :, :], in0=identf[:, :].unsqueeze(1).to_broadcast([128, 4, 128]), scalar1=-30000.0, scalar2=None, op0=ALU.mult), reads=[identf], writes=[negI])

        def ln_consts(gd, bd, stk, nm, dt=F32):
            g_t = mk.sb(nm + "gs", [128, D], dt, st=stk)
            b_t = mk.sb(nm + "bs", [128, D], dt, st=stk)
            if dt == F32:
                mk.dma("sp", lambda: SP.dma_start(out=g_t[:, :], in_=gd.broadcast_to([128, D])), g_t, writes=[g_t])
                mk.dma("sp", lambda: SP.dma_start(out=b_t[:, :], in_=bd.broadcast_to([128, D])), b_t, writes=[b_t])
            else:
                mk.dma("pool", lambda: G.dma_start(out=g_t[:, :], in_=gd.broadcast_to([128, D])), g_t, writes=[g_t])
                mk.dma("pool", lambda: G.dma_start(out=b_t[:, :], in_=bd.broadcast_to([128, D])), b_t, writes=[b_t])
            return g_t, b_t

        def layer_norm(r_t, g_t, b_t, out_t, wk_t):
            mk.op("dve", lambda: V.bn_stats(out=wk_t[:, 0:6], in_=r_t[:, 0:512]), reads=[r_t], writes=[wk_t])
            mk.op("dve", lambda: V.bn_stats(out=wk_t[:, 6:12], in_=r_t[:, 512:1024]), reads=[r_t], writes=[wk_t])
            mk.op("dve", lambda: V.bn_aggr(out=wk_t[:, 12:14], in_=wk_t[:, 0:12]), reads=[wk_t], writes=[wk_t])
            mk.op("dve", lambda: V.tensor_scalar_add(out=wk_t[:, 14:15], in0=wk_t[:, 13:14], scalar1=1e-5), reads=[wk_t], writes=[wk_t])
            mk.op("act", lambda: A.sqrt(out=wk_t[:, 15:16], in_=wk_t[:, 14:15]), reads=[wk_t], writes=[wk_t])
            mk.op("dve", lambda: V.reciprocal(out=wk_t[:, 16:17], in_=wk_t[:, 15:16]), reads=[wk_t], writes=[wk_t])
            mk.op("dve", lambda: V.tensor_scalar(out=r_t[:, :], in0=r_t[:, :], scalar1=wk_t[:, 12:13], scalar2=wk_t[:, 16:17], op0=ALU.subtract, op1=ALU.mult), reads=[r_t, wk_t], writes=[r_t])
            mk.op("pool", lambda: G.tensor_tensor(out=r_t[:, :], in0=r_t[:, :], in1=g_t[:, :], op=ALU.mult), reads=[r_t, g_t], writes=[r_t])
            mk.op("pool", lambda: G.tensor_tensor(out=out_t[:, :], in0=r_t[:, :], in1=b_t[:, :], op=ALU.add), reads=[r_t, b_t], writes=[out_t])

        def transpose8(src_bf, dst_fn, bank_i, nblk=8):
            bk = banks[bank_i]
            for k in range(nblk):
                mk.op("pe", lambda k=k: PE.transpose(out=bb(bank_i)[:, k * 128:(k + 1) * 128], in_=src_bf[:, k * 128:(k + 1) * 128], identity=ident[:, :]), reads=[src_bf, ident], writes=[bk])
            dst_fn(bk)

        if "A" in phases:
          with contextlib.ExitStack() as sa:
            def sb(name, shape, dt):
                return mk.sb(name, shape, dt, st=sa)
            winb = sb("winb", [128, 8, INW], BF16)
            WQ = [sb("WQ%d" % q, [128, 8, 256], BF16) for q in range(2)]
            for k in range(8):
                mk.dma("pool", lambda k=k: G.dma_start(out=woutd[k * 128:(k + 1) * 128, :], in_=w_out[k * 128:(k + 1) * 128, :], max_dma_last_dim=4096), WOD, writes=[WOD])

            def wq_load(q):
                wt = WQ[q % 2]
                mk.dma("sp", lambda q=q, wt=wt: SP.dma_start(out=wt[:, :, :], in_=woutd[:, q * 256:(q + 1) * 256].rearrange("(k p) n -> p k n", p=128)), wt, reads=[WOD], writes=[wt])
            for k in range(8):
                mk.dma("pool", lambda k=k: G.dma_start(out=winb[:, k, :], in_=w_in[k * 128:(k + 1) * 128, :], max_dma_last_dim=4096), winb, writes=[winb])
            g1, b1 = ln_consts(ln1g, ln1b, sa, "ln1", BF16)
            kTc = sb("kTc", [128, 4, S], BF16)
            Vc = sb("Vc", [128, NT, 8, 66], BF16)
            kidxT = sb("kidxT", [128, S], BF16)
            mk.op("pool", lambda: G.memset(Vc[:, :, :, :], 1.0), writes=[Vc])
            CB = sb("CB", [128, 128], F32)
            mk.op("pool", lambda: G.memset(CB[:, :], 0.0), writes=[CB])
            mk.op("pool", lambda: G.affine_select(out=CB[:, :], in_=CB[:, :], pattern=[[-1, 128]], compare_op=ALU.is_ge, fill=NEG, base=0, channel_multiplier=1), reads=[CB], writes=[CB])
            TRI = sb("TRI", [128, 128], F32)
            mk.op("pool", lambda: G.memset(TRI[:, :], -1.0 / 16), writes=[TRI])
            mk.op("pool", lambda: G.affine_select(out=TRI[:, :], in_=TRI[:, :], pattern=[[1, 128]], compare_op=ALU.is_ge, fill=0.0, base=0, channel_multiplier=-1), reads=[TRI], writes=[TRI])
            TRU = sb("TRU", [128, 128], F32)
            mk.op("pool", lambda: G.memset(TRU[:, :], -1.0 / 16), writes=[TRU])
            mk.op("pool", lambda: G.affine_select(out=TRU[:, :], in_=TRU[:, :], pattern=[[-1, 128]], compare_op=ALU.is_gt, fill=0.0, base=0, channel_multiplier=1), reads=[TRU], writes=[TRU])
            POW2 = sb("POW2", [128, 16], F32)
            for j in range(16):
                mk.op("pool", lambda j=j: G.memset(POW2[:, j:j + 1], 2.0 ** -(j + 1)), writes=[POW2])
            F12 = sb("F12", [128, 12], F32)
            for j in range(12):
                f = THETA ** (-(j / 8.0)) if j < 8 else THETA ** (-((j - 8) / 4.0))
                mk.op("pool", lambda j=j, f=f: G.memset(F12[:, j:j + 1], f / (2 * np.pi)), writes=[F12])
            posi = sb("posi", [128, NT], I32)
            posf = sb("posf", [128, NT], F32)
            PROJ = sb("PROJ", [128, INW], F32)
            class _V3:
                def __init__(self, c0, dt=None):
                    self.c0 = c0; self.dt = dt
                def __getitem__(self, key):
                    a = PROJ[:, self.c0:self.c0 + NT * 24]
                    if self.dt is not None:
                        a = a.bitcast(self.dt)
                    return a.rearrange("p (t f) -> p t f", t=NT)[key]
            SCt = _V3(0); SCn = _V3(NT * 24, I32); SCf = _V3(2 * NT * 24); SCm = _V3(3 * NT * 24)
            SC = sb("SC", [128, NT, 24], F32)
            mk.dma("sp", lambda: SP.dma_start(out=posi[:, :], in_=pos[:, :]), posi, writes=[posi])
            mk.op("dve", lambda: V.tensor_copy(out=posf[:, :], in_=posi[:, :]), reads=[posi], writes=[posf])
            mk.op("dve", lambda: V.tensor_tensor(out=SCt[:, :, 0:12], in0=posf[:, :].unsqueeze(2).to_broadcast([128, NT, 12]), in1=F12[:, :].unsqueeze(1).to_broadcast([128, NT, 12]), op=ALU.mult), reads=[posf, F12], writes=[PROJ])
            mk.op("dve", lambda: V.tensor_scalar_add(out=SCt[:, :, 12:24], in0=SCt[:, :, 0:12], scalar1=0.25), reads=[PROJ], writes=[PROJ])
            mk.op("dve", lambda: V.tensor_copy(out=SCn[:, :, :], in_=SCt[:, :, :]), reads=[PROJ], writes=[PROJ])
            mk.op("dve", lambda: V.tensor_copy(out=SCf[:, :, :], in_=SCn[:, :, :]), reads=[PROJ], writes=[PROJ])
            mk.op("dve", lambda: V.tensor_tensor(out=SCt[:, :, :], in0=SCt[:, :, :], in1=SCf[:, :, :], op=ALU.subtract), reads=[PROJ], writes=[PROJ])
            mk.op("dve", lambda: V.tensor_single_scalar(out=SCm[:, :, :], in_=SCt[:, :, :], scalar=0.5, op=ALU.is_gt), reads=[PROJ], writes=[PROJ])
            mk.op("dve", lambda: V.tensor_tensor(out=SCt[:, :, :], in0=SCt[:, :, :], in1=SCm[:, :, :], op=ALU.subtract), reads=[PROJ], writes=[PROJ])
            mk.op("dve", lambda: V.tensor_single_scalar(out=SCm[:, :, :], in_=SCt[:, :, :], scalar=-0.5, op=ALU.is_lt), reads=[PROJ], writes=[PROJ])
            mk.op("dve", lambda: V.tensor_tensor(out=SCt[:, :, :], in0=SCt[:, :, :], in1=SCm[:, :, :], op=ALU.add), reads=[PROJ], writes=[PROJ])
            mk.op("act", lambda: A.activation(out=SC[:, :, :], in_=SCt[:, :, :], func=AF.Sin, scale=2 * np.pi * (1 - 2e-6)), reads=[PROJ], writes=[SC])
            GU = sb("GU", [17, 256], F32)
            mk.dma("sp", lambda: SP.dma_start(out=GU[0:16, :], in_=gate_up[:, :]), GU, writes=[GU])
            mk.dma("sp", lambda: SP.dma_start(out=GU[16:17, :], in_=gate_bias[:, :]), GU, writes=[GU])
            NG = sb("NG", [128, 128], F32)
            mk.dma("sp", lambda: SP.dma_start(out=NG[:, :], in_=norm_g.broadcast_to([128, 128])), NG, writes=[NG])
            glrT = sb("glrT", [17, 128], F32)
            mk.op("pool", lambda: G.memset(glrT[:, :], 1.0), writes=[glrT])
            Sst = sb("Sst", [64, 4, 128], F32)
            Sb = sb("Sb", [64, 4, 128], BF16)
            mk.op("pool", lambda: G.memset(Sst[:, :, :], 0.0), writes=[Sst])
            mk.op("pool", lambda: G.memset(Sb[:, :, :], 0.0), writes=[Sb])

            xs = sb("xs", [128, D], F32)
            xb = sb("xb", [128, D], BF16)
            xT = sb("xT", [128, 8, 128], BF16)
            GQK = sb("GQK", [64, 8, 128], BF16)
            QK16 = xb
            IDX16 = TV(xb, 0, 384)
            qT = sb("qT", [128, 4, 128], BF16)
            qidxT = sb("qidxT", [128, 2, 128], BF16)
            wst = sb("wst", [128, 16], F32)
            Ssc = sb("Ssc", [128, max(S, 3200)], F32)
            MB = sb("MB", [128, S], BF16)
            rtmp = [sb("rtmp%d" % i, [128, 512], F32) for i in range(2)]
            t1, t2, t3, t4 = [AV(rtmp[1], (lambda q=q: rtmp[1][:, q * 128:(q + 1) * 128].rearrange("p (a b) -> p a b", a=16))) for q in range(4)]
            sq = rtmp[0]; sil = rtmp[1]
            Lg = AV(Ssc, lambda: Ssc[:, 0:256])
            E1 = AV(Ssc, lambda: Ssc[0:64, 256:768].rearrange("p (a b) -> p a b", a=4))
            E2 = AV(Ssc, lambda: Ssc[0:64, 768:1280].rearrange("p (a b) -> p a b", a=4))
            Ee = AV(Ssc, lambda: Ssc[:, 1280:1536])
            og = AV(Ssc, lambda: Ssc[:, 1536:2048].rearrange("p (a b) -> p a b", a=4))
            qdT = AV(Ssc, lambda: Ssc[0:64, 2048:2304].bitcast(BF16).rearrange("p (a b) -> p a b", a=4))
            kiT = AV(Ssc, lambda: Ssc[0:64, 2304:2560].bitcast(BF16).rearrange("p (a b) -> p a b", a=4))
            kte = AV(Ssc, lambda: Ssc[:, 2560:2688].bitcast(BF16))
            gvb = AV(Ssc, lambda: Ssc[:, 2688:2944].bitcast(BF16))
            ATb = AV(Ssc, lambda: Ssc[:, 2944:3200].bitcast(BF16).rearrange("p (a b) -> p a b", a=4))
            tk = sb("tk", [128, 64], F32)
            dmy = sb("dmy", [128, 8], F32)
            PT0 = sb("PT0", [128, 8, 128], BF16)
            PT = [PT0, sb("PT1", [128, 8, 128], BF16)]
            Y16 = xb
            yT = xT
            rec8 = sb("rec8", [128, 8], F32)
            gst = sb("gst", [128, 16], F32)
            rr = TV(PROJ, 0, D)
            h1 = rr
            lnw = sb("lnw", [128, 32], F32)

            for i in range(NT):
                nk = (i + 1) * 128
                mk.dma("sp", lambda i=i: SP.dma_start(out=xs[:, :], in_=x[i * 128:(i + 1) * 128, :]), xs, writes=[xs])
                wq_load(0); wq_load(1)
                mk.op("act", lambda: A.copy(out=xb[:, :], in_=xs[:, :]), reads=[xs], writes=[xb])
                transpose8(xb, lambda bk: mk.op("dve", lambda: V.tensor_copy(out=xT[:, :, :], in_=bb(0).rearrange("p (a b) -> p a b", a=8)), reads=[bk], writes=[xT]), 0)
                if STOP < 1:
                    continue
                c0 = 0
                ci = 0
                while c0 < INW:
                    cw = min(512, INW - c0)
                    bi = 1 + (ci % 2)
                    for k in range(8):
                        mk.op("pe", lambda k=k, c0=c0, cw=cw, bi=bi: PE.matmul(bf(bi)[:, 0:cw], lhsT=xT[:, k, :], rhs=winb[:, k, c0:c0 + cw], start=(k == 0), stop=(k == 7)), reads=[xT, winb], writes=[banks[bi]])
                    if ci % 2 == 0:
                        mk.op("act", lambda c0=c0, cw=cw, bi=bi: A.copy(out=PROJ[:, c0:c0 + cw], in_=bf(bi)[:, 0:cw]), reads=[banks[bi]], writes=[PROJ])
                    else:
                        mk.op("dve", lambda c0=c0, cw=cw, bi=bi: V.tensor_copy(out=PROJ[:, c0:c0 + cw], in_=bf(bi)[:, 0:cw]), reads=[banks[bi]], writes=[PROJ])
                    c0 += cw
                    ci += 1
                for j in range(8):
                    col = (1832 if j < 4 else 2088) + (j % 4) * 64
                    bi = 3 + (j // 4)
                    for k in range(8):
                        mk.op("pe", lambda k=k, col=col, bi=bi, j=j: PE.matmul(bf(bi)[0:64, (j % 4) * 128:(j % 4 + 1) * 128], lhsT=winb[:, k, col:col + 64], rhs=xT[:, k, :], start=(k == 0), stop=(k == 7)), reads=[xT, winb], writes=[banks[bi]])
                mk.op("act", lambda: A.copy(out=GQK[:, 0:4, :], in_=bf(3)[0:64, :].rearrange("p (a b) -> p a b", a=4)), reads=[banks[3]], writes=[GQK])
                mk.op("act", lambda: A.copy(out=GQK[:, 4:8, :], in_=bf(4)[0:64, :].rearrange("p (a b) -> p a b", a=4)), reads=[banks[4]], writes=[GQK])

                if STOP < 2:
                    continue
                def rot(base, nh, hd, half, f0, i=i):
                    v3 = PROJ[:, base:base + nh * hd].rearrange("p (h d) -> p h d", h=nh)
                    x1 = v3[:, :, 0:half]; x2 = v3[:, :, half:2 * half]
                    sn = SC[:, i, f0:f0 + half].unsqueeze(1).to_broadcast([128, nh, half])
                    cs = SC[:, i, 12 + f0:12 + f0 + half].unsqueeze(1).to_broadcast([128, nh, half])
                    a1 = t1[:, 0:nh, 0:half]; a2 = t2[:, 0:nh, 0:half]; a3 = t3[:, 0:nh, 0:half]; a4 = t4[:, 0:nh, 0:half]
                    mk.op("dve", lambda: V.tensor_tensor(out=a1, in0=x1, in1=cs, op=ALU.mult), reads=[PROJ, SC], writes=[t1])
                    mk.op("dve", lambda: V.tensor_tensor(out=a2, in0=x2, in1=sn, op=ALU.mult), reads=[PROJ, SC], writes=[t2])
                    mk.op("dve", lambda: V.tensor_tensor(out=a3, in0=x2, in1=cs, op=ALU.mult), reads=[PROJ, SC], writes=[t3])
                    mk.op("dve", lambda: V.tensor_tensor(out=a4, in0=x1, in1=sn, op=ALU.mult), reads=[PROJ, SC], writes=[t4])
                    mk.op("dve", lambda: V.tensor_tensor(out=x1, in0=a1, in1=a2, op=ALU.subtract), reads=[t1, t2], writes=[PROJ])
                    mk.op("dve", lambda: V.tensor_tensor(out=x2, in0=a3, in1=a4, op=ALU.add), reads=[t3, t4], writes=[PROJ])
                rot(0, 16, 64, 8, 0)
                rot(1536, 9, 32, 4, 8)

                if i == 0:
                    print("OPS before stage3", len(mk.ops))
                if STOP < 3:
                    continue
                mk.op("act", lambda: A.copy(out=QK16[:, :], in_=PROJ[:, 0:1024]), reads=[PROJ], writes=[QK16])
                def ev_qk(bk, i=i):
                    mk.op("dve", lambda: V.tensor_copy(out=qT[:, :, :], in_=bb(0)[:, 0:512].rearrange("p (a b) -> p a b", a=4)), reads=[bk], writes=[qT])
                    mk.op("dve", lambda: V.tensor_copy(out=kTc[:, :, i * 128:(i + 1) * 128], in_=bb(0)[:, 512:1024].rearrange("p (a b) -> p a b", a=4)), reads=[bk], writes=[kTc])
                transpose8(QK16, ev_qk, 0)
                mk.op("act", lambda i=i: A.copy(out=Vc[:, i, :, 0:64], in_=PROJ[:, 1024:1536].rearrange("p (h d) -> p h d", h=8)), reads=[PROJ], writes=[Vc])
                mk.op("dve", lambda: V.tensor_copy(out=IDX16[:, 0:256], in_=PROJ[:, 1536:1792]), reads=[PROJ], writes=[IDX16])
                mk.op("dve", lambda: V.tensor_copy(out=IDX16[:, 256:384].rearrange("p (r d) -> p r d", r=4), in_=PROJ[:, 1792:1824].unsqueeze(1).to_broadcast([128, 4, 32])), reads=[PROJ], writes=[IDX16])
                def ev_idx(bk, i=i):
                    mk.op("dve", lambda: V.tensor_copy(out=qidxT[:, :, :], in_=bb(0)[:, 0:256].rearrange("p (a b) -> p a b", a=2)), reads=[bk], writes=[qidxT])
                    mk.op("dve", lambda: V.tensor_copy(out=kidxT[:, i * 128:(i + 1) * 128], in_=bb(0)[:, 256:384]), reads=[bk], writes=[kidxT])
                transpose8(IDX16, ev_idx, 0, nblk=3)
                mk.op("dve", lambda: V.tensor_scalar(out=wst[:, 8:16], in0=PROJ[:, 1824:1832], scalar1=0.0, scalar2=2.0, op0=ALU.is_gt, op1=ALU.mult), reads=[PROJ], writes=[wst])
                mk.op("dve", lambda: V.tensor_scalar_add(out=wst[:, 8:16], in0=wst[:, 8:16], scalar1=-1.0), reads=[wst], writes=[wst])
                mk.op("dve", lambda: V.tensor_tensor(out=wst[:, 0:8], in0=PROJ[:, 1824:1832], in1=wst[:, 8:16], op=ALU.mult), reads=[PROJ, wst], writes=[wst])

                if i == 0:
                    print("OPS before stage4", len(mk.ops))
                if STOP < 4:
                    continue
                nblk = (nk + 511) // 512
                for b in range(nblk):
                    k0 = b * 512
                    kw = min(512, nk - k0)
                    for h in range(8):
                        bi = 1 + (h % 2)
                        rt = rtmp[h % 2]
                        pb = 32 * (h % 4)
                        mk.op("pe", lambda h=h, bi=bi, pb=pb, k0=k0, kw=kw: PE.matmul(bf(bi)[:, 0:kw], lhsT=qidxT[pb:pb + 32, h // 4, :], rhs=kidxT[pb:pb + 32, k0:k0 + kw], start=True, stop=True, tile_position=(pb, 0)), reads=[qidxT, kidxT], writes=[banks[bi]])
                        mk.op("act", lambda h=h, bi=bi, kw=kw, rt=rt: A.activation(out=rt[:, 0:kw], in_=bf(bi)[:, 0:kw], func=AF.Relu, scale=wst[:, h:h + 1]), reads=[banks[bi], wst], writes=[rt])
                        if h == 0:
                            mk.op("dve", lambda h=h, k0=k0, kw=kw, rt=rt: V.tensor_scalar(out=Ssc[:, k0:k0 + kw], in0=rt[:, 0:kw], scalar1=wst[:, 8 + h:9 + h], scalar2=None, op0=ALU.mult), reads=[rt, wst], writes=[Ssc])
                        else:
                            mk.op("dve", lambda h=h, k0=k0, kw=kw, rt=rt: V.scalar_tensor_tensor(out=Ssc[:, k0:k0 + kw], in0=rt[:, 0:kw], scalar=wst[:, 8 + h:9 + h], in1=Ssc[:, k0:k0 + kw], op0=ALU.mult, op1=ALU.add), reads=[rt, wst, Ssc], writes=[Ssc])
                if STOP < 5:
                    continue
                use_topk = nk > n_sel
                if use_topk:
                    mk.op("dve", lambda nk=nk: V.tensor_reduce(out=tk[:, 0:1], in_=Ssc[:, 0:nk], axis=AX.X, op=ALU.max), reads=[Ssc], writes=[tk])
                    mk.op("dve", lambda nk=nk: V.tensor_reduce(out=tk[:, 1:2], in_=Ssc[:, 0:nk], axis=AX.X, op=ALU.min), reads=[Ssc], writes=[tk])
                mk.op("dve", lambda i=i: V.tensor_tensor(out=Ssc[:, i * 128:(i + 1) * 128], in0=Ssc[:, i * 128:(i + 1) * 128], in1=CB[:, :], op=ALU.add), reads=[Ssc, CB], writes=[Ssc])
                if use_topk:
                    mk.op("dve", lambda: V.tensor_tensor(out=tk[:, 2:3], in0=tk[:, 0:1], in1=tk[:, 1:2], op=ALU.subtract), reads=[tk], writes=[tk])
                    mk.op("dve", lambda: V.tensor_scalar(out=tk[:, 8:24], in0=POW2[:, :], scalar1=tk[:, 2:3], scalar2=None, op0=ALU.mult), reads=[tk, POW2], writes=[tk])
                    mk.op("dve", lambda: V.tensor_tensor(out=tk[:, 3:4], in0=tk[:, 1:2], in1=tk[:, 8:9], op=ALU.add), reads=[tk], writes=[tk])
                    for it in range(16):
                        mk.op("dve", lambda nk=nk: V.tensor_scalar(out=MB[:, 0:nk], in0=Ssc[:, 0:nk], scalar1=tk[:, 3:4], scalar2=None, op0=ALU.is_ge, op1=ALU.add, accum_out=tk[:, 4:5]), reads=[Ssc, tk], writes=[MB, tk])
                        mk.op("dve", lambda: V.memset(dmy[:, 0:1], 0.0), writes=[dmy])
                        mk.op("dve", lambda it=it: V.scalar_tensor_tensor(out=tk[:, 5:6], in0=tk[:, 4:5], scalar=float(n_sel), in1=tk[:, 8 + it:9 + it], op0=ALU.is_ge, op1=ALU.mult), reads=[tk], writes=[tk])
                        if it < 15:
                            mk.op("dve", lambda it=it: V.scalar_tensor_tensor(out=tk[:, 3:4], in0=tk[:, 5:6], scalar=tk[:, 9 + it:10 + it], in1=tk[:, 3:4], op0=ALU.subtract, op1=ALU.add), reads=[tk], writes=[tk])
                        else:
                            mk.op("dve", lambda: V.scalar_tensor_tensor(out=tk[:, 1:2], in0=tk[:, 5:6], scalar=tk[:, 23:24], in1=tk[:, 3:4], op0=ALU.subtract, op1=ALU.add), reads=[tk], writes=[tk])
                    mk.op("dve", lambda nk=nk: V.tensor_scalar(out=MB[:, 0:nk], in0=Ssc[:, 0:nk], scalar1=tk[:, 1:2], scalar2=None, op0=ALU.is_lt), reads=[Ssc, tk], writes=[MB])
                else:
                    mk.op("dve", lambda nk=nk: V.tensor_scalar(out=MB[:, 0:nk], in0=Ssc[:, 0:nk], scalar1=-1.0e29, scalar2=None, op0=ALU.is_lt), reads=[Ssc], writes=[MB])

                if STOP < 6:
                    continue
                for j in range(i + 1):
                    lb = 2 + 2 * (j % 2)
                    pt = PT[j % 2]
                    for h in range(8) if os.environ.get("MK_MERGE", "1") == "0" else []:
                        bi = lb + h // 4
                        pb = 64 * (h % 2)
                        o_ap = (lambda bi=bi, h=h: bf(bi)[:, (h % 4) * 128:(h % 4 + 1) * 128])
                        mk.op("pe", lambda h=h, pb=pb, j=j, o_ap=o_ap: PE.matmul(o_ap(), lhsT=kTc[pb:pb + 64, h // 2, j * 128:(j + 1) * 128], rhs=qT[pb:pb + 64, h // 2, :], start=True, stop=False, tile_position=(pb, 0)), reads=[kTc, qT], writes=[banks[bi]])
                        mk.op("pe", lambda j=j, o_ap=o_ap: PE.matmul(o_ap(), lhsT=MB[:, j * 128:(j + 1) * 128], rhs=negI[:, 0, :], start=False, stop=True), reads=[MB, negI], writes=[banks[bi]])
                    if os.environ.get("MK_MERGE", "1") == "1":
                        for r_ in range(4):
                            for par in range(2):
                                h = 2 * r_ + par
                                pb = 64 * par
                                bi = lb + par
                                mk.op("pe", lambda h=h, pb=pb, j=j, bi=bi, r_=r_: PE.matmul(bf(bi)[:, r_ * 128:(r_ + 1) * 128], lhsT=kTc[pb:pb + 64, h // 2, j * 128:(j + 1) * 128], rhs=qT[pb:pb + 64, h // 2, :], start=(r_ == 0), stop=False, tile_position=(pb, 0), skip_group_check=True), reads=[kTc, qT], writes=[banks[bi]])
                        for par in range(2):
                            bi = lb + par
                            mk.op("pe", lambda j=j, bi=bi: PE.matmul(bf(bi)[:, :], lhsT=MB[:, j * 128:(j + 1) * 128], rhs=negI[:, :, :].rearrange("p a b -> p (a b)"), start=False, stop=True, skip_group_check=True), reads=[MB, negI], writes=[banks[bi]])
                        ptv = (lambda pt=pt: pt[:, :, :].rearrange("p (a two) b -> p a two b", two=2))
                        mk.op("act", lambda lb=lb, ptv=ptv: A.activation(out=ptv()[:, :, 0, :], in_=bf(lb).rearrange("p (a b) -> p a b", a=4), func=AF.Exp, scale=0.125), reads=[banks[lb]], writes=[pt])
                        mk.op("act", lambda lb=lb, ptv=ptv: A.activation(out=ptv()[:, :, 1, :], in_=bf(lb + 1).rearrange("p (a b) -> p a b", a=4), func=AF.Exp, scale=0.125), reads=[banks[lb + 1]], writes=[pt])
                    if os.environ.get("MK_MERGE", "1") == "0":
                        mk.op("act", lambda lb=lb, pt=pt: A.activation(out=pt[:, 0:4, :], in_=bf(lb).rearrange("p (a b) -> p a b", a=4), func=AF.Exp, scale=0.125), reads=[banks[lb]], writes=[pt])
                        mk.op("act", lambda lb=lb, pt=pt: A.activation(out=pt[:, 4:8, :], in_=bf(lb + 1).rearrange("p (a b) -> p a b", a=4), func=AF.Exp, scale=0.125), reads=[banks[lb + 1]], writes=[pt])
                    for h in range(8):
                        bi = 6 + h // 4
                        mk.op("pe", lambda h=h, bi=bi, j=j, pt=pt: PE.matmul(bf(bi)[:, (h % 4) * 65:(h % 4) * 65 + 65], lhsT=pt[:, h, :], rhs=Vc[:, j, h, 0:65], start=(j == 0 and h % 4 == 0), stop=(j == i), skip_group_check=True), reads=[pt, Vc], writes=[banks[bi]])
                for hb in range(2):
                    ov = (lambda hb=hb: bf(6 + hb)[:, 0:260].rearrange("p (h e) -> p h e", h=4))
                    mk.op("dve", lambda hb=hb, ov=ov: V.reciprocal(out=rec8[:, hb * 4:hb * 4 + 4], in_=ov()[:, :, 64]), reads=[banks[6 + hb]], writes=[rec8])
                    mk.op("dve", lambda hb=hb, ov=ov: V.tensor_tensor(out=Y16[:, hb * 256:(hb + 1) * 256].rearrange("p (h e) -> p h e", h=4), in0=ov()[:, :, 0:64], in1=rec8[:, hb * 4:hb * 4 + 4].unsqueeze(2).to_broadcast([128, 4, 64]), op=ALU.mult), reads=[banks[6 + hb], rec8], writes=[Y16])

                if STOP < 7:
                    continue
                mk.op("pe", lambda: PE.transpose(out=bf(1)[0:16, 0:128], in_=PROJ[:, 2856:2872], identity=identf[:, :]), reads=[PROJ, identf], writes=[banks[1]])
                mk.op("act", lambda: A.copy(out=glrT[0:16, :], in_=bf(1)[0:16, 0:128]), reads=[banks[1]], writes=[glrT])
                mk.op("pe", lambda: PE.matmul(bf(2)[:, 0:256], lhsT=glrT[:, :], rhs=GU[:, :], start=True, stop=True), reads=[glrT, GU], writes=[banks[2]])
                mk.op("act", lambda: A.activation(out=Lg[:, :], in_=bf(2)[:, 0:256], func=AF.Exp, scale=-1.0), reads=[banks[2]], writes=[Lg])
                mk.op("act", lambda: A.activation(out=Lg[:, :], in_=Lg[:, :], func=AF.Ln, bias=1.0), reads=[Lg], writes=[Lg])
                for h in range(4):
                    mk.op("pe", lambda h=h: PE.matmul(bf(3)[0:64, h * 128:(h + 1) * 128], lhsT=Lg[:, h * 64:(h + 1) * 64], rhs=TRI[:, :], start=True, stop=True), reads=[Lg, TRI], writes=[banks[3]])
                mk.op("pe", lambda: PE.matmul(bf(4)[:, 0:256], lhsT=TRU[:, :], rhs=Lg[:, :], start=True, stop=True), reads=[Lg, TRU], writes=[banks[4]])
                b3v = (lambda: bf(3)[0:64, :].rearrange("p (a b) -> p a b", a=4))
                mk.op("act", lambda: A.activation(out=E1[:, :, :], in_=b3v(), func=AF.Exp), reads=[banks[3]], writes=[E1])
                mk.op("act", lambda: A.activation(out=E2[:, :, :], in_=b3v(), func=AF.Exp, scale=-1.0), reads=[banks[3]], writes=[E2])
                mk.op("act", lambda: A.activation(out=Ee[:, :], in_=bf(4)[:, 0:256], func=AF.Exp), reads=[banks[4]], writes=[Ee])
                mk.op("dve", lambda: V.scalar_tensor_tensor(out=qdT[:, :, :], in0=GQK[:, 0:4, :], scalar=0.125, in1=E1[:, :, :], op0=ALU.mult, op1=ALU.mult), reads=[GQK, E1], writes=[qdT])
                mk.op("dve", lambda: V.tensor_tensor(out=kiT[:, :, :], in0=GQK[:, 4:8, :], in1=E2[:, :, :], op=ALU.mult), reads=[GQK, E2], writes=[kiT])
                mk.op("dve", lambda: V.tensor_tensor(out=kte[:, :], in0=PROJ[:, 2088:2344], in1=Ee[:, :], op=ALU.mult), reads=[PROJ, Ee], writes=[kte])
                mk.op("act", lambda: A.copy(out=gvb[:, :], in_=PROJ[:, 2344:2856]), reads=[PROJ], writes=[gvb])
                for h in range(4):
                    mk.op("pe", lambda h=h: PE.matmul(bf(5)[:, h * 128:(h + 1) * 128], lhsT=kiT[:, h, :], rhs=qdT[:, h, :], start=True, stop=True), reads=[kiT, qdT], writes=[banks[5]])
                mk.op("dve", lambda: V.scalar_tensor_tensor(out=ATb[:, :, :], in0=bf(5).rearrange("p (a b) -> p a b", a=4), scalar=-16.0, in1=TRI[:, :].unsqueeze(1).to_broadcast([128, 4, 128]), op0=ALU.mult, op1=ALU.mult), reads=[banks[5], TRI], writes=[ATb])
                for h in range(4):
                    mk.op("pe", lambda h=h: PE.matmul(bf(1)[:, h * 128:(h + 1) * 128], lhsT=ATb[:, h, :], rhs=gvb[:, h * 128:(h + 1) * 128], start=True, stop=False), reads=[ATb, gvb], writes=[banks[1]])
                    mk.op("pe", lambda h=h: PE.matmul(bf(1)[:, h * 128:(h + 1) * 128], lhsT=qdT[:, h, :], rhs=Sb[:, h, :], start=False, stop=True), reads=[qdT, Sb], writes=[banks[1]])
                for h in range(4):
                    mk.op("pe", lambda h=h: PE.matmul(bf(2)[0:64, h * 128:(h + 1) * 128], lhsT=kte[:, h * 64:(h + 1) * 64], rhs=gvb[:, h * 128:(h + 1) * 128], start=True, stop=True), reads=[kte, gvb], writes=[banks[2]])
                for h in range(4):
                    mk.op("dve", lambda h=h: V.scalar_tensor_tensor(out=Sst[:, h, :], in0=Sst[:, h, :], scalar=E1[:, h, 127:128], in1=bf(2)[0:64, h * 128:(h + 1) * 128], op0=ALU.mult, op1=ALU.add), reads=[Sst, E1, banks[2]], writes=[Sst])
                mk.op("dve", lambda: V.tensor_copy(out=Sb[:, :, :], in_=Sst[:, :, :]), reads=[Sst], writes=[Sb])
                mk.op("act", lambda: A.copy(out=og[:, :, :], in_=bf(1).rearrange("p (a b) -> p a b", a=4)), reads=[banks[1]], writes=[og])
                mk.op("dve", lambda: V.tensor_tensor(out=sq[:, :], in0=og[:, :, :].rearrange("p a b -> p (a b)"), in1=og[:, :, :].rearrange("p a b -> p (a b)"), op=ALU.mult), reads=[og], writes=[sq])
                mk.op("dve", lambda: V.tensor_reduce(out=gst[:, 0:4], in_=sq[:, :].rearrange("p (a b) -> p a b", a=4), axis=AX.X, op=ALU.add), reads=[sq], writes=[gst])
                mk.op("dve", lambda: V.tensor_scalar(out=gst[:, 4:8], in0=gst[:, 0:4], scalar1=1.0 / 128, scalar2=1e-6, op0=ALU.mult, op1=ALU.add), reads=[gst], writes=[gst])
                mk.op("act", lambda: A.sqrt(out=gst[:, 8:12], in_=gst[:, 4:8]), reads=[gst], writes=[gst])
                mk.op("dve", lambda: V.reciprocal(out=gst[:, 12:16], in_=gst[:, 8:12]), reads=[gst], writes=[gst])
                mk.op("dve", lambda: V.tensor_tensor(out=og[:, :, :], in0=og[:, :, :], in1=gst[:, 12:16].unsqueeze(2).to_broadcast([128, 4, 128]), op=ALU.mult), reads=[og, gst], writes=[og])
                mk.op("dve", lambda: V.tensor_tensor(out=og[:, :, :], in0=og[:, :, :], in1=NG[:, :].unsqueeze(1).to_broadcast([128, 4, 128]), op=ALU.mult), reads=[og, NG], writes=[og])
                mk.op("act", lambda: A.activation(out=sil[:, :], in_=PROJ[:, 2872:3384], func=AF.Silu), reads=[PROJ], writes=[sil])
                mk.op("dve", lambda: V.tensor_tensor(out=Y16[:, 512:1024], in0=og[:, :, :].rearrange("p a b -> p (a b)"), in1=sil[:, :], op=ALU.mult), reads=[og, sil], writes=[Y16])

                if STOP < 8:
                    continue
                transpose8(Y16, lambda bk: mk.op("dve", lambda: V.tensor_copy(out=yT[:, :, :], in_=bb(0).rearrange("p (a b) -> p a b", a=8)), reads=[bk], writes=[yT]), 0)
                for hf in range(2):
                    bi = 1 + hf
                    for qd in range(2):
                        q4 = hf * 2 + qd
                        wt = WQ[q4 % 2]
                        for k in range(8):
                            mk.op("pe", lambda k=k, qd=qd, bi=bi, wt=wt: PE.matmul(bf(bi)[:, qd * 256:(qd + 1) * 256], lhsT=yT[:, k, :], rhs=wt[:, k, :], start=(k == 0), stop=(k == 7)), reads=[yT, wt], writes=[banks[bi]])
                        if q4 + 2 < 4:
                            wq_load(q4 + 2)
                    mk.op("dve", lambda hf=hf, bi=bi: V.scalar_tensor_tensor(out=rr[:, hf * 512:(hf + 1) * 512], in0=xs[:, hf * 512:(hf + 1) * 512], scalar=ALPHA, in1=bf(bi)[:, :], op0=ALU.mult, op1=ALU.add), reads=[xs, banks[bi]], writes=[rr])
                layer_norm(rr, g1, b1, h1, lnw)
                conv_issue((32 + NT - 1) // NT)
                if os.environ.get("MK_DBG") == "M":
                    mk.op("dve", lambda: V.tensor_copy(out=h1[:, 0:24], in_=tk[:, 0:24]), reads=[tk], writes=[h1])
                    mk.op("dve", lambda nk=nk: V.tensor_reduce(out=h1[:, 24:25], in_=MB[:, 0:nk], axis=AX.X, op=ALU.add), reads=[MB], writes=[h1])
                    mk.op("dve", lambda nk=nk: V.tensor_copy(out=h1[:, 32:32 + nk], in_=Ssc[:, 0:nk]), reads=[Ssc], writes=[h1])
                if os.environ.get("MK_DBG") == "Y":
                    mk.op("dve", lambda: V.tensor_copy(out=h1[:, :], in_=Y16[:, :]), reads=[Y16], writes=[h1])
                mk.dma("sp", lambda i=i: SP.dma_start(out=h1d[i * 128:(i + 1) * 128, :], in_=h1[:, :]), h1, reads=[h1], writes=[H1D])
            mk.barrier()
            mk.flush()


        if "B" in phases:
          with contextlib.ExitStack() as sbk:
            def sb(name, shape, dt):
                return mk.sb(name, shape, dt, st=sbk)
            conv_issue(64)
            wqb = sb("wqb", [128, 8, D], BF16); wob = sb("wob", [128, 8, D], BF16)
            wpqb = sb("wpqb", [128, 8, 2048], BF16)
            KmT = sb("KmT", [128, 8, 256], BF16); Vm = sb("Vm", [128, 2, D], BF16)
            skT = sb("skT", [128, 2, 128], BF16)
            g2, b2 = ln_consts(ln2g, ln2b, sbk, "ln2")
            g3, b3 = ln_consts(ln3g, ln3b, sbk, "ln3")
            ones = sb("ones", [128, 128], BF16)
            mk.op("pool", lambda: G.memset(ones[:, :], 1.0), writes=[ones])
            io_i = sb("io_i", [128, 16], I32); IO16 = sb("IO16", [128, 16], F32)
            mk.op("pool", lambda: G.iota(out=io_i[:, :], pattern=[[1, 16]], base=0, channel_multiplier=0), writes=[io_i])
            mk.op("dve", lambda: V.tensor_copy(out=IO16[:, :], in_=io_i[:, :]), reads=[io_i], writes=[IO16])
            dm2 = sb("dm2", [128, 8], F32)
            pro = contextlib.ExitStack()
            def sbp(name, shape, dt):
                return mk.sb(name, shape, dt, st=pro)
            wkb = sbp("wkb", [128, 8, D], BF16); wvb = sbp("wvb", [128, 8, D], BF16)
            for (wt, wd, wn) in ((wkb, wk, D), (wvb, wv, D), (wqb, wq, D), (wob, wo, D), (wpqb, wpq, 2048)):
                for k in range(8):
                    mk.dma("pool", lambda k=k, wt=wt, wd=wd: G.dma_start(out=wt[:, k, :], in_=wd[k * 128:(k + 1) * 128, :], max_dma_last_dim=4096), wt, writes=[wt])
            ms = sbp("ms", [128, 2, D], F32); mb = sbp("mb", [128, 2 * D], BF16)
            memT = sbp("memT", [128, 8, 256], BF16)
            mk.dma("sp", lambda: SP.dma_start(out=ms[:, :, :], in_=mem.rearrange("(a p) d -> p a d", p=128)), ms, writes=[ms])
            mk.op("dve", lambda: V.tensor_copy(out=mb[:, :], in_=ms[:, :, :].rearrange("p a d -> p (a d)")), reads=[ms], writes=[mb])
            for a in range(2):
                def ev_m(bk, a=a):
                    mk.op("dve", lambda: V.tensor_copy(out=memT[:, :, a * 128:(a + 1) * 128], in_=bb(0).rearrange("p (k b) -> p k b", k=8)), reads=[bk], writes=[memT])
                for k in range(8):
                    mk.op("pe", lambda k=k, a=a: PE.transpose(out=bb(0)[:, k * 128:(k + 1) * 128], in_=mb[:, a * D + k * 128:a * D + (k + 1) * 128], identity=ident[:, :]), reads=[mb, ident], writes=[banks[0]])
                ev_m(banks[0])
            for c in range(8):
                bi = 1 + c % 2
                for k in range(8):
                    mk.op("pe", lambda k=k, c=c, bi=bi: PE.matmul(bf(bi)[:, 0:256], lhsT=wkb[:, k, c * 128:(c + 1) * 128], rhs=memT[:, k, :], start=(k == 0), stop=(k == 7)), reads=[wkb, memT], writes=[banks[bi]])
                mk.op("dve", lambda c=c, bi=bi: V.tensor_copy(out=KmT[:, c, :], in_=bf(bi)[:, 0:256]), reads=[banks[bi]], writes=[KmT])
            for mt in range(2):
                for hf in range(2):
                    bi = 3 + hf
                    for k in range(8):
                        mk.op("pe", lambda k=k, mt=mt, hf=hf, bi=bi: PE.matmul(bf(bi)[:, :], lhsT=memT[:, k, mt * 128:(mt + 1) * 128], rhs=wvb[:, k, hf * 512:(hf + 1) * 512], start=(k == 0), stop=(k == 7)), reads=[wvb, memT], writes=[banks[bi]])
                    mk.op("dve", lambda mt=mt, hf=hf, bi=bi: V.tensor_copy(out=Vm[:, mt, hf * 512:(hf + 1) * 512], in_=bf(bi)[:, :]), reads=[banks[bi]], writes=[Vm])
            sks = sbp("sks", [128, 256], F32); skb = sbp("skb", [128, 256], BF16)
            mk.dma("sp", lambda: SP.dma_start(out=sks[:, 0:128], in_=sk1[:, :]), sks, writes=[sks])
            mk.dma("sp", lambda: SP.dma_start(out=sks[:, 128:256], in_=sk2[:, :]), sks, writes=[sks])
            mk.op("dve", lambda: V.tensor_copy(out=skb[:, :], in_=sks[:, :]), reads=[sks], writes=[skb])
            for a in range(2):
                mk.op("pe", lambda a=a: PE.transpose(out=bb(0)[:, a * 128:(a + 1) * 128], in_=skb[:, a * 128:(a + 1) * 128], identity=ident[:, :]), reads=[skb, ident], writes=[banks[0]])
            mk.op("dve", lambda: V.tensor_copy(out=skT[:, :, :], in_=bb(0)[:, 0:256].rearrange("p (a b) -> p a b", a=2)), reads=[banks[0]], writes=[skT])

            mk.barrier()
            mk.flush()
            pro.close()
            h1s = sb("h1s", [128, D], F32); hb = sb("hb", [128, D], BF16); hT = sb("hT", [128, 8, 128], BF16)
            qTb = sb("qTb", [128, 8, 128], BF16)
            PTx = sb("PTx", [128, 2, 4, 128], BF16)
            recs = sb("recs", [128, 4, 128], F32)
            oTn = sb("oTn", [128, 8, 128], BF16)
            rr2 = sb("rr2", [128, D], F32); H2L = [sb("h2_%d" % q, [128, D], F32) for q in range(2)]; lnw2 = sb("lnw2", [128, 32], F32)
            pqT = sb("pqT", [128, 16, 128], BF16)
            scr8 = sb("scr8", [128, 2048], F32)
            SCs = sb("SCs", [128, 16, 128], F32); SC2 = AV(scr8, lambda: scr8[:, :].rearrange("p (a b) -> p a b", a=16))
            V8 = sb("V8", [128, 16, 16], F32); I8 = sb("I8", [128, 16, 16], U32); I8f = sb("I8f", [128, 16, 16], F32)
            cand = AV(SCs, lambda: SCs[:, :, :].rearrange("p (h s) n -> p h (s n)", s=2)); cand2 = AV(scr8, lambda: scr8[:, :].rearrange("p (a b) -> p a b", a=8))
            T16 = sb("T16", [128, 8, 16], F32); P16 = sb("P16", [128, 8, 16], U32)
            Pa = sb("Pa", [128, 8, 16], U32); Pb = sb("Pb", [128, 8, 16], U32)
            Af = sb("Af", [128, 8, 16], F32); Bf = sb("Bf", [128, 8, 16], F32)
            OH = AV(scr8, lambda: scr8[:, :].rearrange("p (h r a) -> p h r a", h=8, r=16))
            i1s = sb("i1s", [128, 128], F32); i2s = sb("i2s", [128, 128], F32)
            EIDL = [sb("EID%d" % q, [128, 128], U32) for q in range(2)]
            GTL = [sb("GT%d" % q, [128, 8, 16], F32) for q in range(2)]; gz = sb("gz", [128, 16], F32)
            dots = sb("dots", [128, 128], F32); coef = sb("coef", [128, 128], F32)
            NB = 4
            GS = 4; NGB = 3
            UG = [sb("UG%d" % q, [128, 2 * D], BF16) for q in range(GS * NGB)]
            DG = [sb("DG%d" % q, [128, GS, 128], BF16) for q in range(NGB)]
            junk = sb("junk", [128, D], BF16)
            gl = sb("gl", [128, 128], F32)
            rr3 = rr2; yo = h1s

            def top16(src, srcT, scratch, scrT, vout, iout, tl):
                vt, it_ = tl
                mk.op("dve", lambda: V.max(out=vout[:, 0:8], in_=src), reads=[srcT], writes=[vt])
                mk.op("dve", lambda: V.max_index(out=iout[:, 0:8], in_max=vout[:, 0:8], in_values=src), reads=[srcT, vt], writes=[it_])
                mk.op("dve", lambda: V.match_replace(out=scratch, in_to_replace=vout[:, 0:8], in_values=src, imm_value=-1.0e30), reads=[srcT, vt], writes=[scrT])
                mk.op("dve", lambda: V.memset(dm2[:, 0:1], 0.0), writes=[dm2])
                mk.op("dve", lambda: V.max(out=vout[:, 8:16], in_=scratch), reads=[scrT], writes=[vt])
                mk.op("dve", lambda: V.max_index(out=iout[:, 8:16], in_max=vout[:, 8:16], in_values=scratch), reads=[scrT, vt], writes=[it_])

            def stage_P(i):
                h2 = H2L[i % 2]; EID = EIDL[i % 2]; GT = GTL[i % 2]
                mk.dma("sp", lambda i=i: SP.dma_start(out=h1s[:, :], in_=h1d[i * 128:(i + 1) * 128, :]), h1s, reads=[H1D], writes=[h1s])
                mk.op("act", lambda: A.copy(out=hb[:, :], in_=h1s[:, :]), reads=[h1s], writes=[hb])
                transpose8(hb, lambda bk: mk.op("dve", lambda: V.tensor_copy(out=hT[:, :, :], in_=bb(0).rearrange("p (a b) -> p a b", a=8)), reads=[bk], writes=[hT]), 0)
                for c in range(8):
                    bi = 1 + c // 4
                    for k in range(8):
                        mk.op("pe", lambda k=k, c=c, bi=bi: PE.matmul(bf(bi)[:, (c % 4) * 128:(c % 4 + 1) * 128], lhsT=wqb[:, k, c * 128:(c + 1) * 128], rhs=hT[:, k, :], start=(k == 0), stop=(k == 7)), reads=[wqb, hT], writes=[banks[bi]])
                for hf in range(2):
                    mk.op("act", lambda hf=hf: A.copy(out=qTb[:, hf * 4:(hf + 1) * 4, :], in_=bf(1 + hf).rearrange("p (a b) -> p a b", a=4)), reads=[banks[1 + hf]], writes=[qTb])
                for mt in range(2):
                    bi = 3 + mt
                    for h in range(4):
                        for cc in range(2):
                            c = 2 * h + cc
                            mk.op("pe", lambda h=h, c=c, cc=cc, mt=mt, bi=bi: PE.matmul(bf(bi)[:, h * 128:(h + 1) * 128], lhsT=KmT[:, c, mt * 128:(mt + 1) * 128], rhs=qTb[:, c, :], start=(cc == 0), stop=(cc == 1)), reads=[KmT, qTb], writes=[banks[bi]])
                    mk.op("act", lambda mt=mt, bi=bi: A.activation(out=PTx[:, mt, :, :], in_=bf(bi).rearrange("p (a b) -> p a b", a=4), func=AF.Exp, scale=1.0 / 16), reads=[banks[bi]], writes=[PTx])
                for c in range(8):
                    bi = (5, 1)[c // 4]
                    for mt in range(2):
                        mk.op("pe", lambda c=c, mt=mt, bi=bi: PE.matmul(bf(bi)[:, (c % 4) * 128:(c % 4 + 1) * 128], lhsT=Vm[:, mt, c * 128:(c + 1) * 128], rhs=PTx[:, mt, c // 2, :], start=(mt == 0), stop=(mt == 1)), reads=[Vm, PTx], writes=[banks[bi]])
                for h in range(4):
                    for mt in range(2):
                        mk.op("pe", lambda h=h, mt=mt: PE.matmul(bf(2)[:, h * 128:(h + 1) * 128], lhsT=ones[:, :], rhs=PTx[:, mt, h, :], start=(mt == 0), stop=(mt == 1)), reads=[ones, PTx], writes=[banks[2]])
                mk.op("dve", lambda: V.reciprocal(out=recs[:, :, :], in_=bf(2).rearrange("p (a b) -> p a b", a=4)), reads=[banks[2]], writes=[recs])
                for hf in range(2):
                    mk.op("dve", lambda hf=hf: V.tensor_tensor(out=oTn[:, hf * 4:(hf + 1) * 4, :].rearrange("p (h c) t -> p h c t", h=2), in0=bf((5, 1)[hf]).rearrange("p (h c t) -> p h c t", h=2, c=2), in1=recs[:, hf * 2:(hf + 1) * 2, :].unsqueeze(2).to_broadcast([128, 2, 2, 128]), op=ALU.mult), reads=[banks[(5, 1)[hf]], recs], writes=[oTn])
                for hf in range(2):
                    bi = 3 + hf
                    for k in range(8):
                        mk.op("pe", lambda k=k, hf=hf, bi=bi: PE.matmul(bf(bi)[:, :], lhsT=oTn[:, k, :], rhs=wob[:, k, hf * 512:(hf + 1) * 512], start=(k == 0), stop=(k == 7)), reads=[oTn, wob], writes=[banks[bi]])
                    mk.op("dve", lambda hf=hf, bi=bi: V.scalar_tensor_tensor(out=rr2[:, hf * 512:(hf + 1) * 512], in0=h1s[:, hf * 512:(hf + 1) * 512], scalar=ALPHA, in1=bf(bi)[:, :], op0=ALU.mult, op1=ALU.add), reads=[h1s, banks[bi]], writes=[rr2])
                layer_norm(rr2, g2, b2, h2, lnw2)
                if os.environ.get("MK_DBG") == "H2":
                    mk.dma("sp", lambda i=i: SP.dma_start(out=y[i * 128:(i + 1) * 128, :], in_=h2[:, :]), h2, reads=[h2])
                    return
                mk.op("act", lambda: A.copy(out=hb[:, :], in_=h2[:, :]), reads=[h2], writes=[hb])
                transpose8(hb, lambda bk: mk.op("dve", lambda: V.tensor_copy(out=hT[:, :, :], in_=bb(0).rearrange("p (a b) -> p a b", a=8)), reads=[bk], writes=[hT]), 0)
                for c in range(16):
                    bi = 1 + c // 4
                    for k in range(8):
                        mk.op("pe", lambda k=k, c=c, bi=bi: PE.matmul(bf(bi)[:, (c % 4) * 128:(c % 4 + 1) * 128], lhsT=wpqb[:, k, c * 128:(c + 1) * 128], rhs=hT[:, k, :], start=(k == 0), stop=(k == 7)), reads=[wpqb, hT], writes=[banks[bi]])
                for g4 in range(4):
                    mk.op("act", lambda g4=g4: A.copy(out=pqT[:, g4 * 4:(g4 + 1) * 4, :], in_=bf(1 + g4).rearrange("p (a b) -> p a b", a=4)), reads=[banks[1 + g4]], writes=[pqT])
                for c in range(16):
                    bi = (5, 0, 1, 2)[c // 4]
                    mk.op("pe", lambda c=c, bi=bi: PE.matmul(bf(bi)[:, (c % 4) * 128:(c % 4 + 1) * 128], lhsT=pqT[:, c, :], rhs=skT[:, c % 2, :], start=True, stop=True), reads=[pqT, skT], writes=[banks[bi]])
                    if c % 4 == 3:
                        g4 = c // 4
                        mk.op("dve", lambda g4=g4, bi=bi: V.tensor_copy(out=SCs[:, g4 * 4:(g4 + 1) * 4, :], in_=bf(bi).rearrange("p (a b) -> p a b", a=4)), reads=[banks[bi]], writes=[SCs])
                for c in range(16):
                    top16(SCs[:, c, :], SCs, SC2[:, c, :], SC2, V8[:, c, :], I8[:, c, :], (V8, I8))
                V8v = V8[:, :, :].rearrange("p (h s) a -> p h s a", s=2)
                mk.op("dve", lambda: V.tensor_tensor(out=cand[:, :, :].rearrange("p h (a b) -> p h a b", a=16), in0=V8v[:, :, 0, :].unsqueeze(3).to_broadcast([128, 8, 16, 16]), in1=V8v[:, :, 1, :].unsqueeze(2).to_broadcast([128, 8, 16, 16]), op=ALU.add), reads=[V8], writes=[cand])
                for h in range(8):
                    top16(cand[:, h, :], cand, cand2[:, h, :], cand2, T16[:, h, :], P16[:, h, :], (T16, P16))
                mk.op("dve", lambda: V.tensor_single_scalar(out=Pa[:, :, :], in_=P16[:, :, :], scalar=4, op=ALU.logical_shift_right), reads=[P16], writes=[Pa])
                mk.op("dve", lambda: V.tensor_single_scalar(out=Pb[:, :, :], in_=P16[:, :, :], scalar=15, op=ALU.bitwise_and), reads=[P16], writes=[Pb])
                mk.op("dve", lambda: V.tensor_copy(out=Af[:, :, :], in_=Pa[:, :, :]), reads=[Pa], writes=[Af])
                mk.op("dve", lambda: V.tensor_copy(out=Bf[:, :, :], in_=Pb[:, :, :]), reads=[Pb], writes=[Bf])
                mk.op("dve", lambda: V.tensor_copy(out=I8f[:, :, :], in_=I8[:, :, :]), reads=[I8], writes=[I8f])
                I8v = I8f[:, :, :].rearrange("p (h s) a -> p h s a", s=2)
                for (sel, which, dst) in ((Af, 0, i1s), (Bf, 1, i2s)):
                    mk.op("dve", lambda sel=sel: V.tensor_tensor(out=OH[:, :, :, :], in0=sel[:, :, :].unsqueeze(3).to_broadcast([128, 8, 16, 16]), in1=IO16[:, :].unsqueeze(1).unsqueeze(1).to_broadcast([128, 8, 16, 16]), op=ALU.is_equal), reads=[sel, IO16], writes=[OH])
                    mk.op("dve", lambda which=which: V.tensor_tensor(out=OH[:, :, :, :], in0=OH[:, :, :, :], in1=I8v[:, :, which, :].unsqueeze(2).to_broadcast([128, 8, 16, 16]), op=ALU.mult), reads=[OH, I8f], writes=[OH])
                    mk.op("dve", lambda dst=dst: V.tensor_reduce(out=dst[:, :], in_=OH[:, :, :, :].rearrange("p h r a -> p (h r) a"), axis=AX.X, op=ALU.add), reads=[OH], writes=[dst])
                mk.op("dve", lambda: V.scalar_tensor_tensor(out=i1s[:, :], in0=i1s[:, :], scalar=128.0, in1=i2s[:, :], op0=ALU.mult, op1=ALU.add), reads=[i1s, i2s], writes=[i1s])
                mk.op("dve", lambda: V.tensor_copy(out=EID[:, :], in_=i1s[:, :]), reads=[i1s], writes=[EID])
                mk.op("dve", lambda: V.tensor_tensor(out=GT[:, :, :], in0=T16[:, :, :], in1=T16[:, :, 0:1].to_broadcast([128, 8, 16]), op=ALU.subtract), reads=[T16], writes=[GT])
                mk.op("act", lambda: A.activation(out=GT[:, :, :], in_=GT[:, :, :], func=AF.Exp), reads=[GT], writes=[GT])
                mk.op("dve", lambda: V.tensor_reduce(out=gz[:, 0:8], in_=GT[:, :, :], axis=AX.X, op=ALU.add), reads=[GT], writes=[gz])
                mk.op("dve", lambda: V.reciprocal(out=gz[:, 8:16], in_=gz[:, 0:8]), reads=[gz], writes=[gz])
                mk.op("dve", lambda: V.tensor_tensor(out=GT[:, :, :], in0=GT[:, :, :], in1=gz[:, 8:16].unsqueeze(2).to_broadcast([128, 8, 16]), op=ALU.mult), reads=[GT, gz], writes=[GT])

            def stage_G(i, pend_ops):
                per_grp = (len(pend_ops) + 27) // 28
                h2 = H2L[i % 2]; EID = EIDL[i % 2]; GT = GTL[i % 2]
                GTf = GT[:, :, :].rearrange("p h r -> p (h r)")
                for g in range(128 // GS):
                    bufs = [UG[(g % NGB) * GS + j] for j in range(GS)]
                    dg = DG[g % NGB]
                    for j in range(GS):
                        s_ = g * GS + j
                        ug = bufs[j]
                        mk.dma("pool", lambda s_=s_, ug=ug: G.indirect_dma_start(out=ug[:, :], out_offset=None, in_=edu[:, :], in_offset=bass.IndirectOffsetOnAxis(ap=EID[:, s_:s_ + 1], axis=0)), ug, reads=[EID, EDU], writes=[ug])
                        mk.op("dve", lambda s_=s_, ug=ug: V.scalar_tensor_tensor(out=junk[:, :], in0=ug[:, 0:D], scalar=1.0, in1=h2[:, :], op0=ALU.mult, op1=ALU.mult, accum_out=dots[:, s_:s_ + 1]), reads=[ug, h2], writes=[junk, dots])
                    mk.op("dve", lambda: V.memset(dm2[:, 0:1], 0.0), writes=[dm2, dots])
                    mk.op("act", lambda g=g: A.activation(out=gl[:, g * GS:(g + 1) * GS], in_=dots[:, g * GS:(g + 1) * GS], func=AF.Gelu), reads=[dots], writes=[gl])
                    mk.replay(pend_ops, per_grp)
                    mk.op("dve", lambda g=g: V.tensor_tensor(out=coef[:, g * GS:(g + 1) * GS], in0=gl[:, g * GS:(g + 1) * GS], in1=GTf[:, g * GS:(g + 1) * GS], op=ALU.mult), reads=[gl, GT], writes=[coef])
                    mk.op("dve", lambda g=g, dg=dg: V.tensor_tensor(out=dg[:, :, :], in0=ident[:, :].unsqueeze(1).to_broadcast([128, GS, 128]), in1=coef[:, g * GS:(g + 1) * GS].unsqueeze(2).to_broadcast([128, GS, 128]), op=ALU.mult), reads=[ident, coef], writes=[dg])
                    for j in range(GS):
                        s_ = g * GS + j
                        ug = bufs[j]
                        for hf in range(2):
                            mk.op("pe", lambda s_=s_, ug=ug, dg=dg, j=j, hf=hf: PE.matmul(bf(6 + hf)[:, :], lhsT=dg[:, j, :], rhs=ug[:, D + hf * 512:D + (hf + 1) * 512], start=(s_ == 0), stop=(s_ == 127)), reads=[dg, ug], writes=[banks[6 + hf]])
                mk.replay(pend_ops, len(pend_ops))
                for hf in range(2):
                    mk.op("dve", lambda hf=hf: V.scalar_tensor_tensor(out=rr3[:, hf * 512:(hf + 1) * 512], in0=h2[:, hf * 512:(hf + 1) * 512], scalar=ALPHA, in1=bf(6 + hf)[:, :], op0=ALU.mult, op1=ALU.add), reads=[h2, banks[6 + hf]], writes=[rr3])
                layer_norm(rr3, g3, b3, yo, lnw2)
                mk.dma("sp", lambda i=i: SP.dma_start(out=y[i * 128:(i + 1) * 128, :], in_=yo[:, :]), yo, reads=[yo])

            stage_P(0)
            for i in range(NT):
                pend_ops = []
                if i + 1 < NT:
                    mk.defer = pend_ops
                    stage_P(i + 1)
                    mk.defer = None
                stage_G(i, pend_ops)
            mk.barrier()
            mk.flush()
        if "B" not in phases:
            with contextlib.ExitStack() as sd:
                tmp = mk.sb("dbgt", [128, D], F32, st=sd)
                for i in range(NT):
                    mk.dma("sp", lambda i=i: SP.dma_start(out=tmp[:, :], in_=h1d[i * 128:(i + 1) * 128, :]), tmp, reads=[H1D], writes=[tmp])
                    mk.dma("sp", lambda i=i: SP.dma_start(out=y[i * 128:(i + 1) * 128, :], in_=tmp[:, :]), tmp, reads=[tmp])
                mk.barrier()
                mk.flush()
        nw = mk.flush(final=True)
    return nc


_NC_CACHE = {}


def _in_map(inp, b, S):
    NT = S // 128
    f = lambda a: np.ascontiguousarray(np.asarray(a, dtype=np.float32))
    return {
        "x": f(inp["x"][b]), "pos": np.ascontiguousarray(np.asarray(inp["positions"][b]).astype(np.int32).reshape(NT, 128).T),
        "mem": f(inp["mem"][b]), "w_in": f(inp["w_in"][0]), "gate_up": f(inp["gla_gate_up"][0]),
        "gate_bias": f(inp["gla_gate_bias"][0]).reshape(1, -1), "norm_g": f(inp["gla_norm_g"][0]).reshape(1, -1),
        "w_out": f(inp["w_out"][0]), "ln1g": f(inp["ln_mix_g"][0]).reshape(1, -1), "ln1b": f(inp["ln_mix_b"][0]).reshape(1, -1),
        "wq": f(inp["xattn_w_q"][0]), "wk": f(inp["xattn_w_k"][0]), "wv": f(inp["xattn_w_v"][0]), "wo": f(inp["xattn_w_o"][0]),
        "ln2g": f(inp["ln_mem_g"][0]).reshape(1, -1), "ln2b": f(inp["ln_mem_b"][0]).reshape(1, -1),
        "wpq": f(inp["peer_w_query"][0]), "sk1": f(inp["peer_sub_keys_1"][0]), "sk2": f(inp["peer_sub_keys_2"][0]),
        "edown": f(inp["peer_expert_down"][0]), "eup": f(inp["peer_expert_up"][0]),
        "ln3g": f(inp["ln_ffn_g"][0]).reshape(1, -1), "ln3b": f(inp["ln_ffn_b"][0]).reshape(1, -1),
    }


def run(inp, phases="AB", cores=None):
    B, S, _ = inp["x"].shape
    n_sel = min(256, S // 4)
    key = (S, n_sel, phases)
    if key not in _NC_CACHE:
        _NC_CACHE[key] = build(S, n_sel, phases)
    nc = _NC_CACHE[key]
    cores = list(range(B)) if cores is None else cores
    in_maps = [_in_map(inp, b, S) for b in cores]
    res = run_bass_kernel_spmd(nc, in_maps, core_ids=list(range(len(cores))))
    return np.stack([np.asarray(r["y"], dtype=np.float32) for r in res.results], axis=0)


def kernel(**inputs):
    return run(inputs, "AB")
```
